# Optimizing a Trainium2 kernel written in Bass

```python
import math
import jax, jax.numpy as jnp
from jax import lax
import numpy as np

D_MODEL = 1024
BATCH = 8
SEQ = 8192
DEPTH = 1

CHUNK = 64
LN_EPS = 1e-5
DEEPNORM_ALPHA = (2.0 * DEPTH) ** 0.25
DEEPNORM_BETA = (8.0 * DEPTH) ** -0.25

ATT_HEADS = 8
ATT_HEAD_DIM = 64
ATT_WIDTH = ATT_HEADS * ATT_HEAD_DIM
ATT_LEFT_CHUNKS = 8
ATT_BAND = (ATT_LEFT_CHUNKS + 1) * CHUNK
MAX_REL = 128
N_REL = 2 * MAX_REL + 1

MLSTM_HEADS = 4
MLSTM_HEAD_DIM = 128
MLSTM_WIDTH = MLSTM_HEADS * MLSTM_HEAD_DIM
CONV_WIDTH = 4

IN_PROJ_WIDTH = 3 * ATT_WIDTH + 4 * MLSTM_WIDTH + 2 * MLSTM_HEADS
IN_SPLITS = (ATT_WIDTH, 2 * ATT_WIDTH, 3 * ATT_WIDTH,
             3 * ATT_WIDTH + 2 * MLSTM_WIDTH,
             3 * ATT_WIDTH + 3 * MLSTM_WIDTH,
             3 * ATT_WIDTH + 4 * MLSTM_WIDTH,
             3 * ATT_WIDTH + 4 * MLSTM_WIDTH + MLSTM_HEADS)

MEM_TOKENS = 256
XATT_HEADS = 4
XATT_HEAD_DIM = D_MODEL // XATT_HEADS

PEER_HEADS = 8
N_KEYS = 128
N_EXPERTS = N_KEYS * N_KEYS
PEER_TOPK = 16
PEER_KEY_DIM = 256
PEER_HALF = PEER_KEY_DIM // 2
PEER_TOKEN_BLOCK = 128

kernel_name = "hybrid_chunkattn_mlstm_peer_deepnorm"


def layer_norm(x, g, b):
    xf = x.astype(jnp.float32)
    mu = jnp.mean(xf, -1, keepdims=True)
    var = jnp.mean(jnp.square(xf - mu), -1, keepdims=True)
    return ((xf - mu) * lax.rsqrt(var + LN_EPS) * g.astype(jnp.float32) + b.astype(jnp.float32)).astype(x.dtype)


def headwise_norm(x, g):
    xf = x.astype(jnp.float32)
    mu = jnp.mean(xf, -1, keepdims=True)
    var = jnp.mean(jnp.square(xf - mu), -1, keepdims=True)
    return ((xf - mu) * lax.rsqrt(var + LN_EPS) * g.astype(jnp.float32)).astype(x.dtype)


def causal_depthwise_conv(x, w, b):
    c = x.shape[-1]
    y = lax.conv_general_dilated(x, w[:, None, :].astype(x.dtype), window_strides=(1,),
                                 padding=((CONV_WIDTH - 1, 0),),
                                 dimension_numbers=('NWC', 'WIO', 'NWC'),
                                 feature_group_count=c)
    return y + b.astype(x.dtype)


def chunked_rel_attention(q, k, v, rel_bias):
    B, S, H, Dh = q.shape
    n_chunks = S // CHUNK
    pad = ATT_LEFT_CHUNKS * CHUNK
    kp = jnp.pad(k, ((0, 0), (pad, 0), (0, 0), (0, 0)))
    vp = jnp.pad(v, ((0, 0), (pad, 0), (0, 0), (0, 0)))
    q_off = jnp.arange(CHUNK)[:, None]
    k_off = jnp.arange(ATT_BAND)[None, :] - pad
    rel = jnp.clip(k_off - q_off, -MAX_REL, MAX_REL) + MAX_REL
    bias = rel_bias[:, rel].astype(jnp.float32)
    scale = Dh ** -0.5

    def one_chunk(c):
        start = c * CHUNK
        qc = lax.dynamic_slice_in_dim(q, start, CHUNK, axis=1)
        kc = lax.dynamic_slice_in_dim(kp, start, ATT_BAND, axis=1)
        vc = lax.dynamic_slice_in_dim(vp, start, ATT_BAND, axis=1)
        s = jnp.einsum('bqhd,bkhd->bhqk', qc, kc, preferred_element_type=jnp.float32) * scale + bias
        valid = (start - pad + jnp.arange(ATT_BAND)) >= 0
        s = jnp.where(valid, s, -jnp.inf)
        p = jax.nn.softmax(s, axis=-1)
        return jnp.einsum('bhqk,bkhd->bqhd', p.astype(vc.dtype), vc)

    out = lax.map(one_chunk, jnp.arange(n_chunks))
    return jnp.moveaxis(out, 0, 1).reshape(B, S, H * Dh)


def mlstm_chunkwise(q, k, v, i_pre, f_pre):
    B, S, H, D = q.shape
    n_chunks = S // CHUNK
    f32 = jnp.float32

    def to_chunks(a):
        a = a.reshape(B, n_chunks, CHUNK, H, *a.shape[3:])
        return jnp.moveaxis(a, (1, 3), (0, 2))

    qs = to_chunks(q.astype(f32))
    ks = to_chunks(k.astype(f32) * (D ** -0.5))
    vs = to_chunks(v.astype(f32))
    log_i = to_chunks(i_pre.astype(f32))
    log_f = to_chunks(jax.nn.log_sigmoid(f_pre.astype(f32)))
    causal = jnp.tril(jnp.ones((CHUNK, CHUNK), dtype=bool))

    def step(carry, xs):
        C_prev, n_prev, m_prev = carry
        qc, kc, vc, ic, fc = xs
        b = jnp.cumsum(fc, axis=-1)
        log_d = b[..., :, None] - b[..., None, :] + ic[..., None, :]
        log_d = jnp.where(causal, log_d, -jnp.inf)
        inter = b + m_prev[..., None]
        m_t = jnp.maximum(inter, jnp.max(log_d, -1))
        d_mat = jnp.exp(log_d - m_t[..., None])
        w_inter = jnp.exp(inter - m_t)
        qk = jnp.einsum('bhtd,bhsd->bhts', qc, kc) * d_mat
        num = w_inter[..., None] * jnp.einsum('bhtd,bhde->bhte', qc, C_prev) + jnp.einsum('bhts,bhse->bhte', qk, vc)
        den = w_inter * jnp.einsum('bhtd,bhd->bht', qc, n_prev) + jnp.sum(qk, -1)
        h = num / jnp.maximum(jnp.abs(den), jnp.exp(-m_t))[..., None]
        b_last = b[..., -1]
        log_in = b_last[..., None] - b + ic
        m_new = jnp.maximum(b_last + m_prev, jnp.max(log_in, -1))
        w_prev = jnp.exp(b_last + m_prev - m_new)
        w_in = jnp.exp(log_in - m_new[..., None])
        C_new = w_prev[..., None, None] * C_prev + jnp.einsum('bhs,bhsd,bhse->bhde', w_in, kc, vc)
        n_new = w_prev[..., None] * n_prev + jnp.einsum('bhs,bhsd->bhd', w_in, kc)
        return (C_new, n_new, m_new), h

    init = (jnp.zeros((B, H, D, D), f32), jnp.zeros((B, H, D), f32), jnp.zeros((B, H), f32))
    _, h = lax.scan(step, init, (qs, ks, vs, log_i, log_f))
    h = jnp.moveaxis(h, (0, 2), (1, 3)).reshape(B, S, H, D)
    return h.astype(q.dtype)


def hybrid_mixer(h, w_in, conv_w, conv_b, i_bias, f_bias, norm_g, rel_bias, w_out):
    B, S, _ = h.shape
    proj = h @ w_in
    a_q, a_k, a_v, m_qk, m_v, m_o, m_i, m_f = jnp.split(proj, IN_SPLITS, axis=-1)
    hs = (B, S, ATT_HEADS, ATT_HEAD_DIM)
    att = chunked_rel_attention(a_q.reshape(hs), a_k.reshape(hs), a_v.reshape(hs), rel_bias)
    qk = jax.nn.silu(causal_depthwise_conv(m_qk, conv_w, conv_b))
    m_q, m_k = jnp.split(qk, 2, axis=-1)
    ms = (B, S, MLSTM_HEADS, MLSTM_HEAD_DIM)
    hm = mlstm_chunkwise(m_q.reshape(ms), m_k.reshape(ms), m_v.reshape(ms),
                         m_i + i_bias.astype(m_i.dtype), m_f + f_bias.astype(m_f.dtype))
    hm = headwise_norm(hm, norm_g.reshape(MLSTM_HEADS, MLSTM_HEAD_DIM)).reshape(B, S, MLSTM_WIDTH)
    hm = jax.nn.sigmoid(m_o) * hm
    return jnp.concatenate([att, hm], axis=-1) @ w_out


def memory_cross_attention(h, mem, w_q, w_kv, w_o):
    B, S, _ = h.shape
    M = mem.shape[1]
    q = (h @ w_q).reshape(B, S, XATT_HEADS, XATT_HEAD_DIM)
    k, v = jnp.split(mem @ w_kv, 2, axis=-1)
    k = k.reshape(B, M, XATT_HEADS, XATT_HEAD_DIM)
    v = v.reshape(B, M, XATT_HEADS, XATT_HEAD_DIM)
    s = jnp.einsum('bshd,bmhd->bhsm', q, k, preferred_element_type=jnp.float32) * (XATT_HEAD_DIM ** -0.5)
    p = jax.nn.softmax(s, axis=-1)
    o = jnp.einsum('bhsm,bmhd->bshd', p.astype(v.dtype), v).reshape(B, S, D_MODEL)
    return o @ w_o


def peer_ffn(x, w_query, sub_keys, expert_u, expert_v):
    B, S, D = x.shape
    xt = x.reshape(-1, PEER_TOKEN_BLOCK, D)

    def block(xb):
        T = xb.shape[0]
        q = (xb @ w_query).reshape(T, PEER_HEADS, 2, PEER_HALF)
        s = jnp.einsum('thpc,pnc->thpn', q, sub_keys, preferred_element_type=jnp.float32)
        top_s, top_i = lax.top_k(s, PEER_TOPK)
        cand = top_s[:, :, 0, :, None] + top_s[:, :, 1, None, :]
        best_s, best_j = lax.top_k(cand.reshape(T, PEER_HEADS, PEER_TOPK * PEER_TOPK), PEER_TOPK)
        i1 = jnp.take_along_axis(top_i[:, :, 0], best_j // PEER_TOPK, axis=-1)
        i2 = jnp.take_along_axis(top_i[:, :, 1], best_j % PEER_TOPK, axis=-1)
        idx = i1 * N_KEYS + i2
        g = jax.nn.softmax(best_s, axis=-1)
        u = expert_u[idx]
        a = jax.nn.gelu(jnp.einsum('thkd,td->thk', u, xb, preferred_element_type=jnp.float32), approximate=False)
        vv = expert_v[idx]
        out = jnp.einsum('thk,thkd->td', (g * a).astype(vv.dtype), vv)
        return out.astype(xb.dtype)

    return lax.map(block, xt).reshape(B, S, D)


def setup_inputs(seed: int = 0) -> dict:
    key = jax.random.key(seed)
    ks = jax.random.split(key, 26)

    def nrm(k, shape, scale):
        return jax.random.normal(k, shape, jnp.float32) * scale

    L, D = DEPTH, D_MODEL
    beta = DEEPNORM_BETA
    return {
        "x": nrm(ks[0], (BATCH, SEQ, D), 1.0),
        "mem": nrm(ks[1], (BATCH, MEM_TOKENS, D), 1.0),
        "ln_in_g": 1.0 + nrm(ks[2], (D,), 0.02),
        "ln_in_b": nrm(ks[3], (D,), 0.02),
        "w_in": nrm(ks[4], (L, D, IN_PROJ_WIDTH), D ** -0.5),
        "conv_w": nrm(ks[5], (L, CONV_WIDTH, 2 * MLSTM_WIDTH), CONV_WIDTH ** -0.5),
        "conv_b": nrm(ks[6], (L, 2 * MLSTM_WIDTH), 0.02),
        "mlstm_i_bias": nrm(ks[7], (L, MLSTM_HEADS), 0.1),
        "mlstm_f_bias": jnp.linspace(3.0, 6.0, MLSTM_HEADS, dtype=jnp.float32)[None, :] + nrm(ks[8], (L, MLSTM_HEADS), 0.1),
        "mlstm_norm_g": 1.0 + nrm(ks[9], (L, MLSTM_WIDTH), 0.02),
        "rel_bias": nrm(ks[10], (L, ATT_HEADS, N_REL), 0.5),
        "w_out": nrm(ks[11], (L, D, D), beta * D ** -0.5),
        "ln1_g": 1.0 + nrm(ks[12], (L, D), 0.02),
        "ln1_b": nrm(ks[13], (L, D), 0.02),
        "xattn_w_q": nrm(ks[14], (L, D, D), D ** -0.5),
        "xattn_w_kv": nrm(ks[15], (L, D, 2 * D), D ** -0.5),
        "xattn_w_o": nrm(ks[16], (L, D, D), beta * D ** -0.5),
        "ln2_g": 1.0 + nrm(ks[17], (L, D), 0.02),
        "ln2_b": nrm(ks[18], (L, D), 0.02),
        "peer_w_query": nrm(ks[19], (L, D, PEER_HEADS * PEER_KEY_DIM), D ** -0.5),
        "peer_sub_keys": nrm(ks[20], (L, 2, N_KEYS, PEER_HALF), PEER_HALF ** -0.5),
        "peer_u": nrm(ks[21], (L, N_EXPERTS, D), D ** -0.5),
        "peer_v": nrm(ks[22], (L, N_EXPERTS, D), beta * PEER_HEADS ** -0.5),
        "ln3_g": 1.0 + nrm(ks[23], (L, D), 0.02),
        "ln3_b": nrm(ks[24], (L, D), 0.02),
    }


def reference(x, mem, ln_in_g, ln_in_b, w_in, conv_w, conv_b, mlstm_i_bias, mlstm_f_bias,
              mlstm_norm_g, rel_bias, w_out, ln1_g, ln1_b, xattn_w_q, xattn_w_kv, xattn_w_o,
              ln2_g, ln2_b, peer_w_query, peer_sub_keys, peer_u, peer_v, ln3_g, ln3_b):
    a = DEEPNORM_ALPHA
    h = layer_norm(x, ln_in_g, ln_in_b)
    for l in range(DEPTH):
        y = hybrid_mixer(h, w_in[l], conv_w[l], conv_b[l], mlstm_i_bias[l], mlstm_f_bias[l],
                         mlstm_norm_g[l], rel_bias[l], w_out[l])
        h = layer_norm(a * h + y, ln1_g[l], ln1_b[l])
        y = memory_cross_attention(h, mem, xattn_w_q[l], xattn_w_kv[l], xattn_w_o[l])
        h = layer_norm(a * h + y, ln2_g[l], ln2_b[l])
        y = peer_ffn(h, peer_w_query[l], peer_sub_keys[l], peer_u[l], peer_v[l])
        h = layer_norm(a * h + y, ln3_g[l], ln3_b[l])
    return h
```

```python
import math
from contextlib import ExitStack
import numpy as np
import concourse.bass as bass
import concourse.mybir as mybir
from concourse.bass_utils import run_bass_kernel_spmd

F32 = mybir.dt.float32
BF16 = mybir.dt.bfloat16
ALU = mybir.AluOpType
AF = mybir.ActivationFunctionType
AX = mybir.AxisListType

D = 1024
SEQ = 8192
NCORES = 8
LN_EPS = 1e-5
ALPHA = 2.0 ** 0.25
NEG = -30000.0
INW = 3592


class Buf:
    __slots__ = ("name", "w", "r", "dw", "dr", "dsem", "dcnt")

    def __init__(self, name):
        self.name = name
        self.w = {}
        self.r = {}
        self.dw = {}
        self.dr = {}
        self.dsem = None
        self.dcnt = 0


class Eng:
    def __init__(self, name, eng, sem, same_sync):
        self.name = name
        self.eng = eng
        self.sem = sem
        self.cnt = 0
        self.seen = {}
        self.seen_dma = {}
        self.same_sync = same_sync


def _mx(d, k, v):
    if d.get(k, 0) < v:
        d[k] = v


class FW:
    def __init__(self, nc, st):
        self.nc = nc
        self.st = st
        mk = lambda n: st.enter_context(nc.semaphore(n))
        self.pe = Eng("pe", nc.tensor, mk("s_pe"), False)
        self.act = Eng("act", nc.scalar, mk("s_act"), True)
        self.dve = Eng("dve", nc.vector, mk("s_dve"), True)
        self.pool = Eng("pool", nc.gpsimd, mk("s_pool"), True)
        self.sp = Eng("sp", nc.sync, mk("s_sp"), False)
        self.engs = [self.pe, self.act, self.dve, self.pool, self.sp]
        self.n_instr = 0
        self.dma_bufs = []

    def _dsem(self, b):
        if b.dsem is None:
            b.dsem = self.st.enter_context(self.nc.semaphore("d_" + b.name))
            self.dma_bufs.append(b)
        return b.dsem

    def _need(self, deps, ddeps, b, is_write):
        for k, v in b.w.items():
            _mx(deps, k, v)
        for k, v in b.dw.items():
            _mx(ddeps, k, v)
        if is_write:
            for k, v in b.r.items():
                _mx(deps, k, v)
            for k, v in b.dr.items():
                _mx(ddeps, k, v)

    def _waits(self, e, deps, ddeps, is_dma=False):
        for oe, c in deps.items():
            if oe is e and not e.same_sync:
                continue
            if e.seen.get(oe, 0) >= c:
                continue
            e.eng.wait_ge(oe.sem, c)
            e.seen[oe] = c
        for sb_, c in ddeps.items():
            if e.seen_dma.get(sb_, 0) >= c:
                continue
            e.eng.wait_ge(sb_.dsem, c)
            e.seen_dma[sb_] = c

    def op(self, e, fn, reads=(), writes=()):
        deps, ddeps = {}, {}
        for b in reads:
            self._need(deps, ddeps, b, False)
        for b in writes:
            self._need(deps, ddeps, b, True)
        self._waits(e, deps, ddeps)
        ins = fn()
        e.cnt += 1
        ins.then_inc(e.sem, 1)
        self.n_instr += 1
        for b in reads:
            b.r[e] = e.cnt
        for b in writes:
            b.w[e] = e.cnt
        return ins

    def dma(self, q, out_ap, in_ap, wbuf=None, rbuf=None, sembuf=None, nowait_prev=False, **kw):
        sb_ = sembuf or wbuf or rbuf
        self._dsem(sb_)
        deps, ddeps = {}, {}
        if rbuf is not None:
            self._need(deps, ddeps, rbuf, False)
        if wbuf is not None:
            self._need(deps, ddeps, wbuf, True)
            if nowait_prev:
                for k in list(ddeps.keys()):
                    if k in wbuf.dw and ddeps[k] == wbuf.dw[k] and (rbuf is None or k not in rbuf.dw):
                        del ddeps[k]
        self._waits(q, deps, ddeps)
        ins = q.eng.dma_start(out=out_ap, in_=in_ap, **kw)
        sb_.dcnt += 16
        ins.then_inc(sb_.dsem, 16)
        self.n_instr += 1
        if wbuf is not None:
            wbuf.dw[sb_] = sb_.dcnt
        if rbuf is not None:
            rbuf.dr[sb_] = sb_.dcnt
        return ins

    def barrier(self):
        for e in self.engs:
            for oe in self.engs:
                if oe is e or oe.cnt == 0:
                    continue
                if e.seen.get(oe, 0) >= oe.cnt:
                    continue
                e.eng.wait_ge(oe.sem, oe.cnt)
                e.seen[oe] = oe.cnt
            for b in self.dma_bufs:
                if b.dcnt and e.seen_dma.get(b, 0) < b.dcnt:
                    e.eng.wait_ge(b.dsem, b.dcnt)
                    e.seen_dma[b] = b.dcnt


def build(S, debug=False, stop=None, tap=None, prof=False):
    NT = S // 128
    nc = bass.Bass("TRN2", target_bir_lowering=False)

    def din(name, shape):
        return nc.dram_tensor(name, list(shape), F32, kind="ExternalInput").ap()

    x_d = din("x", [S, D])
    mem_d = din("mem", [256, D])
    vec_d = din("vecs", [8, D])
    w_in_d = din("w_in", [D, INW])
    cw_d = din("cw", [128, 8, 4])
    cb_d = din("cb", [128, 8])
    gb_d = din("gbias", [1, 8])
    ng_d = din("norm_g", [1, 512])
    rbT_d = din("rbT", [128, 8, 2, 128])
    rb0_d = din("rb0", [1, 8])
    w_out_d = din("w_out", [D, D])
    w_q_d = din("w_q", [D, D])
    w_kv_d = din("w_kv", [D, 2 * D])
    w_o_d = din("w_o", [D, D])
    w_pq_d = din("w_pq", [D, 2 * D])
    sk_d = din("sub_keys", [2, 128, 128])
    pu_d = din("peer_u", [16384, D])
    pv_d = din("peer_v", [16384, D])
    y_d = nc.dram_tensor("y", [S, D], F32, kind="ExternalOutput").ap()
    if debug:
        h2_d = nc.dram_tensor("h2dbg", [S, D], F32, kind="ExternalOutput").ap()
    else:
        h2_d = nc.dram_tensor("h2s", [S, D], F32, kind="Internal").ap()
    ut_d = nc.dram_tensor("ut_s", [128, 128, 8, 128], BF16, kind="Internal").ap()
    vb_d = nc.dram_tensor("vb_s", [128, 128, D], BF16, kind="Internal").ap()
    h2_b = Buf("h2s")
    ut_b = Buf("ut_s")
    vb_b = Buf("vb_s")

    with ExitStack() as st0:
        fw = FW(nc, st0)
        PE, ACT, DVE, POOL, SP = fw.pe, fw.act, fw.dve, fw.pool, fw.sp
        tns, vec, sca, gps = nc.tensor, nc.vector, nc.scalar, nc.gpsimd

        from contextlib import nullcontext

        cur_scope = [None]

        def mark(name, cond=True):
            if not prof:
                return
            if cur_scope[0] is not None:
                nc.leave_named_scope(cur_scope[0][0], cur_scope[0][1], False)
                cur_scope[0] = None
            if name is not None and cond:
                sid, _ = nc.enter_named_scope(name, False)
                cur_scope[0] = (name, sid)

        def sb(st, name, shape, dt=F32):
            t = st.enter_context(nc.sbuf_tensor("sb_" + name, list(shape), dt))
            return t, Buf(name)

        banks = []
        for i in range(8):
            t = st0.enter_context(nc.psum_tensor("pb%d" % i, [128, 512], F32))
            banks.append((t, Buf("pb%d" % i)))
        bank_rr = [0]

        def bank(lo=0, hi=8):
            i = bank_rr[0]
            if i < lo or i >= hi:
                i = lo
            bank_rr[0] = i + 1 if i + 1 < hi else lo
            return banks[i]

        idf, idf_b = sb(st0, "idf", [128, 128])
        idb, idb_b = sb(st0, "idb", [128, 128], BF16)
        fw.op(POOL, lambda: gps.memset(idf[:], 0.0), writes=[idf_b])
        fw.op(POOL, lambda: gps.affine_select(out=idf[:], in_=idf[:], pattern=[[-1, 128]], compare_op=ALU.not_equal,
                                              fill=1.0, base=0, channel_multiplier=1), reads=[idf_b], writes=[idf_b])
        fw.op(DVE, lambda: vec.tensor_copy(out=idb[:], in_=idf[:]), reads=[idf_b], writes=[idb_b])
        lnv_box = [None, None, 0]
        epsc, epsc_b = sb(st0, "epsc", [128, 1])
        fw.op(POOL, lambda: gps.memset(epsc[:], LN_EPS), writes=[epsc_b])
        stg = [sb(st0, "stg%d" % i, [128, 1024]) for i in range(2)]
        stg_rr = [0]
        cast_rr = [0]

        def cast_copy(out_ap, in_ap, rb, wb):
            k = cast_rr[0] % 3
            cast_rr[0] += 1
            if k == 0:
                fw.op(DVE, lambda: vec.tensor_copy(out=out_ap, in_=in_ap), reads=[rb], writes=[wb])
            elif k == 1:
                fw.op(ACT, lambda: sca.copy(out=out_ap, in_=in_ap), reads=[rb], writes=[wb])
            else:
                fw.op(POOL, lambda: gps.tensor_copy(out=out_ap, in_=in_ap), reads=[rb], writes=[wb])

        def load_weight_bf(w_d, ncols, dst, dst_b):
            for kc in range(8):
                for c0 in range(0, ncols, 1024):
                    cw = min(1024, ncols - c0)
                    s_t, s_b = stg[stg_rr[0] % 2]
                    stg_rr[0] += 1
                    fw.dma(SP, s_t[:, 0:cw], w_d[kc * 128:(kc + 1) * 128, c0:c0 + cw], wbuf=s_b)
                    cast_copy(dst[:, kc, c0:c0 + cw], s_t[:, 0:cw], s_b, dst_b)

        def layer_norm(st_name, xt, xt_b, gi, scr):
            st6, st6_b, mv, mv_b, rs, rs_b = scr
            lnv, lnv_b, base = lnv_box
            gi = gi - base
            for hh in range(2):
                fw.op(DVE, lambda: vec.bn_stats(out=st6[:, hh, :], in_=xt[:, hh * 512:(hh + 1) * 512]),
                      reads=[xt_b], writes=[st6_b])
            fw.op(DVE, lambda: vec.bn_aggr(out=mv[:], in_=st6[:].rearrange("p a b -> p (a b)")), reads=[st6_b], writes=[mv_b])
            fw.op(ACT, lambda: sca.activation(out=rs[:], in_=mv[:, 1:2], func=AF.Sqrt, bias=epsc[:, 0:1]), reads=[mv_b, epsc_b], writes=[rs_b])
            fw.op(DVE, lambda: vec.reciprocal(out=rs[:], in_=rs[:]), reads=[rs_b], writes=[rs_b])
            fw.op(DVE, lambda: vec.scalar_tensor_tensor(out=xt[:], in0=xt[:], scalar=mv[:, 0:1], in1=lnv[:, gi, :],
                                                        op0=ALU.subtract, op1=ALU.mult), reads=[xt_b, mv_b, lnv_b], writes=[xt_b])
            fw.op(DVE, lambda: vec.scalar_tensor_tensor(out=xt[:], in0=xt[:], scalar=rs[:, 0:1], in1=lnv[:, gi + 1, :],
                                                        op0=ALU.mult, op1=ALU.add), reads=[xt_b, rs_b, lnv_b], writes=[xt_b])

        def to_featmajor(src, src_b, srcbf, srcbf_b, dstT, dstT_b, off=0, width=128, src_is_bf=False):
            if not src_is_bf:
                fw.op(ACT, lambda: sca.copy(out=srcbf[:], in_=src[:]), reads=[src_b], writes=[srcbf_b])
            else:
                srcbf, srcbf_b = src, src_b
            pt, pt_b = bank()
            ptv = pt[:].bitcast(BF16)
            for kc in range(8):
                fw.op(PE, lambda: tns.transpose(out=ptv[:, kc * 128:(kc + 1) * 128], in_=srcbf[:, kc * 128:(kc + 1) * 128],
                                                identity=idb[:]), reads=[srcbf_b, idb_b], writes=[pt_b])
            fw.op(DVE, lambda: vec.tensor_copy(out=dstT[:, :, off:off + 128],
                                               in_=ptv.rearrange("p (k t) -> p k t", k=8)), reads=[pt_b], writes=[dstT_b])

        with ExitStack() as stA:
            kxT, kxT_b = sb(stA, "kxT", [128, 8, 256], BF16)
            vx, vx_b = sb(stA, "vx", [128, 2, 4, 257], BF16)
            rbT, rbT_b = sb(stA, "rbT", [128, 8, 2, 128])
            rb0, rb0_b = sb(stA, "rb0", [128, 8])
            cw, cw_b = sb(stA, "cw", [128, 8, 4])
            cb, cb_b = sb(stA, "cb", [128, 8])
            gbias, gbias_b = sb(stA, "gbias", [128, 8])
            ng, ng_b = sb(stA, "ng", [128, 512])
            triu, triu_b = sb(stA, "triu", [128, 128])
            m01, m01_b = sb(stA, "m01", [128, 128])
            fw.dma(SP, rbT[:], rbT_d[:, :, :, :], wbuf=rbT_b)
            fw.dma(SP, rb0[:], rb0_d[0:1, :].partition_broadcast(128), wbuf=rb0_b)
            fw.dma(SP, cw[:], cw_d[:, :, :], wbuf=cw_b)
            fw.dma(SP, cb[:], cb_d[:, :], wbuf=cb_b)
            fw.dma(SP, gbias[:], gb_d[0:1, :].partition_broadcast(128), wbuf=gbias_b)
            fw.dma(SP, ng[:], ng_d[0:1, :].partition_broadcast(128), wbuf=ng_b)
            fw.op(POOL, lambda: gps.memset(triu[:], 1.0), writes=[triu_b])
            fw.op(POOL, lambda: gps.affine_select(out=triu[:], in_=triu[:], pattern=[[1, 128]], compare_op=ALU.is_ge,
                                                  fill=0.0, base=0, channel_multiplier=-1), reads=[triu_b], writes=[triu_b])
            fw.op(POOL, lambda: gps.tensor_copy(out=m01[:], in_=triu[:]), reads=[triu_b], writes=[m01_b])
            fw.op(POOL, lambda: gps.memset(rbT[64:128, :, 0, 0:64], NEG), reads=[rbT_b], writes=[rbT_b])

            with ExitStack() as stM:
                wkv, wkv_b = sb(stM, "wkv", [128, 8, 2 * D], BF16)
                memT, memT_b = sb(stM, "memT", [128, 8, 256], BF16)
                mt, mt_b = sb(stM, "mt", [128, D])
                mtb, mtb_b = sb(stM, "mtb", [128, D], BF16)
                load_weight_bf(w_kv_d, 2 * D, wkv, wkv_b)
                for mc in range(2):
                    fw.dma(SP, mt[:], mem_d[mc * 128:(mc + 1) * 128, :], wbuf=mt_b)
                    to_featmajor(mt, mt_b, mtb, mtb_b, memT, memT_b, off=mc * 128)
                for c in range(8):
                    pt, pt_b = bank()
                    for kc in range(8):
                        fw.op(PE, lambda: tns.matmul(pt[:, 0:256], lhsT=wkv[:, kc, c * 128:(c + 1) * 128], rhs=memT[:, kc, :],
                                                     start=(kc == 0), stop=(kc == 7)), reads=[wkv_b, memT_b], writes=[pt_b])
                    fw.op(ACT, lambda: sca.activation(out=kxT[:, c, :], in_=pt[:, 0:256], func=AF.Copy, scale=1.0 / 16.0),
                          reads=[pt_b], writes=[kxT_b])
                for mc in range(2):
                    for hf in range(2):
                        pt, pt_b = bank()
                        for kc in range(8):
                            fw.op(PE, lambda: tns.matmul(pt[:, :], lhsT=memT[:, kc, mc * 128:(mc + 1) * 128],
                                                         rhs=wkv[:, kc, D + hf * 512:D + (hf + 1) * 512],
                                                         start=(kc == 0), stop=(kc == 7)), reads=[wkv_b, memT_b], writes=[pt_b])
                        fw.op(DVE, lambda: vec.tensor_copy(out=vx[:, mc, 2 * hf:2 * hf + 2, 0:256],
                                                           in_=pt[:, :].rearrange("p (h d) -> p h d", h=2)),
                              reads=[pt_b], writes=[vx_b])
                fw.op(DVE, lambda: vec.memset(vx[:, :, :, 256:257], 1.0), writes=[vx_b])

                ub, ub_b = sb(stM, "ub", [128, D], BF16)
                uts = [sb(stM, "uts%d" % i, [128, 8, 128], BF16) for i in range(2)]
                vbs = [sb(stM, "vbs%d" % i, [128, D], BF16) for i in range(2)]
                for n1 in range(128):
                    s_t, s_b = stg[stg_rr[0] % 2]
                    stg_rr[0] += 1
                    fw.dma(SP, s_t[:, 0:D], pu_d[n1 * 128:(n1 + 1) * 128, :], wbuf=s_b)
                    cast_copy(ub[:], s_t[:, 0:D], s_b, ub_b)
                    u_t, u_b = uts[n1 % 2]
                    to_featmajor(ub, ub_b, None, None, u_t, u_b, off=0, src_is_bf=True)
                    fw.dma(SP, ut_d[n1], u_t[:], wbuf=ut_b, rbuf=u_b, sembuf=u_b)
                    s_t, s_b = stg[stg_rr[0] % 2]
                    stg_rr[0] += 1
                    fw.dma(SP, s_t[:, 0:D], pv_d[n1 * 128:(n1 + 1) * 128, :], wbuf=s_b)
                    v_t, v_b = vbs[n1 % 2]
                    cast_copy(v_t[:], s_t[:, 0:D], s_b, v_b)
                    fw.dma(SP, vb_d[n1], v_t[:], wbuf=vb_b, rbuf=v_b, sembuf=v_b)
                fw.barrier()
            if stop == "setup":
                return nc

            win, win_b = sb(stA, "win", [128, 8, INW], BF16)
            wout, wout_b = sb(stA, "wout", [128, 8, D], BF16)
            wq, wq_b = sb(stA, "wq", [128, 8, D], BF16)
            wo, wo_b = sb(stA, "wo", [128, 8, D], BF16)
            load_weight_bf(w_in_d, INW, win, win_b)
            load_weight_bf(w_out_d, D, wout, wout_b)
            load_weight_bf(w_q_d, D, wq, wq_b)
            load_weight_bf(w_o_d, D, wo, wo_b)
            lnvA, lnvA_b = sb(stA, "lnvA", [128, 6, D])
            for i_ in range(6):
                fw.dma(SP, lnvA[:, i_, :], vec_d[i_:i_ + 1, :].partition_broadcast(128), wbuf=lnvA_b, nowait_prev=(i_ > 0))
            lnv_box[0], lnv_box[1], lnv_box[2] = lnvA, lnvA_b, 0
            hbs = [sb(stA, "hb%d" % i, [128, D]) for i in range(1)]
            hbf, hbf_b = sb(stA, "hbf", [128, D], BF16)
            hT, hT_b = sb(stA, "hT", [128, 8, 128], BF16)
            st6, st6_b = sb(stA, "st6", [128, 2, 6])
            mv, mv_b = sb(stA, "mv", [128, 2])
            rs, rs_b = sb(stA, "rs", [128, 1])
            lnscr = (st6, st6_b, mv, mv_b, rs, rs_b)
            qT, qT_b = sb(stA, "qT", [128, 4, 128], BF16)
            kring, kring_b = sb(stA, "kring", [128, 4, 5, 128], BF16)
            vring, vring_b = sb(stA, "vring", [128, 5, 8, 65], BF16)
            pTs = [sb(stA, "pT%d" % i, [128, 5, 128], BF16) for i in range(2)]
            stmp, stmp_b = sb(stA, "stmp", [128, 256])
            cat, cat_b = sb(stA, "cat", [128, D], BF16)
            rcp, rcp_b = sb(stA, "rcp", [128, 8])
            pc, pc_b = sb(stA, "pc", [128, 8, 131])
            cv, cv_b = sb(stA, "cv", [128, 4, 128])
            qkT, qkT_b = sb(stA, "qkT", [128, 8, 128], BF16)
            vaug, vaug_b = sb(stA, "vaug", [128, 4, 129], BF16)
            og, og_b = sb(stA, "og", [128, 512])
            gif, gif_b = sb(stA, "gif", [128, 8])
            lgf, lgf_b = sb(stA, "lgf", [128, 4])
            dbias, dbias_b = sb(stA, "dbias", [128, 4])
            DT, DT_b = sb(stA, "DT", [128, 4, 128])
            Eb, Eb_b = sb(stA, "Eb", [128, 4, 128])
            qpT, qpT_b = sb(stA, "qpT", [128, 4, 128], BF16)
            WT, WT_b = sb(stA, "WT", [128, 4, 128], BF16)
            Cst, Cst_b = sb(stA, "Cst", [128, 4, 129])
            Cbf, Cbf_b = sb(stA, "Cbf", [128, 4, 129], BF16)
            ksc, ksc_b = sb(stA, "ksc", [128, 4, 128], BF16)
            hm, hm_b = sb(stA, "hm", [128, 4, 128])
            rden, rden_b = sb(stA, "rden", [128, 4])
            hst, hst_b = sb(stA, "hst", [128, 4, 6])
            hmv, hmv_b = sb(stA, "hmv", [128, 4, 2])
            hrs, hrs_b = sb(stA, "hrs", [128, 4])
            pxT, pxT_b = qkT, qkT_b
            ob, ob_b = cat, cat_b
            rdx, rdx_b = sb(stA, "rdx", [128, 4])

            fw.op(POOL, lambda: gps.memset(pc[:], 0.0), writes=[pc_b])
            fw.op(POOL, lambda: gps.memset(Cst[:], 0.0), writes=[Cst_b])
            fw.op(POOL, lambda: gps.memset(Cbf[:], 0.0), writes=[Cbf_b])
            fw.op(POOL, lambda: gps.memset(vring[:, :, :, 64:65], 1.0), writes=[vring_b])
            fw.op(POOL, lambda: gps.memset(vaug[:, :, 128:129], 1.0), writes=[vaug_b])
            for i in range(2):
                fw.op(POOL, lambda: gps.memset(pTs[i][0][:], 0.0), writes=[pTs[i][1]])

            def load_x(i):
                fw.dma(SP, hbs[0][0][:], x_d[i * 128:(i + 1) * 128, :], wbuf=hbs[0][1])

            for i in range(NT):
                load_x(i)
                hb, hb_b = hbs[0]
                slot = i % 5
                mark("A_lnin", i == 2)
                layer_norm("lnin", hb, hb_b, 0, lnscr)
                if tap == "h0":
                    fw.dma(SP, h2_d[i * 128:(i + 1) * 128, :], hb[:], wbuf=h2_b, rbuf=hb_b, sembuf=hb_b)
                    continue
                to_featmajor(hb, hb_b, hbf, hbf_b, hT, hT_b)
                mark("A_inproj", i == 2)
                for grp in range(4):
                    pt, pt_b = bank()
                    for bi in range(4):
                        blk = grp * 4 + bi
                        col0 = blk * 128 if blk < 8 else 1536 + (blk - 8) * 128
                        for kc in range(8):
                            fw.op(PE, lambda: tns.matmul(pt[:, bi * 128:(bi + 1) * 128], lhsT=win[:, kc, col0:col0 + 128],
                                                         rhs=hT[:, kc, :], start=(kc == 0), stop=(kc == 7)),
                                  reads=[win_b, hT_b], writes=[pt_b])
                    pv3 = pt[:, :].rearrange("p (b t) -> p b t", b=4)
                    if grp == 0:
                        fw.op(ACT, lambda: sca.activation(out=qT[:], in_=pv3, func=AF.Copy, scale=0.125),
                              reads=[pt_b], writes=[qT_b])
                    elif grp == 1:
                        fw.op(ACT, lambda: sca.copy(out=kring[:, :, slot, :], in_=pv3), reads=[pt_b], writes=[kring_b])
                    else:
                        b0 = (grp - 2) * 4
                        fw.op(ACT, lambda: sca.copy(out=pc[:, b0:b0 + 4, 3:131], in_=pv3), reads=[pt_b], writes=[pc_b])
                for j, col0 in enumerate((1024, 2560, 3072)):
                    pt, pt_b = bank()
                    for kc in range(8):
                        fw.op(PE, lambda: tns.matmul(pt[:, :], lhsT=hT[:, kc, :], rhs=win[:, kc, col0:col0 + 512],
                                                     start=(kc == 0), stop=(kc == 7)), reads=[win_b, hT_b], writes=[pt_b])
                    if j == 0:
                        fw.op(DVE, lambda: vec.tensor_copy(out=vring[:, slot, :, 0:64],
                                                           in_=pt[:, :].rearrange("p (h d) -> p h d", h=8)),
                              reads=[pt_b], writes=[vring_b])
                    elif j == 1:
                        fw.op(DVE, lambda: vec.tensor_copy(out=vaug[:, :, 0:128],
                                                           in_=pt[:, :].rearrange("p (h d) -> p h d", h=4)),
                              reads=[pt_b], writes=[vaug_b])
                    else:
                        fw.op(ACT, lambda: sca.activation(out=og[:], in_=pt[:, :], func=AF.Sigmoid), reads=[pt_b], writes=[og_b])
                pt, pt_b = bank()
                for kc in range(8):
                    fw.op(PE, lambda: tns.matmul(pt[:, 0:8], lhsT=hT[:, kc, :], rhs=win[:, kc, 3584:3592],
                                                 start=(kc == 0), stop=(kc == 7)), reads=[win_b, hT_b], writes=[pt_b])
                fw.op(DVE, lambda: vec.tensor_tensor(out=gif[:], in0=pt[:, 0:8], in1=gbias[:], op=ALU.add),
                      reads=[pt_b, gbias_b], writes=[gif_b])

                mark("A_attn", i == 2)
                nr = min(i, 4) + 1
                pacc = [banks[0], banks[1]]
                for h in range(8):
                    pr, hh = h // 2, h % 2
                    p0 = hh * 64
                    pA, pA_b = bank(2, 8)
                    pB, pB_b = bank(2, 8)
                    pT, pT_b = pTs[h % 2]
                    for r in range(nr):
                        sl = (i - r) % 5
                        dst = pA[:, r * 128:(r + 1) * 128] if r < 4 else pB[:, 0:128]
                        fw.op(PE, lambda: tns.matmul(dst, lhsT=kring[p0:p0 + 64, pr, sl, :], rhs=qT[p0:p0 + 64, pr, :],
                                                     start=True, stop=True), reads=[kring_b, qT_b],
                              writes=[pA_b if r < 4 else pB_b])
                    n01 = min(nr, 2)
                    fw.op(DVE, lambda: vec.tensor_tensor(out=stmp[:, 0:n01 * 128], in0=pA[:, 0:n01 * 128],
                                                         in1=rbT[:, h, 0:n01, :].rearrange("p r t -> p (r t)"), op=ALU.add),
                          reads=[pA_b, rbT_b], writes=[stmp_b])
                    fw.op(ACT, lambda: sca.activation(out=pT[:, 0:n01, :].rearrange("p r t -> p (r t)"), in_=stmp[:, 0:n01 * 128],
                                                      func=AF.Exp), reads=[stmp_b], writes=[pT_b])
                    if nr > 2:
                        n23 = min(nr, 4) - 2
                        fw.op(ACT, lambda: sca.activation(out=pT[:, 2:2 + n23, :].rearrange("p r t -> p (r t)"),
                                                          in_=pA[:, 256:256 + n23 * 128], func=AF.Exp, bias=rb0[:, h:h + 1]),
                              reads=[pA_b, rb0_b], writes=[pT_b])
                    if nr > 4:
                        fw.op(ACT, lambda: sca.activation(out=pT[64:128, 4, :], in_=pB[64:128, 0:128], func=AF.Exp,
                                                          bias=rb0[64:128, h:h + 1]), reads=[pB_b, rb0_b], writes=[pT_b])
                        fw.op(ACT, lambda: sca.activation(out=pT[0:64, 4, 0:64], in_=pB[0:64, 0:64], func=AF.Exp,
                                                          bias=rb0[0:64, h:h + 1]), reads=[pB_b, rb0_b], writes=[pT_b])
                    pc_t, pc_tb = pacc[h // 4]
                    o0 = (h % 4) * 65
                    for r in range(nr):
                        sl = (i - r) % 5
                        fw.op(PE, lambda: tns.matmul(pc_t[:, o0:o0 + 65], lhsT=pT[:, r, :], rhs=vring[:, sl, h, :],
                                                     start=(r == 0), stop=(r == nr - 1)), reads=[pT_b, vring_b], writes=[pc_tb])
                for half in range(2):
                    pc_t, pc_tb = pacc[half]
                    v3 = pc_t[:, 0:260].rearrange("p (h d) -> p h d", h=4)
                    fw.op(DVE, lambda: vec.reciprocal(out=rcp[:, half * 4:half * 4 + 4], in_=v3[:, :, 64]),
                          reads=[pc_tb], writes=[rcp_b])
                    fw.op(DVE, lambda: vec.tensor_tensor(
                        out=cat[:, half * 256:(half + 1) * 256].rearrange("p (h d) -> p h d", h=4), in0=v3[:, :, 0:64],
                        in1=rcp[:, half * 4:half * 4 + 4].unsqueeze(2).to_broadcast([128, 4, 64]), op=ALU.mult),
                        reads=[pc_tb, rcp_b], writes=[cat_b])

                mark("A_mlstm", i == 2)
                for cg in range(2):
                    cvb = [Buf("cv%d" % k) for k in range(4)]
                    for b4 in range(4):
                        blk = cg * 4 + b4
                        fw.op(POOL, lambda: gps.tensor_scalar(out=cv[:, b4, :], in0=pc[:, blk, 0:128], scalar1=cw[:, blk, 0:1],
                                                              scalar2=cb[:, blk:blk + 1], op0=ALU.mult, op1=ALU.add),
                              reads=[pc_b, cw_b, cb_b], writes=[cvb[b4]] + ([cv_b] if b4 == 0 else []))
                    for j in range(1, 4):
                        for b4 in range(4):
                            blk = cg * 4 + b4
                            fw.op(DVE, lambda: vec.scalar_tensor_tensor(out=cv[:, b4, :], in0=pc[:, blk, j:j + 128],
                                                                        scalar=cw[:, blk, j:j + 1], in1=cv[:, b4, :],
                                                                        op0=ALU.mult, op1=ALU.add),
                                  reads=[pc_b, cw_b, cvb[b4]], writes=[cvb[b4]])
                    for bb_ in cvb:
                        for k_, v_ in bb_.w.items():
                            _mx(cv_b.w, k_, v_)
                        for k_, v_ in bb_.r.items():
                            _mx(cv_b.r, k_, v_)
                    fw.op(ACT, lambda: sca.activation(out=qkT[:, cg * 4:cg * 4 + 4, :], in_=cv[:], func=AF.Silu),
                          reads=[cv_b], writes=[qkT_b])
                fw.op(POOL, lambda: gps.tensor_copy(out=pc[:, :, 0:3], in_=pc[:, :, 128:131]), reads=[pc_b], writes=[pc_b])
                fw.op(ACT, lambda: sca.activation(out=lgf[:], in_=gif[:, 4:8], func=AF.Exp, scale=-1.0), reads=[gif_b], writes=[lgf_b])
                fw.op(ACT, lambda: sca.activation(out=lgf[:], in_=lgf[:], func=AF.Ln, bias=1.0), reads=[lgf_b], writes=[lgf_b])
                fw.op(DVE, lambda: vec.tensor_scalar(out=lgf[:], in0=lgf[:], scalar1=-1.0, scalar2=None, op0=ALU.mult),
                      reads=[lgf_b], writes=[lgf_b])
                pbb, pbb_b = bank()
                for h in range(4):
                    fw.op(PE, lambda: tns.matmul(pbb[:, h * 128:(h + 1) * 128], lhsT=lgf[:, h:h + 1].to_broadcast([128, 128]),
                                                 rhs=triu[:], start=True, stop=True), reads=[lgf_b, triu_b], writes=[pbb_b])
                pcol, pcol_b = bank()
                fw.op(PE, lambda: tns.matmul(pcol[:, 0:4], lhsT=triu[:], rhs=lgf[:], start=True, stop=True),
                      reads=[lgf_b, triu_b], writes=[pcol_b])
                fw.op(DVE, lambda: vec.scalar_tensor_tensor(out=dbias[:], in0=gif[:, 0:4], scalar=-0.5 * math.log(128.0),
                                                            in1=pcol[:, 0:4], op0=ALU.add, op1=ALU.subtract),
                      reads=[gif_b, pcol_b], writes=[dbias_b])
                fw.op(ACT, lambda: sca.activation(out=Eb[:].rearrange("p h t -> p (h t)"), in_=pbb[:, :], func=AF.Exp),
                      reads=[pbb_b], writes=[Eb_b])
                for h in range(4):
                    fw.op(ACT, lambda: sca.activation(out=DT[:, h, :], in_=pbb[:, h * 128:(h + 1) * 128], func=AF.Exp,
                                                      bias=dbias[:, h:h + 1]), reads=[pbb_b, dbias_b], writes=[DT_b])
                fw.op(POOL, lambda: gps.tensor_tensor(out=DT[:], in0=DT[:], in1=m01[:].unsqueeze(1).to_broadcast([128, 4, 128]),
                                                      op=ALU.mult), reads=[DT_b, m01_b], writes=[DT_b])
                fw.op(POOL, lambda: gps.tensor_tensor(out=qpT[:], in0=qkT[:, 0:4, :], in1=Eb[:], op=ALU.mult),
                      reads=[qkT_b, Eb_b], writes=[qpT_b])
                pqk, pqk_b = bank()
                for h in range(4):
                    fw.op(PE, lambda: tns.matmul(pqk[:, h * 128:(h + 1) * 128], lhsT=qkT[:, 4 + h, :], rhs=qkT[:, h, :],
                                                 start=True, stop=True), reads=[qkT_b], writes=[pqk_b])
                fw.op(DVE, lambda: vec.tensor_tensor(out=WT[:].rearrange("p h t -> p (h t)"), in0=pqk[:, :],
                                                     in1=DT[:].rearrange("p h t -> p (h t)"), op=ALU.mult),
                      reads=[pqk_b, DT_b], writes=[WT_b])
                pn = [bank(), bank()]
                for h in range(4):
                    pn_t, pn_b = pn[h // 2]
                    o0 = (h % 2) * 129
                    fw.op(PE, lambda: tns.matmul(pn_t[:, o0:o0 + 129], lhsT=WT[:, h, :], rhs=vaug[:, h, :], start=True, stop=False),
                          reads=[WT_b, vaug_b], writes=[pn_b])
                    fw.op(PE, lambda: tns.matmul(pn_t[:, o0:o0 + 129], lhsT=qpT[:, h, :], rhs=Cbf[:, h, :], start=False, stop=True),
                          reads=[qpT_b, Cbf_b], writes=[pn_b])
                for half in range(2):
                    pn_t, pn_b = pn[half]
                    v3 = pn_t[:, 0:258].rearrange("p (h d) -> p h d", h=2)
                    fw.op(ACT, lambda: sca.activation(out=rden[:, half * 2:half * 2 + 2], in_=v3[:, :, 128], func=AF.Abs),
                          reads=[pn_b], writes=[rden_b])
                    fw.op(DVE, lambda: vec.tensor_scalar(out=rden[:, half * 2:half * 2 + 2], in0=rden[:, half * 2:half * 2 + 2],
                                                         scalar1=1.0, scalar2=None, op0=ALU.max), reads=[rden_b], writes=[rden_b])
                    fw.op(DVE, lambda: vec.reciprocal(out=rden[:, half * 2:half * 2 + 2], in_=rden[:, half * 2:half * 2 + 2]),
                          reads=[rden_b], writes=[rden_b])
                    fw.op(DVE, lambda: vec.tensor_tensor(out=hm[:, half * 2:half * 2 + 2, :], in0=v3[:, :, 0:128],
                                                         in1=rden[:, half * 2:half * 2 + 2].unsqueeze(2).to_broadcast([128, 2, 128]),
                                                         op=ALU.mult), reads=[pn_b, rden_b], writes=[hm_b])
                pkt, pkt_b = bank()
                pktv = pkt[:].bitcast(BF16)
                for h in range(4):
                    fw.op(PE, lambda: tns.transpose(out=pktv[:, h * 128:(h + 1) * 128], in_=qkT[:, 4 + h, :], identity=idb[:]),
                          reads=[qkT_b, idb_b], writes=[pkt_b])
                for h in range(4):
                    fw.op(ACT, lambda: sca.activation(out=ksc[:, h, :], in_=pktv[:, h * 128:(h + 1) * 128], func=AF.Copy,
                                                      scale=DT[:, h, 127:128]), reads=[pkt_b, DT_b], writes=[ksc_b])
                pcu = [bank(), bank()]
                for h in range(4):
                    pcu_t, pcu_b = pcu[h // 2]
                    o0 = (h % 2) * 129
                    fw.op(PE, lambda: tns.matmul(pcu_t[:, o0:o0 + 129], lhsT=ksc[:, h, :], rhs=vaug[:, h, :], start=True, stop=True),
                          reads=[ksc_b, vaug_b], writes=[pcu_b])
                for h in range(4):
                    pcu_t, pcu_b = pcu[h // 2]
                    o0 = (h % 2) * 129
                    fw.op(DVE, lambda: vec.scalar_tensor_tensor(out=Cst[:, h, :], in0=Cst[:, h, :], scalar=Eb[:, h, 127:128],
                                                                in1=pcu_t[:, o0:o0 + 129], op0=ALU.mult, op1=ALU.add),
                          reads=[Cst_b, Eb_b, pcu_b], writes=[Cst_b])
                fw.op(POOL, lambda: gps.tensor_copy(out=Cbf[:], in_=Cst[:]), reads=[Cst_b], writes=[Cbf_b])
                for h in range(4):
                    fw.op(DVE, lambda: vec.bn_stats(out=hst[:, h, :], in_=hm[:, h, :]), reads=[hm_b], writes=[hst_b])
                for h in range(4):
                    fw.op(DVE, lambda: vec.bn_aggr(out=hmv[:, h, :], in_=hst[:, h, :]), reads=[hst_b], writes=[hmv_b])
                fw.op(ACT, lambda: sca.activation(out=hrs[:], in_=hmv[:, :, 1], func=AF.Sqrt, bias=epsc[:, 0:1]), reads=[hmv_b, epsc_b], writes=[hrs_b])
                fw.op(DVE, lambda: vec.reciprocal(out=hrs[:], in_=hrs[:]), reads=[hrs_b], writes=[hrs_b])
                fw.op(DVE, lambda: vec.tensor_tensor(out=hm[:], in0=hm[:], in1=hmv[:, :, 0:1].to_broadcast([128, 4, 128]),
                                                     op=ALU.subtract), reads=[hm_b, hmv_b], writes=[hm_b])
                fw.op(DVE, lambda: vec.tensor_tensor(out=hm[:], in0=hm[:], in1=hrs[:].unsqueeze(2).to_broadcast([128, 4, 128]),
                                                     op=ALU.mult), reads=[hm_b, hrs_b], writes=[hm_b])
                fw.op(POOL, lambda: gps.tensor_tensor(out=hm[:].rearrange("p h d -> p (h d)"), in0=hm[:].rearrange("p h d -> p (h d)"),
                                                      in1=ng[:], op=ALU.mult), reads=[hm_b, ng_b], writes=[hm_b])
                fw.op(POOL, lambda: gps.tensor_tensor(out=cat[:, 512:1024], in0=hm[:].rearrange("p h d -> p (h d)"), in1=og[:],
                                                      op=ALU.mult), reads=[hm_b, og_b], writes=[cat_b])

                mark("A_wout", i == 2)
                if tap == "cat":
                    fw.op(ACT, lambda: sca.copy(out=hb[:], in_=cat[:]), reads=[cat_b], writes=[hb_b])
                    fw.dma(SP, h2_d[i * 128:(i + 1) * 128, :], hb[:], wbuf=h2_b, rbuf=hb_b, sembuf=hb_b)
                    continue
                to_featmajor(cat, cat_b, None, None, hT, hT_b, src_is_bf=True)
                for half in range(2):
                    pt, pt_b = bank()
                    for kc in range(8):
                        fw.op(PE, lambda: tns.matmul(pt[:, :], lhsT=hT[:, kc, :], rhs=wout[:, kc, half * 512:(half + 1) * 512],
                                                     start=(kc == 0), stop=(kc == 7)), reads=[wout_b, hT_b], writes=[pt_b])
                    fw.op(DVE, lambda: vec.scalar_tensor_tensor(out=hb[:, half * 512:(half + 1) * 512],
                                                                in0=hb[:, half * 512:(half + 1) * 512], scalar=ALPHA,
                                                                in1=pt[:, :], op0=ALU.mult, op1=ALU.add),
                          reads=[hb_b, pt_b], writes=[hb_b])
                layer_norm("ln1", hb, hb_b, 2, lnscr)
                if tap == "h1":
                    fw.dma(SP, h2_d[i * 128:(i + 1) * 128, :], hb[:], wbuf=h2_b, rbuf=hb_b, sembuf=hb_b)
                    continue

                mark("A_xattn", i == 2)
                to_featmajor(hb, hb_b, hbf, hbf_b, hT, hT_b)
                for grp in range(2):
                    pt, pt_b = bank()
                    for bi in range(4):
                        c = grp * 4 + bi
                        for kc in range(8):
                            fw.op(PE, lambda: tns.matmul(pt[:, bi * 128:(bi + 1) * 128], lhsT=wq[:, kc, c * 128:(c + 1) * 128],
                                                         rhs=hT[:, kc, :], start=(kc == 0), stop=(kc == 7)),
                                  reads=[wq_b, hT_b], writes=[pt_b])
                    fw.op(ACT, lambda: sca.copy(out=hbf[:, grp * 512:(grp + 1) * 512], in_=pt[:, :]),
                          reads=[pt_b], writes=[hbf_b])
                for grp in range(2):
                    pt, pt_b = bank()
                    for bi in range(4):
                        hx = grp * 2 + bi // 2
                        mc = bi % 2
                        for hf in range(2):
                            fw.op(PE, lambda: tns.matmul(pt[:, bi * 128:(bi + 1) * 128],
                                                         lhsT=kxT[:, hx * 2 + hf, mc * 128:(mc + 1) * 128],
                                                         rhs=hbf[:, (hx * 2 + hf) * 128:(hx * 2 + hf + 1) * 128],
                                                         start=(hf == 0), stop=(hf == 1)),
                                  reads=[kxT_b, hbf_b], writes=[pt_b])
                    fw.op(ACT, lambda: sca.activation(out=pxT[:, grp * 4:grp * 4 + 4, :].rearrange("p b t -> p (b t)"), in_=pt[:, :],
                                                      func=AF.Exp), reads=[pt_b], writes=[pxT_b])
                for hx in range(4):
                    pt, pt_b = bank()
                    for mc in range(2):
                        fw.op(PE, lambda: tns.matmul(pt[:, 0:257], lhsT=pxT[:, hx * 2 + mc, :], rhs=vx[:, mc, hx, :],
                                                     start=(mc == 0), stop=(mc == 1)), reads=[pxT_b, vx_b], writes=[pt_b])
                    fw.op(DVE, lambda: vec.reciprocal(out=rdx[:, hx:hx + 1], in_=pt[:, 256:257]), reads=[pt_b], writes=[rdx_b])
                    fw.op(DVE, lambda: vec.tensor_scalar(out=ob[:, hx * 256:(hx + 1) * 256], in0=pt[:, 0:256],
                                                         scalar1=rdx[:, hx:hx + 1], scalar2=None, op0=ALU.mult),
                          reads=[pt_b, rdx_b], writes=[ob_b])
                to_featmajor(ob, ob_b, None, None, hT, hT_b, src_is_bf=True)
                for half in range(2):
                    pt, pt_b = bank()
                    for kc in range(8):
                        fw.op(PE, lambda: tns.matmul(pt[:, :], lhsT=hT[:, kc, :], rhs=wo[:, kc, half * 512:(half + 1) * 512],
                                                     start=(kc == 0), stop=(kc == 7)), reads=[wo_b, hT_b], writes=[pt_b])
                    fw.op(DVE, lambda: vec.scalar_tensor_tensor(out=hb[:, half * 512:(half + 1) * 512],
                                                                in0=hb[:, half * 512:(half + 1) * 512], scalar=ALPHA,
                                                                in1=pt[:, :], op0=ALU.mult, op1=ALU.add),
                          reads=[hb_b, pt_b], writes=[hb_b])
                layer_norm("ln2", hb, hb_b, 4, lnscr)
                fw.dma(SP, h2_d[i * 128:(i + 1) * 128, :], hb[:], wbuf=h2_b, rbuf=hb_b, sembuf=hb_b)
                mark(None)
            fw.barrier()
        if stop == "A":
            return nc

        TB = 2 if NT % 2 == 0 else 1
        T = TB * 128
        with ExitStack() as stB:
            wpq, wpq_b = sb(stB, "wpq", [128, 8, 2 * D], BF16)
            ksT, ksT_b = sb(stB, "ksT", [128, 2, 128], BF16)
            skf, skf_b = sb(stB, "skf", [128, 128])
            skb, skb_b = sb(stB, "skb", [128, 128], BF16)
            load_weight_bf(w_pq_d, 2 * D, wpq, wpq_b)
            lnvB, lnvB_b = sb(stB, "lnvB", [128, 2, D])
            for i_ in range(2):
                fw.dma(SP, lnvB[:, i_, :], vec_d[6 + i_:7 + i_, :].partition_broadcast(128), wbuf=lnvB_b, nowait_prev=(i_ > 0))
            lnv_box[0], lnv_box[1], lnv_box[2] = lnvB, lnvB_b, 6
            for p in range(2):
                fw.dma(SP, skf[:], sk_d[p], wbuf=skf_b)
                fw.op(ACT, lambda: sca.copy(out=skb[:], in_=skf[:]), reads=[skf_b], writes=[skb_b])
                pt, pt_b = bank(4, 8)
                ptv = pt[:].bitcast(BF16)
                fw.op(PE, lambda: tns.transpose(out=ptv[:, 0:128], in_=skb[:], identity=idb[:]), reads=[skb_b, idb_b], writes=[pt_b])
                fw.op(DVE, lambda: vec.tensor_copy(out=ksT[:, p, :], in_=ptv[:, 0:128]), reads=[pt_b], writes=[ksT_b])
            h2t = [sb(stB, "h2t%d" % i, [128, D]) for i in range(2)]
            h2bf, h2bf_b = sb(stB, "h2bf", [128, D], BF16)
            h2Ts = [sb(stB, "h2T%d" % i, [128, 8, T], BF16) for i in range(2)]
            pqT, pqT_b = sb(stB, "pqT", [128, 16, T], BF16)
            ssb, ssb_b = sb(stB, "ssb", [128, 16, 128])
            srep, srep_b = sb(stB, "srep", [128, 16, 128])
            vtop, vtop_b = sb(stB, "vtop", [128, 16, 16])
            cand, cand_b = ssb[:].rearrange("p (h a) n -> p h (a n)", a=2), ssb_b
            cand2, cand2_b = srep[:].rearrange("p (h a) n -> p h (a n)", a=2), srep_b
            best, best_b = sb(stB, "best", [128, 8, 16])
            etmp, etmp_b = sb(stB, "etmp", [128, 8, 16])
            zz, zz_b = sb(stB, "zz", [128, 8])
            c1, c1_b = sb(stB, "c1", [128, 8])
            v0c, v0c_b = sb(stB, "v0c", [128, 8, 16])
            thr, thr_b = sb(stB, "thr", [128, 8, 16])
            bia, bia_b = sb(stB, "bia", [128, 8, 16])
            v0Ts = [sb(stB, "v0T%d" % i, [128, 128]) for i in range(TB)]
            thrTs = [sb(stB, "thrT%d" % i, [128, 128]) for i in range(TB)]
            biaTs = [sb(stB, "biaT%d" % i, [128, 128]) for i in range(TB)]
            pcs, pcs_b = sb(stB, "pcs", [128, 3, 128], BF16)
            rres, rres_b = sb(stB, "rres", [128, 128])
            At = [(sb(stB, "At%d" % i, [128, 4, 128], BF16)[0], [Buf("At%d_%d" % (i, k)) for k in range(4)]) for i in range(3)]
            Bt = [(sb(stB, "Bt%d" % i, [128, 4, 128], BF16)[0], [Buf("Bt%d_%d" % (i, k)) for k in range(4)]) for i in range(3)]
            E4 = [sb(stB, "E4%d" % i, [128, 512]) for i in range(2)]
            ebTs = [sb(stB, "ebT%d" % i, [128, 128]) for i in range(TB)]
            NB = 6
            Qrep = [sb(stB, "Qrep%d" % i, [128, 4, 128], BF16) for i in range(4)]
            Gs, Gs_b = sb(stB, "Gs", [128, 128, T], BF16)
            utc = [sb(stB, "utc%d" % i, [128, 8, 128], BF16) for i in range(NB)]
            vbc = [sb(stB, "vbc%d" % i, [128, D], BF16) for i in range(NB)]
            ag = [sb(stB, "ag%d" % i, [128, T], BF16) for i in range(3)]
            cT = [sb(stB, "cT%d" % i, [128, T], BF16) for i in range(3)]
            print("phaseB sbuf remaining", nc.sbuf_bytes_remaining)
            st6, st6_b = sb(stB, "st6B", [128, 2, 6])
            mv, mv_b = sb(stB, "mvB", [128, 2])
            rs, rs_b = sb(stB, "rsB", [128, 1])
            lnscr = (st6, st6_b, mv, mv_b, rs, rs_b)

            NTB = NT // TB
            per = 512 // T
            fst = {}
            LT = h2t[1 % len(h2t)]

            def piece_load(bt, ts_):
                hT, hT_b = h2Ts[bt % 2]
                t_, t_b = h2t[0]
                row0 = (bt * TB + ts_) * 128
                fw.dma(SP, t_[:], h2_d[row0:row0 + 128, :], wbuf=t_b, rbuf=h2_b)
                fw.op(ACT, lambda: sca.copy(out=h2bf[:], in_=t_[:]), reads=[t_b], writes=[h2bf_b])
                pt, pt_b = bank(4, 8)
                ptv = pt[:].bitcast(BF16)
                for kc in range(8):
                    fw.op(PE, lambda: tns.transpose(out=ptv[:, kc * 128:(kc + 1) * 128], in_=h2bf[:, kc * 128:(kc + 1) * 128],
                                                    identity=idb[:]), reads=[h2bf_b, idb_b], writes=[pt_b])
                fw.op(DVE, lambda: vec.tensor_copy(out=hT[:, :, ts_ * 128:(ts_ + 1) * 128],
                                                   in_=ptv.rearrange("p (k t) -> p k t", k=8)), reads=[pt_b], writes=[hT_b])

            def piece_pq(bt, g0):
                hT, hT_b = h2Ts[bt % 2]
                pt, pt_b = bank(4, 8)
                for bi in range(per):
                    hp = g0 + bi
                    for kc in range(8):
                        fw.op(PE, lambda: tns.matmul(pt[:, bi * T:(bi + 1) * T], lhsT=wpq[:, kc, hp * 128:(hp + 1) * 128],
                                                     rhs=hT[:, kc, :], start=(kc == 0), stop=(kc == 7)),
                              reads=[wpq_b, hT_b], writes=[pt_b])
                fw.op(ACT, lambda: sca.copy(out=pqT[:, g0:g0 + per, :], in_=pt[:, 0:per * T].rearrange("p (b t) -> p b t", b=per)),
                      reads=[pt_b], writes=[pqT_b])

            def piece_s(ts_, g):
                tsl = slice(ts_ * 128, (ts_ + 1) * 128)
                pt, pt_b = bank(4, 8)
                for bi in range(4):
                    hp = g * 4 + bi
                    fw.op(PE, lambda: tns.matmul(pt[:, bi * 128:(bi + 1) * 128], lhsT=pqT[:, hp, tsl], rhs=ksT[:, hp % 2, :],
                                                 start=True, stop=True), reads=[pqT_b, ksT_b], writes=[pt_b])
                fw.op(ACT, lambda: sca.copy(out=ssb[:, g * 4:g * 4 + 4, :], in_=pt[:, :].rearrange("p (b n) -> p b n", b=4)),
                      reads=[pt_b], writes=[ssb_b])

            def piece_top_a(ts_):
                fst["vt"] = [Buf("vt%d" % k) for k in range(32)]
                fst["sr"] = [Buf("sr%d" % k) for k in range(16)]
                vt_bs = fst["vt"]
                for hp in range(16):
                    fw.op(DVE, lambda: vec.max(out=vtop[:, hp, 0:8], in_=ssb[:, hp, :]), reads=[ssb_b, vtop_b],
                          writes=[vt_bs[hp * 2]] + ([vtop_b] if hp == 0 else []))

            def piece_top_b(ts_):
                vt_bs, sr_bs = fst["vt"], fst["sr"]
                for hp in range(16):
                    fw.op(DVE, lambda: vec.match_replace(out=srep[:, hp, :], in_to_replace=vtop[:, hp, 0:8], in_values=ssb[:, hp, :],
                                                         imm_value=-1e30), reads=[ssb_b, vt_bs[hp * 2], srep_b], writes=[sr_bs[hp]])

            def piece_top_c(ts_):
                vt_bs, sr_bs = fst["vt"], fst["sr"]
                for hp in range(16):
                    fw.op(DVE, lambda: vec.max(out=vtop[:, hp, 8:16], in_=srep[:, hp, :]), reads=[sr_bs[hp], vtop_b], writes=[vt_bs[hp * 2 + 1]])
                vt4 = vtop[:].rearrange("p (h q) k -> p h q k", q=2)
                fw.op(POOL, lambda: gps.tensor_tensor(out=cand[:].rearrange("p h (a b) -> p h a b", a=16),
                                                      in0=vt4[:, :, 0, :].unsqueeze(3).to_broadcast([128, 8, 16, 16]),
                                                      in1=vt4[:, :, 1, :].unsqueeze(2).to_broadcast([128, 8, 16, 16]), op=ALU.add),
                      reads=vt_bs, writes=[cand_b])

            def piece_best(ts_):
                vt_bs, sr_bs = fst["vt"], fst["sr"]
                bs_bs = [Buf("bs%d" % k) for k in range(16)]
                c2_bs = [Buf("c2%d" % k) for k in range(8)]
                for h in range(8):
                    fw.op(DVE, lambda: vec.max(out=best[:, h, 0:8], in_=cand[:, h, :]), reads=[cand_b, best_b],
                          writes=[bs_bs[h * 2]] + ([best_b] if h == 0 else []))
                for h in range(8):
                    fw.op(DVE, lambda: vec.match_replace(out=cand2[:, h, :], in_to_replace=best[:, h, 0:8], in_values=cand[:, h, :],
                                                         imm_value=-1e30), reads=[cand_b, bs_bs[h * 2], cand2_b] + sr_bs, writes=[c2_bs[h]])
                for h in range(8):
                    fw.op(DVE, lambda: vec.max(out=best[:, h, 8:16], in_=cand2[:, h, :]), reads=[c2_bs[h], best_b], writes=[bs_bs[h * 2 + 1]])
                for bb_ in bs_bs:
                    for k_, v_ in bb_.w.items():
                        _mx(best_b.w, k_, v_)
                    for k_, v_ in bb_.r.items():
                        _mx(best_b.r, k_, v_)
                for bb_ in vt_bs:
                    for k_, v_ in bb_.w.items():
                        _mx(vtop_b.w, k_, v_)
                    for k_, v_ in bb_.r.items():
                        _mx(vtop_b.r, k_, v_)
                for bb_ in sr_bs + c2_bs:
                    for k_, v_ in bb_.w.items():
                        _mx(srep_b.w, k_, v_)
                    for k_, v_ in bb_.r.items():
                        _mx(srep_b.r, k_, v_)

            def piece_misc(ts_):
                vt4 = vtop[:].rearrange("p (h q) k -> p h q k", q=2)
                fw.op(POOL, lambda: gps.tensor_tensor(out=etmp[:], in0=best[:], in1=best[:, :, 0:1].to_broadcast([128, 8, 16]),
                                                      op=ALU.subtract), reads=[best_b], writes=[etmp_b])
                fw.op(ACT, lambda: sca.activation(out=etmp[:], in_=etmp[:], func=AF.Exp), reads=[etmp_b], writes=[etmp_b])
                fw.op(DVE, lambda: vec.reduce_sum(out=zz[:], in_=etmp[:], axis=AX.X), reads=[etmp_b], writes=[zz_b])
                fw.op(ACT, lambda: sca.activation(out=zz[:], in_=zz[:], func=AF.Ln), reads=[zz_b], writes=[zz_b])
                fw.op(POOL, lambda: gps.tensor_tensor(out=c1[:], in0=zz[:], in1=best[:, :, 0], op=ALU.add),
                      reads=[zz_b, best_b], writes=[c1_b])
                fw.op(POOL, lambda: gps.tensor_copy(out=v0c[:], in_=vt4[:, :, 0, :]), reads=[vtop_b], writes=[v0c_b])
                fw.op(POOL, lambda: gps.tensor_tensor(out=thr[:], in0=best[:, :, 15:16].to_broadcast([128, 8, 16]), in1=v0c[:],
                                                      op=ALU.subtract), reads=[best_b, v0c_b], writes=[thr_b])
                fw.op(POOL, lambda: gps.tensor_scalar(out=thr[:], in0=thr[:], scalar1=-1e-5, scalar2=None, op0=ALU.add),
                      reads=[thr_b], writes=[thr_b])
                fw.op(POOL, lambda: gps.tensor_tensor(out=bia[:], in0=v0c[:], in1=c1[:].unsqueeze(2).to_broadcast([128, 8, 16]),
                                                      op=ALU.subtract), reads=[c1_b, v0c_b], writes=[bia_b])

            def piece_tr(ts_, which):
                src, src_b = ((v0c, v0c_b), (thr, thr_b), (bia, bia_b))[which]
                dst, dst_b = (v0Ts[ts_], thrTs[ts_], biaTs[ts_])[which]
                srcf = src[:].rearrange("p h r -> p (h r)")
                pt, pt_b = bank(4, 8)
                ptv = pt[:].bitcast(BF16)
                for pc_i in range(3):
                    fw.op(ACT, lambda: sca.copy(out=pcs[:, pc_i, :], in_=(srcf if pc_i == 0 else rres[:])),
                          reads=[src_b, rres_b], writes=[pcs_b])
                    if pc_i < 2:
                        fw.op(DVE, lambda: vec.tensor_tensor(out=rres[:], in0=(srcf if pc_i == 0 else rres[:]), in1=pcs[:, pc_i, :],
                                                             op=ALU.subtract), reads=[src_b, rres_b, pcs_b], writes=[rres_b])
                    fw.op(PE, lambda: tns.transpose(out=ptv[:, pc_i * 128:(pc_i + 1) * 128], in_=pcs[:, pc_i, :], identity=idb[:]),
                          reads=[pcs_b, idb_b], writes=[pt_b])
                fw.op(ACT, lambda: sca.copy(out=dst[:], in_=ptv[:, 0:128]), reads=[pt_b], writes=[dst_b])
                fw.op(DVE, lambda: vec.tensor_tensor(out=dst[:], in0=dst[:], in1=ptv[:, 128:256], op=ALU.add),
                      reads=[pt_b, dst_b], writes=[dst_b])
                fw.op(DVE, lambda: vec.tensor_tensor(out=dst[:], in0=dst[:], in1=ptv[:, 256:384], op=ALU.add),
                      reads=[pt_b, dst_b], writes=[dst_b])
                if which == 2:
                    ebT, ebT_b = ebTs[ts_]
                    fw.op(ACT, lambda: sca.activation(out=ebT[:], in_=dst[:], func=AF.Exp), reads=[dst_b], writes=[ebT_b])

            def front_pieces(bt):
                P = []
                for ts_ in range(TB):
                    P.append(lambda ts_=ts_: piece_load(bt, ts_))
                for g0 in range(0, 16, per):
                    P.append(lambda g0=g0: piece_pq(bt, g0))
                for ts_ in range(TB):
                    for g in range(4):
                        P.append(lambda ts_=ts_, g=g: piece_s(ts_, g))
                    P.append(lambda ts_=ts_: piece_top_a(ts_))
                    P.append(lambda ts_=ts_: piece_top_b(ts_))
                    P.append(lambda ts_=ts_: piece_top_c(ts_))
                    P.append(lambda ts_=ts_: piece_best(ts_))
                    P.append(lambda ts_=ts_: piece_misc(ts_))
                    for which in range(3):
                        P.append(lambda ts_=ts_, which=which: piece_tr(ts_, which))
                return P

            def tok_loop(bt, ts_):
                v0T, v0T_b = v0Ts[ts_]
                thrT, thrT_b = thrTs[ts_]
                biaT, biaT_b = biaTs[ts_]
                itst = {}

                def st12(t4):
                    a_t, a_bs = At[t4 % 3]
                    b_t, b_bs = Bt[t4 % 3]
                    e4, e4_b = E4[t4 % 2]
                    p0_, p0_b = bank(0, 8)
                    p1_, p1_b = bank(0, 8)
                    tt0 = ts_ * 128 + t4 * 4
                    qr = []
                    for p_ in range(2):
                        q_t, q_b = Qrep[(t4 % 2) * 2 + p_]
                        src = pqT[:, :, tt0:tt0 + 4].rearrange("c (h q) t -> c q t h", q=2)[:, p_]
                        fw.op(POOL, lambda: gps.tensor_copy(out=q_t[:].rearrange("c t (h r) -> c t h r", r=16),
                                                            in_=src.unsqueeze(3).to_broadcast([128, 4, 8, 16])),
                              reads=[pqT_b], writes=[q_b])
                        qr.append((q_t, q_b))
                    for k in range(4):
                        for p_, (pp, pp_b) in enumerate(((p0_, p0_b), (p1_, p1_b))):
                            q_t, q_b = qr[p_]
                            fw.op(PE, lambda: tns.matmul(pp[:, k * 128:(k + 1) * 128], lhsT=q_t[:, k, :], rhs=ksT[:, p_, :],
                                                         start=True, stop=True), reads=[q_b, ksT_b], writes=[pp_b])
                    tl = t4 * 4
                    for k in range(4):
                        fw.op(ACT, lambda: sca.activation(out=e4[:, k * 128:(k + 1) * 128], in_=p1_[:, k * 128:(k + 1) * 128], func=AF.Exp,
                                                          bias=biaT[:, tl + k:tl + k + 1]), reads=[p1_b, biaT_b], writes=[e4_b])
                    fw.op(DVE, lambda: vec.tensor_tensor(out=a_t[:], in0=p0_[:, :].rearrange("p (k n) -> p k n", k=4),
                                                         in1=v0T[:, tl:tl + 4].unsqueeze(2).to_broadcast([128, 4, 128]),
                                                         op=ALU.is_equal), reads=[p0_b, v0T_b], writes=a_bs)
                    for k in range(4):
                        fw.op(DVE, lambda: vec.scalar_tensor_tensor(out=b_t[:, k, :], in0=p1_[:, k * 128:(k + 1) * 128],
                                                                    scalar=thrT[:, tl + k:tl + k + 1], in1=e4[:, k * 128:(k + 1) * 128],
                                                                    op0=ALU.is_ge, op1=ALU.mult),
                              reads=[p1_b, thrT_b, e4_b], writes=[b_bs[k]])

                def st3(t4):
                    a_t, a_bs = At[t4 % 3]
                    b_t, b_bs = Bt[t4 % 3]
                    pg, pg_b = bank(0, 8)
                    for k in range(4):
                        fw.op(PE, lambda: tns.matmul(pg[:, k * 128:(k + 1) * 128], lhsT=b_t[:, k, :], rhs=a_t[:, k, :],
                                                     start=True, stop=True), reads=[a_bs[k], b_bs[k]], writes=[pg_b])
                    itst[t4] = (pg, pg_b)

                def st4(t4):
                    pg, pg_b = itst.pop(t4)
                    tg = ts_ * 128 + t4 * 4
                    fw.op(ACT, lambda: sca.copy(out=Gs[:, :, tg:tg + 4].rearrange("p n t -> p t n"),
                                                in_=pg[:, :].rearrange("p (t n) -> p t n", t=4)), reads=[pg_b], writes=[Gs_b])

                for t4 in range(34):
                    if t4 < 32:
                        st12(t4)
                    if 0 <= t4 - 1 < 32:
                        st3(t4 - 1)
                    if 0 <= t4 - 2 < 32:
                        st4(t4 - 2)

            accs = [banks[j] for j in range(TB * 2)]

            def expert_loop(bt, nxt):
                hT, hT_b = h2Ts[bt % 2]
                every = max(1, 120 // max(1, len(nxt)))

                def issue_load(n):
                    u_t, u_b = utc[n % NB]
                    v_t, v_b = vbc[n % NB]
                    fw.dma(SP, u_t[:], ut_d[n], wbuf=u_b, rbuf=ut_b)
                    fw.dma(SP, v_t[:], vb_d[n], wbuf=v_b, rbuf=vb_b)

                def stage2(n):
                    c_t, c_b = cT[n % 3]
                    v_t, v_b = vbc[n % NB]
                    for ts_ in range(TB):
                        for half in range(2):
                            ac, ac_b = accs[ts_ * 2 + half]
                            fw.op(PE, lambda: tns.matmul(ac[:, :], lhsT=c_t[:, ts_ * 128:(ts_ + 1) * 128],
                                                         rhs=v_t[:, half * 512:(half + 1) * 512], start=(n == 0), stop=(n == 127)),
                                  reads=[c_b, v_b], writes=[ac_b])

                for n1 in range(NB - 1):
                    issue_load(n1)
                for n1 in range(128):
                    u_t, u_b = utc[n1 % NB]
                    pa, pa_b = bank(4, 8)
                    for kc in range(8):
                        fw.op(PE, lambda: tns.matmul(pa[:, 0:T], lhsT=u_t[:, kc, :], rhs=hT[:, kc, :], start=(kc == 0), stop=(kc == 7)),
                              reads=[u_b, hT_b], writes=[pa_b])
                    g_t, g_b = ag[n1 % 3]
                    c_t, c_b = cT[n1 % 3]
                    fw.op(ACT, lambda: sca.activation(out=g_t[:], in_=pa[:, 0:T], func=AF.Gelu), reads=[pa_b], writes=[g_b])
                    eng, ee = POOL, gps
                    fw.op(eng, lambda: ee.tensor_tensor(out=c_t[:], in0=g_t[:], in1=Gs[:, n1, :], op=ALU.mult),
                          reads=[g_b, Gs_b], writes=[c_b])
                    if n1 >= 1:
                        stage2(n1 - 1)
                    if n1 + NB - 1 < 128:
                        issue_load(n1 + NB - 1)
                    if nxt and n1 >= 2 and n1 % every == 0:
                        nxt.pop(0)()
                stage2(127)
                while nxt:
                    nxt.pop(0)()

            def tail(bt):
                for ts_ in range(TB):
                    t_, t_b = LT
                    row0 = (bt * TB + ts_) * 128
                    fw.dma(SP, t_[:], h2_d[row0:row0 + 128, :], wbuf=t_b, rbuf=h2_b)
                    for half in range(2):
                        ac, ac_b = accs[ts_ * 2 + half]
                        fw.op(DVE, lambda: vec.scalar_tensor_tensor(out=t_[:, half * 512:(half + 1) * 512],
                                                                    in0=t_[:, half * 512:(half + 1) * 512], scalar=ALPHA,
                                                                    in1=ac[:, :], op0=ALU.mult, op1=ALU.add),
                              reads=[t_b, ac_b], writes=[t_b])
                    layer_norm("ln3", t_, t_b, 6, lnscr)
                    fw.dma(SP, y_d[row0:row0 + 128, :], t_[:], rbuf=t_b, sembuf=t_b)

            for p_ in front_pieces(0):
                p_()
            for bt in range(NTB):
                mark("B_tok", bt == 1)
                for ts_ in range(TB):
                    tok_loop(bt, ts_)
                mark("B_exp", bt == 1)
                expert_loop(bt, front_pieces(bt + 1) if bt + 1 < NTB else [])
                mark("B_ln3", bt == 1)
                tail(bt)
                mark(None)
            fw.barrier()
        print("n_instr", fw.n_instr, {e.name: e.cnt for e in fw.engs})
    return nc


def prep_inputs(inputs, S):
    f = lambda a: np.ascontiguousarray(np.asarray(a, dtype=np.float32))
    x = f(inputs["x"])
    mem = f(inputs["mem"])
    vecs = np.stack([f(inputs["ln_in_g"]), f(inputs["ln_in_b"]), f(inputs["ln1_g"])[0], f(inputs["ln1_b"])[0],
                     f(inputs["ln2_g"])[0], f(inputs["ln2_b"])[0], f(inputs["ln3_g"])[0], f(inputs["ln3_b"])[0]], axis=0)
    conv_w = f(inputs["conv_w"])[0]
    cw = np.ascontiguousarray(conv_w.reshape(4, 8, 128).transpose(2, 1, 0))
    cb = np.ascontiguousarray(f(inputs["conv_b"])[0].reshape(8, 128).T)
    gb = np.concatenate([f(inputs["mlstm_i_bias"])[0], f(inputs["mlstm_f_bias"])[0]])[None, :]
    rel_bias = f(inputs["rel_bias"])[0]
    s = np.arange(128)[:, None]
    t = np.arange(128)[None, :]
    tabs = []
    for r in range(2):
        idx = np.clip(s - t - 128 * r, -128, 128) + 128
        tabs.append(rel_bias[:, idx])
    rbT = np.ascontiguousarray(np.stack(tabs, axis=0).transpose(2, 1, 0, 3))
    rb0 = np.ascontiguousarray(rel_bias[:, 0][None, :])
    common = {
        "vecs": np.ascontiguousarray(vecs), "w_in": f(inputs["w_in"])[0], "cw": cw, "cb": cb, "gbias": np.ascontiguousarray(gb),
        "norm_g": f(inputs["mlstm_norm_g"]), "rbT": rbT, "rb0": rb0, "w_out": f(inputs["w_out"])[0],
        "w_q": f(inputs["xattn_w_q"])[0], "w_kv": f(inputs["xattn_w_kv"])[0], "w_o": f(inputs["xattn_w_o"])[0],
        "w_pq": f(inputs["peer_w_query"])[0], "sub_keys": f(inputs["peer_sub_keys"])[0],
        "peer_u": f(inputs["peer_u"])[0], "peer_v": f(inputs["peer_v"])[0],
    }
    maps = []
    for c in range(x.shape[0]):
        m = dict(common)
        m["x"] = np.ascontiguousarray(x[c, :S])
        m["mem"] = np.ascontiguousarray(mem[c])
        maps.append(m)
    return maps


def kernel(**inputs):
    S = int(np.asarray(inputs["x"]).shape[1])
    nb = int(np.asarray(inputs["x"]).shape[0])
    nc = build(S)
    maps = prep_inputs(inputs, S)
    res = run_bass_kernel_spmd(nc, maps, core_ids=list(range(nb)))
    out = np.stack([np.asarray(res.results[c]["y"], dtype=np.float32) for c in range(nb)], axis=0)
    return out
```

```python
import math
from contextlib import ExitStack
import numpy as np
import concourse.bass as bass
import concourse.mybir as mybir
from concourse.bass_utils import run_bass_kernel_spmd

F32 = mybir.dt.float32
BF16 = mybir.dt.bfloat16
ALU = mybir.AluOpType
AF = mybir.ActivationFunctionType
AX = mybir.AxisListType

D = 1024
SEQ = 8192
NCORES = 8
LN_EPS = 1e-5
ALPHA = 2.0 ** 0.25
NEG = -30000.0
INW = 3592


class Buf:
    __slots__ = ("name", "w", "r", "dw", "dr", "dsem", "dcnt")

    def __init__(self, name):
        self.name = name
        self.w = {}
        self.r = {}
        self.dw = {}
        self.dr = {}
        self.dsem = None
        self.dcnt = 0


class Eng:
    def __init__(self, name, eng, sem, same_sync):
        self.name = name
        self.eng = eng
        self.sem = sem
        self.cnt = 0
        self.seen = {}
        self.seen_dma = {}
        self.same_sync = same_sync


def _mx(d, k, v):
    if d.get(k, 0) < v:
        d[k] = v


class FW:
    def __init__(self, nc, st):
        self.nc = nc
        self.st = st
        mk = lambda n: st.enter_context(nc.semaphore(n))
        self.pe = Eng("pe", nc.tensor, mk("s_pe"), False)
        self.act = Eng("act", nc.scalar, mk("s_act"), True)
        self.dve = Eng("dve", nc.vector, mk("s_dve"), True)
        self.pool = Eng("pool", nc.gpsimd, mk("s_pool"), True)
        self.sp = Eng("sp", nc.sync, mk("s_sp"), False)
        self.engs = [self.pe, self.act, self.dve, self.pool, self.sp]
        self.n_instr = 0
        self.dma_bufs = []

    def _dsem(self, b):
        if b.dsem is None:
            b.dsem = self.st.enter_context(self.nc.semaphore("d_" + b.name))
            self.dma_bufs.append(b)
        return b.dsem

    def _need(self, deps, ddeps, b, is_write):
        for k, v in b.w.items():
            _mx(deps, k, v)
        for k, v in b.dw.items():
            _mx(ddeps, k, v)
        if is_write:
            for k, v in b.r.items():
                _mx(deps, k, v)
            for k, v in b.dr.items():
                _mx(ddeps, k, v)

    def _waits(self, e, deps, ddeps, is_dma=False):
        for oe, c in deps.items():
            if oe is e and not e.same_sync:
                continue
            if e.seen.get(oe, 0) >= c:
                continue
            e.eng.wait_ge(oe.sem, c)
            e.seen[oe] = c
        for sb_, c in ddeps.items():
            if e.seen_dma.get(sb_, 0) >= c:
                continue
            e.eng.wait_ge(sb_.dsem, c)
            e.seen_dma[sb_] = c

    def op(self, e, fn, reads=(), writes=()):
        deps, ddeps = {}, {}
        for b in reads:
            self._need(deps, ddeps, b, False)
        for b in writes:
            self._need(deps, ddeps, b, True)
        self._waits(e, deps, ddeps)
        ins = fn()
        e.cnt += 1
        ins.then_inc(e.sem, 1)
        self.n_instr += 1
        for b in reads:
            b.r[e] = e.cnt
        for b in writes:
            b.w[e] = e.cnt
        return ins

    def dma(self, q, out_ap, in_ap, wbuf=None, rbuf=None, sembuf=None, nowait_prev=False, **kw):
        sb_ = sembuf or wbuf or rbuf
        self._dsem(sb_)
        deps, ddeps = {}, {}
        if rbuf is not None:
            self._need(deps, ddeps, rbuf, False)
        if wbuf is not None:
            self._need(deps, ddeps, wbuf, True)
            if nowait_prev:
                for k in list(ddeps.keys()):
                    if k in wbuf.dw and ddeps[k] == wbuf.dw[k] and (rbuf is None or k not in rbuf.dw):
                        del ddeps[k]
        self._waits(q, deps, ddeps)
        ins = q.eng.dma_start(out=out_ap, in_=in_ap, **kw)
        sb_.dcnt += 16
        ins.then_inc(sb_.dsem, 16)
        self.n_instr += 1
        if wbuf is not None:
            wbuf.dw[sb_] = sb_.dcnt
        if rbuf is not None:
            rbuf.dr[sb_] = sb_.dcnt
        return ins

    def barrier(self):
        for e in self.engs:
            for oe in self.engs:
                if oe is e or oe.cnt == 0:
                    continue
                if e.seen.get(oe, 0) >= oe.cnt:
                    continue
                e.eng.wait_ge(oe.sem, oe.cnt)
                e.seen[oe] = oe.cnt
            for b in self.dma_bufs:
                if b.dcnt and e.seen_dma.get(b, 0) < b.dcnt:
                    e.eng.wait_ge(b.dsem, b.dcnt)
                    e.seen_dma[b] = b.dcnt


def build(S, debug=False, stop=None, tap=None, prof=False):
    NT = S // 128
    nc = bass.Bass("TRN2", target_bir_lowering=False)

    def din(name, shape):
        return nc.dram_tensor(name, list(shape), F32, kind="ExternalInput").ap()

    x_d = din("x", [S, D])
    mem_d = din("mem", [256, D])
    vec_d = din("vecs", [8, D])
    w_in_d = din("w_in", [D, INW])
    cw_d = din("cw", [128, 8, 4])
    cb_d = din("cb", [128, 8])
    gb_d = din("gbias", [1, 8])
    ng_d = din("norm_g", [1, 512])
    rbT_d = din("rbT", [128, 8, 2, 128])
    rb0_d = din("rb0", [1, 8])
    w_out_d = din("w_out", [D, D])
    w_q_d = din("w_q", [D, D])
    w_kv_d = din("w_kv", [D, 2 * D])
    w_o_d = din("w_o", [D, D])
    w_pq_d = din("w_pq", [D, 2 * D])
    sk_d = din("sub_keys", [2, 128, 128])
    pu_d = din("peer_u", [16384, D])
    pv_d = din("peer_v", [16384, D])
    y_d = nc.dram_tensor("y", [S, D], F32, kind="ExternalOutput").ap()
    if debug:
        h2_d = nc.dram_tensor("h2dbg", [S, D], F32, kind="ExternalOutput").ap()
    else:
        h2_d = nc.dram_tensor("h2s", [S, D], F32, kind="Internal").ap()
    ut_d = nc.dram_tensor("ut_s", [128, 128, 8, 128], BF16, kind="Internal").ap()
    vb_d = nc.dram_tensor("vb_s", [128, 128, D], BF16, kind="Internal").ap()
    h2_b = Buf("h2s")
    ut_b = Buf("ut_s")
    vb_b = Buf("vb_s")

    with ExitStack() as st0:
        fw = FW(nc, st0)
        PE, ACT, DVE, POOL, SP = fw.pe, fw.act, fw.dve, fw.pool, fw.sp
        tns, vec, sca, gps = nc.tensor, nc.vector, nc.scalar, nc.gpsimd

        from contextlib import nullcontext

        cur_scope = [None]

        def mark(name, cond=True):
            if not prof:
                return
            if cur_scope[0] is not None:
                nc.leave_named_scope(cur_scope[0][0], cur_scope[0][1], False)
                cur_scope[0] = None
            if name is not None and cond:
                sid, _ = nc.enter_named_scope(name, False)
                cur_scope[0] = (name, sid)

        def sb(st, name, shape, dt=F32):
            t = st.enter_context(nc.sbuf_tensor("sb_" + name, list(shape), dt))
            return t, Buf(name)

        banks = []
        for i in range(8):
            t = st0.enter_context(nc.psum_tensor("pb%d" % i, [128, 512], F32))
            banks.append((t, Buf("pb%d" % i)))
        bank_rr = [0]

        def bank(lo=0, hi=8):
            i = bank_rr[0]
            if i < lo or i >= hi:
                i = lo
            bank_rr[0] = i + 1 if i + 1 < hi else lo
            return banks[i]

        idf, idf_b = sb(st0, "idf", [128, 128])
        idb, idb_b = sb(st0, "idb", [128, 128], BF16)
        fw.op(POOL, lambda: gps.memset(idf[:], 0.0), writes=[idf_b])
        fw.op(POOL, lambda: gps.affine_select(out=idf[:], in_=idf[:], pattern=[[-1, 128]], compare_op=ALU.not_equal,
                                              fill=1.0, base=0, channel_multiplier=1), reads=[idf_b], writes=[idf_b])
        fw.op(DVE, lambda: vec.tensor_copy(out=idb[:], in_=idf[:]), reads=[idf_b], writes=[idb_b])
        lnv_box = [None, None, 0]
        epsc, epsc_b = sb(st0, "epsc", [128, 1])
        fw.op(POOL, lambda: gps.memset(epsc[:], LN_EPS), writes=[epsc_b])
        stg = [sb(st0, "stg%d" % i, [128, 1024]) for i in range(2)]
        stg_rr = [0]
        cast_rr = [0]

        def cast_copy(out_ap, in_ap, rb, wb):
            k = cast_rr[0] % 3
            cast_rr[0] += 1
            if k == 0:
                fw.op(DVE, lambda: vec.tensor_copy(out=out_ap, in_=in_ap), reads=[rb], writes=[wb])
            elif k == 1:
                fw.op(ACT, lambda: sca.copy(out=out_ap, in_=in_ap), reads=[rb], writes=[wb])
            else:
                fw.op(POOL, lambda: gps.tensor_copy(out=out_ap, in_=in_ap), reads=[rb], writes=[wb])

        def load_weight_bf(w_d, ncols, dst, dst_b):
            for kc in range(8):
                for c0 in range(0, ncols, 1024):
                    cw = min(1024, ncols - c0)
                    s_t, s_b = stg[stg_rr[0] % 2]
                    stg_rr[0] += 1
                    fw.dma(SP, s_t[:, 0:cw], w_d[kc * 128:(kc + 1) * 128, c0:c0 + cw], wbuf=s_b)
                    cast_copy(dst[:, kc, c0:c0 + cw], s_t[:, 0:cw], s_b, dst_b)

        def layer_norm(st_name, xt, xt_b, gi, scr):
            st6, st6_b, mv, mv_b, rs, rs_b = scr
            lnv, lnv_b, base = lnv_box
            gi = gi - base
            for hh in range(2):
                fw.op(DVE, lambda: vec.bn_stats(out=st6[:, hh, :], in_=xt[:, hh * 512:(hh + 1) * 512]),
                      reads=[xt_b], writes=[st6_b])
            fw.op(DVE, lambda: vec.bn_aggr(out=mv[:], in_=st6[:].rearrange("p a b -> p (a b)")), reads=[st6_b], writes=[mv_b])
            fw.op(ACT, lambda: sca.activation(out=rs[:], in_=mv[:, 1:2], func=AF.Sqrt, bias=epsc[:, 0:1]), reads=[mv_b, epsc_b], writes=[rs_b])
            fw.op(DVE, lambda: vec.reciprocal(out=rs[:], in_=rs[:]), reads=[rs_b], writes=[rs_b])
            fw.op(DVE, lambda: vec.scalar_tensor_tensor(out=xt[:], in0=xt[:], scalar=mv[:, 0:1], in1=lnv[:, gi, :],
                                                        op0=ALU.subtract, op1=ALU.mult), reads=[xt_b, mv_b, lnv_b], writes=[xt_b])
            fw.op(DVE, lambda: vec.scalar_tensor_tensor(out=xt[:], in0=xt[:], scalar=rs[:, 0:1], in1=lnv[:, gi + 1, :],
                                                        op0=ALU.mult, op1=ALU.add), reads=[xt_b, rs_b, lnv_b], writes=[xt_b])

        def to_featmajor(src, src_b, srcbf, srcbf_b, dstT, dstT_b, off=0, width=128, src_is_bf=False):
            if not src_is_bf:
                fw.op(ACT, lambda: sca.copy(out=srcbf[:], in_=src[:]), reads=[src_b], writes=[srcbf_b])
            else:
                srcbf, srcbf_b = src, src_b
            pt, pt_b = bank()
            ptv = pt[:].bitcast(BF16)
            for kc in range(8):
                fw.op(PE, lambda: tns.transpose(out=ptv[:, kc * 128:(kc + 1) * 128], in_=srcbf[:, kc * 128:(kc + 1) * 128],
                                                identity=idb[:]), reads=[srcbf_b, idb_b], writes=[pt_b])
            fw.op(DVE, lambda: vec.tensor_copy(out=dstT[:, :, off:off + 128],
                                               in_=ptv.rearrange("p (k t) -> p k t", k=8)), reads=[pt_b], writes=[dstT_b])

        with ExitStack() as stA:
            kxT, kxT_b = sb(stA, "kxT", [128, 8, 256], BF16)
            vx, vx_b = sb(stA, "vx", [128, 2, 4, 257], BF16)
            rbT, rbT_b = sb(stA, "rbT", [128, 8, 2, 128])
            rb0, rb0_b = sb(stA, "rb0", [128, 8])
            cw, cw_b = sb(stA, "cw", [128, 8, 4])
            cb, cb_b = sb(stA, "cb", [128, 8])
            gbias, gbias_b = sb(stA, "gbias", [128, 8])
            ng, ng_b = sb(stA, "ng", [128, 512])
            triu, triu_b = sb(stA, "triu", [128, 128])
            m01, m01_b = sb(stA, "m01", [128, 128])
            fw.dma(SP, rbT[:], rbT_d[:, :, :, :], wbuf=rbT_b)
            fw.dma(SP, rb0[:], rb0_d[0:1, :].partition_broadcast(128), wbuf=rb0_b)
            fw.dma(SP, cw[:], cw_d[:, :, :], wbuf=cw_b)
            fw.dma(SP, cb[:], cb_d[:, :], wbuf=cb_b)
            fw.dma(SP, gbias[:], gb_d[0:1, :].partition_broadcast(128), wbuf=gbias_b)
            fw.dma(SP, ng[:], ng_d[0:1, :].partition_broadcast(128), wbuf=ng_b)
            fw.op(POOL, lambda: gps.memset(triu[:], 1.0), writes=[triu_b])
            fw.op(POOL, lambda: gps.affine_select(out=triu[:], in_=triu[:], pattern=[[1, 128]], compare_op=ALU.is_ge,
                                                  fill=0.0, base=0, channel_multiplier=-1), reads=[triu_b], writes=[triu_b])
            fw.op(POOL, lambda: gps.tensor_copy(out=m01[:], in_=triu[:]), reads=[triu_b], writes=[m01_b])
            fw.op(POOL, lambda: gps.memset(rbT[64:128, :, 0, 0:64], NEG), reads=[rbT_b], writes=[rbT_b])

            with ExitStack() as stM:
                wkv, wkv_b = sb(stM, "wkv", [128, 8, 2 * D], BF16)
                memT, memT_b = sb(stM, "memT", [128, 8, 256], BF16)
                mt, mt_b = sb(stM, "mt", [128, D])
                mtb, mtb_b = sb(stM, "mtb", [128, D], BF16)
                load_weight_bf(w_kv_d, 2 * D, wkv, wkv_b)
                for mc in range(2):
                    fw.dma(SP, mt[:], mem_d[mc * 128:(mc + 1) * 128, :], wbuf=mt_b)
                    to_featmajor(mt, mt_b, mtb, mtb_b, memT, memT_b, off=mc * 128)
                for c in range(8):
                    pt, pt_b = bank()
                    for kc in range(8):
                        fw.op(PE, lambda: tns.matmul(pt[:, 0:256], lhsT=wkv[:, kc, c * 128:(c + 1) * 128], rhs=memT[:, kc, :],
                                                     start=(kc == 0), stop=(kc == 7)), reads=[wkv_b, memT_b], writes=[pt_b])
                    fw.op(ACT, lambda: sca.activation(out=kxT[:, c, :], in_=pt[:, 0:256], func=AF.Copy, scale=1.0 / 16.0),
                          reads=[pt_b], writes=[kxT_b])
                for mc in range(2):
                    for hf in range(2):
                        pt, pt_b = bank()
                        for kc in range(8):
                            fw.op(PE, lambda: tns.matmul(pt[:, :], lhsT=memT[:, kc, mc * 128:(mc + 1) * 128],
                                                         rhs=wkv[:, kc, D + hf * 512:D + (hf + 1) * 512],
                                                         start=(kc == 0), stop=(kc == 7)), reads=[wkv_b, memT_b], writes=[pt_b])
                        fw.op(DVE, lambda: vec.tensor_copy(out=vx[:, mc, 2 * hf:2 * hf + 2, 0:256],
                                                           in_=pt[:, :].rearrange("p (h d) -> p h d", h=2)),
                              reads=[pt_b], writes=[vx_b])
                fw.op(DVE, lambda: vec.memset(vx[:, :, :, 256:257], 1.0), writes=[vx_b])

                ub, ub_b = sb(stM, "ub", [128, D], BF16)
                uts = [sb(stM, "uts%d" % i, [128, 8, 128], BF16) for i in range(2)]
                vbs = [sb(stM, "vbs%d" % i, [128, D], BF16) for i in range(2)]
                for n1 in range(128):
                    s_t, s_b = stg[stg_rr[0] % 2]
                    stg_rr[0] += 1
                    fw.dma(SP, s_t[:, 0:D], pu_d[n1 * 128:(n1 + 1) * 128, :], wbuf=s_b)
                    cast_copy(ub[:], s_t[:, 0:D], s_b, ub_b)
                    u_t, u_b = uts[n1 % 2]
                    to_featmajor(ub, ub_b, None, None, u_t, u_b, off=0, src_is_bf=True)
                    fw.dma(SP, ut_d[n1], u_t[:], wbuf=ut_b, rbuf=u_b, sembuf=u_b)
                    s_t, s_b = stg[stg_rr[0] % 2]
                    stg_rr[0] += 1
                    fw.dma(SP, s_t[:, 0:D], pv_d[n1 * 128:(n1 + 1) * 128, :], wbuf=s_b)
                    v_t, v_b = vbs[n1 % 2]
                    cast_copy(v_t[:], s_t[:, 0:D], s_b, v_b)
                    fw.dma(SP, vb_d[n1], v_t[:], wbuf=vb_b, rbuf=v_b, sembuf=v_b)
                fw.barrier()
            if stop == "setup":
                return nc

            win, win_b = sb(stA, "win", [128, 8, INW], BF16)
            wout, wout_b = sb(stA, "wout", [128, 8, D], BF16)
            wq, wq_b = sb(stA, "wq", [128, 8, D], BF16)
            wo, wo_b = sb(stA, "wo", [128, 8, D], BF16)
            load_weight_bf(w_in_d, INW, win, win_b)
            load_weight_bf(w_out_d, D, wout, wout_b)
            load_weight_bf(w_q_d, D, wq, wq_b)
            load_weight_bf(w_o_d, D, wo, wo_b)
            lnvA, lnvA_b = sb(stA, "lnvA", [128, 6, D])
            for i_ in range(6):
                fw.dma(SP, lnvA[:, i_, :], vec_d[i_:i_ + 1, :].partition_broadcast(128), wbuf=lnvA_b, nowait_prev=(i_ > 0))
            lnv_box[0], lnv_box[1], lnv_box[2] = lnvA, lnvA_b, 0
            hbs = [sb(stA, "hb%d" % i, [128, D]) for i in range(1)]
            hbf, hbf_b = sb(stA, "hbf", [128, D], BF16)
            hT, hT_b = sb(stA, "hT", [128, 8, 128], BF16)
            st6, st6_b = sb(stA, "st6", [128, 2, 6])
            mv, mv_b = sb(stA, "mv", [128, 2])
            rs, rs_b = sb(stA, "rs", [128, 1])
            lnscr = (st6, st6_b, mv, mv_b, rs, rs_b)
            qT, qT_b = sb(stA, "qT", [128, 4, 128], BF16)
            kring, kring_b = sb(stA, "kring", [128, 4, 5, 128], BF16)
            vring, vring_b = sb(stA, "vring", [128, 5, 8, 65], BF16)
            pTs = [sb(stA, "pT%d" % i, [128, 5, 128], BF16) for i in range(2)]
            stmp, stmp_b = sb(stA, "stmp", [128, 256])
            cat, cat_b = sb(stA, "cat", [128, D], BF16)
            rcp, rcp_b = sb(stA, "rcp", [128, 8])
            pc, pc_b = sb(stA, "pc", [128, 8, 131])
            cv, cv_b = sb(stA, "cv", [128, 4, 128])
            qkT, qkT_b = sb(stA, "qkT", [128, 8, 128], BF16)
            vaug, vaug_b = sb(stA, "vaug", [128, 4, 129], BF16)
            og, og_b = sb(stA, "og", [128, 512])
            gif, gif_b = sb(stA, "gif", [128, 8])
            lgf, lgf_b = sb(stA, "lgf", [128, 4])
            dbias, dbias_b = sb(stA, "dbias", [128, 4])
            DT, DT_b = sb(stA, "DT", [128, 4, 128])
            Eb, Eb_b = sb(stA, "Eb", [128, 4, 128])
            qpT, qpT_b = sb(stA, "qpT", [128, 4, 128], BF16)
            WT, WT_b = sb(stA, "WT", [128, 4, 128], BF16)
            Cst, Cst_b = sb(stA, "Cst", [128, 4, 129])
            Cbf, Cbf_b = sb(stA, "Cbf", [128, 4, 129], BF16)
            ksc, ksc_b = sb(stA, "ksc", [128, 4, 128], BF16)
            hm, hm_b = sb(stA, "hm", [128, 4, 128])
            rden, rden_b = sb(stA, "rden", [128, 4])
            hst, hst_b = sb(stA, "hst", [128, 4, 6])
            hmv, hmv_b = sb(stA, "hmv", [128, 4, 2])
            hrs, hrs_b = sb(stA, "hrs", [128, 4])
            pxT, pxT_b = qkT, qkT_b
            ob, ob_b = cat, cat_b
            rdx, rdx_b = sb(stA, "rdx", [128, 4])

            fw.op(POOL, lambda: gps.memset(pc[:], 0.0), writes=[pc_b])
            fw.op(POOL, lambda: gps.memset(Cst[:], 0.0), writes=[Cst_b])
            fw.op(POOL, lambda: gps.memset(Cbf[:], 0.0), writes=[Cbf_b])
            fw.op(POOL, lambda: gps.memset(vring[:, :, :, 64:65], 1.0), writes=[vring_b])
            fw.op(POOL, lambda: gps.memset(vaug[:, :, 128:129], 1.0), writes=[vaug_b])
            for i in range(2):
                fw.op(POOL, lambda: gps.memset(pTs[i][0][:], 0.0), writes=[pTs[i][1]])

            def load_x(i):
                fw.dma(SP, hbs[0][0][:], x_d[i * 128:(i + 1) * 128, :], wbuf=hbs[0][1])

            for i in range(NT):
                load_x(i)
                hb, hb_b = hbs[0]
                slot = i % 5
                mark("A_lnin", i == 2)
                layer_norm("lnin", hb, hb_b, 0, lnscr)
                if tap == "h0":
                    fw.dma(SP, h2_d[i * 128:(i + 1) * 128, :], hb[:], wbuf=h2_b, rbuf=hb_b, sembuf=hb_b)
                    continue
                to_featmajor(hb, hb_b, hbf, hbf_b, hT, hT_b)
                mark("A_inproj", i == 2)
                for grp in range(4):
                    pt, pt_b = bank()
                    for bi in range(4):
                        blk = grp * 4 + bi
                        col0 = blk * 128 if blk < 8 else 1536 + (blk - 8) * 128
                        for kc in range(8):
                            fw.op(PE, lambda: tns.matmul(pt[:, bi * 128:(bi + 1) * 128], lhsT=win[:, kc, col0:col0 + 128],
                                                         rhs=hT[:, kc, :], start=(kc == 0), stop=(kc == 7)),
                                  reads=[win_b, hT_b], writes=[pt_b])
                    pv3 = pt[:, :].rearrange("p (b t) -> p b t", b=4)
                    if grp == 0:
                        fw.op(ACT, lambda: sca.activation(out=qT[:], in_=pv3, func=AF.Copy, scale=0.125),
                              reads=[pt_b], writes=[qT_b])
                    elif grp == 1:
                        fw.op(ACT, lambda: sca.copy(out=kring[:, :, slot, :], in_=pv3), reads=[pt_b], writes=[kring_b])
                    else:
                        b0 = (grp - 2) * 4
                        fw.op(ACT, lambda: sca.copy(out=pc[:, b0:b0 + 4, 3:131], in_=pv3), reads=[pt_b], writes=[pc_b])
                for j, col0 in enumerate((1024, 2560, 3072)):
                    pt, pt_b = bank()
                    for kc in range(8):
                        fw.op(PE, lambda: tns.matmul(pt[:, :], lhsT=hT[:, kc, :], rhs=win[:, kc, col0:col0 + 512],
                                                     start=(kc == 0), stop=(kc == 7)), reads=[win_b, hT_b], writes=[pt_b])
                    if j == 0:
                        fw.op(DVE, lambda: vec.tensor_copy(out=vring[:, slot, :, 0:64],
                                                           in_=pt[:, :].rearrange("p (h d) -> p h d", h=8)),
                              reads=[pt_b], writes=[vring_b])
                    elif j == 1:
                        fw.op(DVE, lambda: vec.tensor_copy(out=vaug[:, :, 0:128],
                                                           in_=pt[:, :].rearrange("p (h d) -> p h d", h=4)),
                              reads=[pt_b], writes=[vaug_b])
                    else:
                        fw.op(ACT, lambda: sca.activation(out=og[:], in_=pt[:, :], func=AF.Sigmoid), reads=[pt_b], writes=[og_b])
                pt, pt_b = bank()
                for kc in range(8):
                    fw.op(PE, lambda: tns.matmul(pt[:, 0:8], lhsT=hT[:, kc, :], rhs=win[:, kc, 3584:3592],
                                                 start=(kc == 0), stop=(kc == 7)), reads=[win_b, hT_b], writes=[pt_b])
                fw.op(DVE, lambda: vec.tensor_tensor(out=gif[:], in0=pt[:, 0:8], in1=gbias[:], op=ALU.add),
                      reads=[pt_b, gbias_b], writes=[gif_b])

                mark("A_attn", i == 2)
                nr = min(i, 4) + 1
                pacc = [banks[0], banks[1]]
                for h in range(8):
                    pr, hh = h // 2, h % 2
                    p0 = hh * 64
                    pA, pA_b = bank(2, 8)
                    pB, pB_b = bank(2, 8)
                    pT, pT_b = pTs[h % 2]
                    for r in range(nr):
                        sl = (i - r) % 5
                        dst = pA[:, r * 128:(r + 1) * 128] if r < 4 else pB[:, 0:128]
                        fw.op(PE, lambda: tns.matmul(dst, lhsT=kring[p0:p0 + 64, pr, sl, :], rhs=qT[p0:p0 + 64, pr, :],
                                                     start=True, stop=True), reads=[kring_b, qT_b],
                              writes=[pA_b if r < 4 else pB_b])
                    n01 = min(nr, 2)
                    fw.op(DVE, lambda: vec.tensor_tensor(out=stmp[:, 0:n01 * 128], in0=pA[:, 0:n01 * 128],
                                                         in1=rbT[:, h, 0:n01, :].rearrange("p r t -> p (r t)"), op=ALU.add),
                          reads=[pA_b, rbT_b], writes=[stmp_b])
                    fw.op(ACT, lambda: sca.activation(out=pT[:, 0:n01, :].rearrange("p r t -> p (r t)"), in_=stmp[:, 0:n01 * 128],
                                                      func=AF.Exp), reads=[stmp_b], writes=[pT_b])
                    if nr > 2:
                        n23 = min(nr, 4) - 2
                        fw.op(ACT, lambda: sca.activation(out=pT[:, 2:2 + n23, :].rearrange("p r t -> p (r t)"),
                                                          in_=pA[:, 256:256 + n23 * 128], func=AF.Exp, bias=rb0[:, h:h + 1]),
                              reads=[pA_b, rb0_b], writes=[pT_b])
                    if nr > 4:
                        fw.op(ACT, lambda: sca.activation(out=pT[64:128, 4, :], in_=pB[64:128, 0:128], func=AF.Exp,
                                                          bias=rb0[64:128, h:h + 1]), reads=[pB_b, rb0_b], writes=[pT_b])
                        fw.op(ACT, lambda: sca.activation(out=pT[0:64, 4, 0:64], in_=pB[0:64, 0:64], func=AF.Exp,
                                                          bias=rb0[0:64, h:h + 1]), reads=[pB_b, rb0_b], writes=[pT_b])
                    pc_t, pc_tb = pacc[h // 4]
                    o0 = (h % 4) * 65
                    for r in range(nr):
                        sl = (i - r) % 5
                        fw.op(PE, lambda: tns.matmul(pc_t[:, o0:o0 + 65], lhsT=pT[:, r, :], rhs=vring[:, sl, h, :],
                                                     start=(r == 0), stop=(r == nr - 1)), reads=[pT_b, vring_b], writes=[pc_tb])
                for half in range(2):
                    pc_t, pc_tb = pacc[half]
                    v3 = pc_t[:, 0:260].rearrange("p (h d) -> p h d", h=4)
                    fw.op(DVE, lambda: vec.reciprocal(out=rcp[:, half * 4:half * 4 + 4], in_=v3[:, :, 64]),
                          reads=[pc_tb], writes=[rcp_b])
                    fw.op(DVE, lambda: vec.tensor_tensor(
                        out=cat[:, half * 256:(half + 1) * 256].rearrange("p (h d) -> p h d", h=4), in0=v3[:, :, 0:64],
                        in1=rcp[:, half * 4:half * 4 + 4].unsqueeze(2).to_broadcast([128, 4, 64]), op=ALU.mult),
                        reads=[pc_tb, rcp_b], writes=[cat_b])

                mark("A_mlstm", i == 2)
                for cg in range(2):
                    cvb = [Buf("cv%d" % k) for k in range(4)]
                    for b4 in range(4):
                        blk = cg * 4 + b4
                        fw.op(POOL, lambda: gps.tensor_scalar(out=cv[:, b4, :], in0=pc[:, blk, 0:128], scalar1=cw[:, blk, 0:1],
                                                              scalar2=cb[:, blk:blk + 1], op0=ALU.mult, op1=ALU.add),
                              reads=[pc_b, cw_b, cb_b], writes=[cvb[b4]] + ([cv_b] if b4 == 0 else []))
                    for j in range(1, 4):
                        for b4 in range(4):
                            blk = cg * 4 + b4
                            fw.op(DVE, lambda: vec.scalar_tensor_tensor(out=cv[:, b4, :], in0=pc[:, blk, j:j + 128],
                                                                        scalar=cw[:, blk, j:j + 1], in1=cv[:, b4, :],
                                                                        op0=ALU.mult, op1=ALU.add),
                                  reads=[pc_b, cw_b, cvb[b4]], writes=[cvb[b4]])
                    for bb_ in cvb:
                        for k_, v_ in bb_.w.items():
                            _mx(cv_b.w, k_, v_)
                        for k_, v_ in bb_.r.items():
                            _mx(cv_b.r, k_, v_)
                    fw.op(ACT, lambda: sca.activation(out=qkT[:, cg * 4:cg * 4 + 4, :], in_=cv[:], func=AF.Silu),
                          reads=[cv_b], writes=[qkT_b])
                fw.op(POOL, lambda: gps.tensor_copy(out=pc[:, :, 0:3], in_=pc[:, :, 128:131]), reads=[pc_b], writes=[pc_b])
                fw.op(ACT, lambda: sca.activation(out=lgf[:], in_=gif[:, 4:8], func=AF.Exp, scale=-1.0), reads=[gif_b], writes=[lgf_b])
                fw.op(ACT, lambda: sca.activation(out=lgf[:], in_=lgf[:], func=AF.Ln, bias=1.0), reads=[lgf_b], writes=[lgf_b])
                fw.op(DVE, lambda: vec.tensor_scalar(out=lgf[:], in0=lgf[:], scalar1=-1.0, scalar2=None, op0=ALU.mult),
                      reads=[lgf_b], writes=[lgf_b])
                pbb, pbb_b = bank()
                for h in range(4):
                    fw.op(PE, lambda: tns.matmul(pbb[:, h * 128:(h + 1) * 128], lhsT=lgf[:, h:h + 1].to_broadcast([128, 128]),
                                                 rhs=triu[:], start=True, stop=True), reads=[lgf_b, triu_b], writes=[pbb_b])
                pcol, pcol_b = bank()
                fw.op(PE, lambda: tns.matmul(pcol[:, 0:4], lhsT=triu[:], rhs=lgf[:], start=True, stop=True),
                      reads=[lgf_b, triu_b], writes=[pcol_b])
                fw.op(DVE, lambda: vec.scalar_tensor_tensor(out=dbias[:], in0=gif[:, 0:4], scalar=-0.5 * math.log(128.0),
                                                            in1=pcol[:, 0:4], op0=ALU.add, op1=ALU.subtract),
                      reads=[gif_b, pcol_b], writes=[dbias_b])
                fw.op(ACT, lambda: sca.activation(out=Eb[:].rearrange("p h t -> p (h t)"), in_=pbb[:, :], func=AF.Exp),
                      reads=[pbb_b], writes=[Eb_b])
                for h in range(4):
                    fw.op(ACT, lambda: sca.activation(out=DT[:, h, :], in_=pbb[:, h * 128:(h + 1) * 128], func=AF.Exp,
                                                      bias=dbias[:, h:h + 1]), reads=[pbb_b, dbias_b], writes=[DT_b])
                fw.op(POOL, lambda: gps.tensor_tensor(out=DT[:], in0=DT[:], in1=m01[:].unsqueeze(1).to_broadcast([128, 4, 128]),
                                                      op=ALU.mult), reads=[DT_b, m01_b], writes=[DT_b])
                fw.op(POOL, lambda: gps.tensor_tensor(out=qpT[:], in0=qkT[:, 0:4, :], in1=Eb[:], op=ALU.mult),
                      reads=[qkT_b, Eb_b], writes=[qpT_b])
                pqk, pqk_b = bank()
                for h in range(4):
                    fw.op(PE, lambda: tns.matmul(pqk[:, h * 128:(h + 1) * 128], lhsT=qkT[:, 4 + h, :], rhs=qkT[:, h, :],
                                                 start=True, stop=True), reads=[qkT_b], writes=[pqk_b])
                fw.op(DVE, lambda: vec.tensor_tensor(out=WT[:].rearrange("p h t -> p (h t)"), in0=pqk[:, :],
                                                     in1=DT[:].rearrange("p h t -> p (h t)"), op=ALU.mult),
                      reads=[pqk_b, DT_b], writes=[WT_b])
                pn = [bank(), bank()]
                for h in range(4):
                    pn_t, pn_b = pn[h // 2]
                    o0 = (h % 2) * 129
                    fw.op(PE, lambda: tns.matmul(pn_t[:, o0:o0 + 129], lhsT=WT[:, h, :], rhs=vaug[:, h, :], start=True, stop=False),
                          reads=[WT_b, vaug_b], writes=[pn_b])
                    fw.op(PE, lambda: tns.matmul(pn_t[:, o0:o0 + 129], lhsT=qpT[:, h, :], rhs=Cbf[:, h, :], start=False, stop=True),
                          reads=[qpT_b, Cbf_b], writes=[pn_b])
                for half in range(2):
                    pn_t, pn_b = pn[half]
                    v3 = pn_t[:, 0:258].rearrange("p (h d) -> p h d", h=2)
                    fw.op(ACT, lambda: sca.activation(out=rden[:, half * 2:half * 2 + 2], in_=v3[:, :, 128], func=AF.Abs),
                          reads=[pn_b], writes=[rden_b])
                    fw.op(DVE, lambda: vec.tensor_scalar(out=rden[:, half * 2:half * 2 + 2], in0=rden[:, half * 2:half * 2 + 2],
                                                         scalar1=1.0, scalar2=None, op0=ALU.max), reads=[rden_b], writes=[rden_b])
                    fw.op(DVE, lambda: vec.reciprocal(out=rden[:, half * 2:half * 2 + 2], in_=rden[:, half * 2:half * 2 + 2]),
                          reads=[rden_b], writes=[rden_b])
                    fw.op(DVE, lambda: vec.tensor_tensor(out=hm[:, half * 2:half * 2 + 2, :], in0=v3[:, :, 0:128],
                                                         in1=rden[:, half * 2:half * 2 + 2].unsqueeze(2).to_broadcast([128, 2, 128]),
                                                         op=ALU.mult), reads=[pn_b, rden_b], writes=[hm_b])
                pkt, pkt_b = bank()
                pktv = pkt[:].bitcast(BF16)
                for h in range(4):
                    fw.op(PE, lambda: tns.transpose(out=pktv[:, h * 128:(h + 1) * 128], in_=qkT[:, 4 + h, :], identity=idb[:]),
                          reads=[qkT_b, idb_b], writes=[pkt_b])
                for h in range(4):
                    fw.op(ACT, lambda: sca.activation(out=ksc[:, h, :], in_=pktv[:, h * 128:(h + 1) * 128], func=AF.Copy,
                                                      scale=DT[:, h, 127:128]), reads=[pkt_b, DT_b], writes=[ksc_b])
                pcu = [bank(), bank()]
                for h in range(4):
                    pcu_t, pcu_b = pcu[h // 2]
                    o0 = (h % 2) * 129
                    fw.op(PE, lambda: tns.matmul(pcu_t[:, o0:o0 + 129], lhsT=ksc[:, h, :], rhs=vaug[:, h, :], start=True, stop=True),
                          reads=[ksc_b, vaug_b], writes=[pcu_b])
                for h in range(4):
                    pcu_t, pcu_b = pcu[h // 2]
                    o0 = (h % 2) * 129
                    fw.op(DVE, lambda: vec.scalar_tensor_tensor(out=Cst[:, h, :], in0=Cst[:, h, :], scalar=Eb[:, h, 127:128],
                                                                in1=pcu_t[:, o0:o0 + 129], op0=ALU.mult, op1=ALU.add),
                          reads=[Cst_b, Eb_b, pcu_b], writes=[Cst_b])
                fw.op(POOL, lambda: gps.tensor_copy(out=Cbf[:], in_=Cst[:]), reads=[Cst_b], writes=[Cbf_b])
                for h in range(4):
                    fw.op(DVE, lambda: vec.bn_stats(out=hst[:, h, :], in_=hm[:, h, :]), reads=[hm_b], writes=[hst_b])
                for h in range(4):
                    fw.op(DVE, lambda: vec.bn_aggr(out=hmv[:, h, :], in_=hst[:, h, :]), reads=[hst_b], writes=[hmv_b])
                fw.op(ACT, lambda: sca.activation(out=hrs[:], in_=hmv[:, :, 1], func=AF.Sqrt, bias=epsc[:, 0:1]), reads=[hmv_b, epsc_b], writes=[hrs_b])
                fw.op(DVE, lambda: vec.reciprocal(out=hrs[:], in_=hrs[:]), reads=[hrs_b], writes=[hrs_b])
                fw.op(DVE, lambda: vec.tensor_tensor(out=hm[:], in0=hm[:], in1=hmv[:, :, 0:1].to_broadcast([128, 4, 128]),
                                                     op=ALU.subtract), reads=[hm_b, hmv_b], writes=[hm_b])
                fw.op(DVE, lambda: vec.tensor_tensor(out=hm[:], in0=hm[:], in1=hrs[:].unsqueeze(2).to_broadcast([128, 4, 128]),
                                                     op=ALU.mult), reads=[hm_b, hrs_b], writes=[hm_b])
                fw.op(POOL, lambda: gps.tensor_tensor(out=hm[:].rearrange("p h d -> p (h d)"), in0=hm[:].rearrange("p h d -> p (h d)"),
                                                      in1=ng[:], op=ALU.mult), reads=[hm_b, ng_b], writes=[hm_b])
                fw.op(POOL, lambda: gps.tensor_tensor(out=cat[:, 512:1024], in0=hm[:].rearrange("p h d -> p (h d)"), in1=og[:],
                                                      op=ALU.mult), reads=[hm_b, og_b], writes=[cat_b])

                mark("A_wout", i == 2)
                if tap == "cat":
                    fw.op(ACT, lambda: sca.copy(out=hb[:], in_=cat[:]), reads=[cat_b], writes=[hb_b])
                    fw.dma(SP, h2_d[i * 128:(i + 1) * 128, :], hb[:], wbuf=h2_b, rbuf=hb_b, sembuf=hb_b)
                    continue
                to_featmajor(cat, cat_b, None, None, hT, hT_b, src_is_bf=True)
                for half in range(2):
                    pt, pt_b = bank()
                    for kc in range(8):
                        fw.op(PE, lambda: tns.matmul(pt[:, :], lhsT=hT[:, kc, :], rhs=wout[:, kc, half * 512:(half + 1) * 512],
                                                     start=(kc == 0), stop=(kc == 7)), reads=[wout_b, hT_b], writes=[pt_b])
                    fw.op(DVE, lambda: vec.scalar_tensor_tensor(out=hb[:, half * 512:(half + 1) * 512],
                                                                in0=hb[:, half * 512:(half + 1) * 512], scalar=ALPHA,
                                                                in1=pt[:, :], op0=ALU.mult, op1=ALU.add),
                          reads=[hb_b, pt_b], writes=[hb_b])
                layer_norm("ln1", hb, hb_b, 2, lnscr)
                if tap == "h1":
                    fw.dma(SP, h2_d[i * 128:(i + 1) * 128, :], hb[:], wbuf=h2_b, rbuf=hb_b, sembuf=hb_b)
                    continue

                mark("A_xattn", i == 2)
                to_featmajor(hb, hb_b, hbf, hbf_b, hT, hT_b)
                for grp in range(2):
                    pt, pt_b = bank()
                    for bi in range(4):
                        c = grp * 4 + bi
                        for kc in range(8):
                            fw.op(PE, lambda: tns.matmul(pt[:, bi * 128:(bi + 1) * 128], lhsT=wq[:, kc, c * 128:(c + 1) * 128],
                                                         rhs=hT[:, kc, :], start=(kc == 0), stop=(kc == 7)),
                                  reads=[wq_b, hT_b], writes=[pt_b])
                    fw.op(ACT, lambda: sca.copy(out=hbf[:, grp * 512:(grp + 1) * 512], in_=pt[:, :]),
                          reads=[pt_b], writes=[hbf_b])
                for grp in range(2):
                    pt, pt_b = bank()
                    for bi in range(4):
                        hx = grp * 2 + bi // 2
                        mc = bi % 2
                        for hf in range(2):
                            fw.op(PE, lambda: tns.matmul(pt[:, bi * 128:(bi + 1) * 128],
                                                         lhsT=kxT[:, hx * 2 + hf, mc * 128:(mc + 1) * 128],
                                                         rhs=hbf[:, (hx * 2 + hf) * 128:(hx * 2 + hf + 1) * 128],
                                                         start=(hf == 0), stop=(hf == 1)),
                                  reads=[kxT_b, hbf_b], writes=[pt_b])
                    fw.op(ACT, lambda: sca.activation(out=pxT[:, grp * 4:grp * 4 + 4, :].rearrange("p b t -> p (b t)"), in_=pt[:, :],
                                                      func=AF.Exp), reads=[pt_b], writes=[pxT_b])
                for hx in range(4):
                    pt, pt_b = bank()
                    for mc in range(2):
                        fw.op(PE, lambda: tns.matmul(pt[:, 0:257], lhsT=pxT[:, hx * 2 + mc, :], rhs=vx[:, mc, hx, :],
                                                     start=(mc == 0), stop=(mc == 1)), reads=[pxT_b, vx_b], writes=[pt_b])
                    fw.op(DVE, lambda: vec.reciprocal(out=rdx[:, hx:hx + 1], in_=pt[:, 256:257]), reads=[pt_b], writes=[rdx_b])
                    fw.op(DVE, lambda: vec.tensor_scalar(out=ob[:, hx * 256:(hx + 1) * 256], in0=pt[:, 0:256],
                                                         scalar1=rdx[:, hx:hx + 1], scalar2=None, op0=ALU.mult),
                          reads=[pt_b, rdx_b], writes=[ob_b])
                to_featmajor(ob, ob_b, None, None, hT, hT_b, src_is_bf=True)
                for half in range(2):
                    pt, pt_b = bank()
                    for kc in range(8):
                        fw.op(PE, lambda: tns.matmul(pt[:, :], lhsT=hT[:, kc, :], rhs=wo[:, kc, half * 512:(half + 1) * 512],
                                                     start=(kc == 0), stop=(kc == 7)), reads=[wo_b, hT_b], writes=[pt_b])
                    fw.op(DVE, lambda: vec.scalar_tensor_tensor(out=hb[:, half * 512:(half + 1) * 512],
                                                                in0=hb[:, half * 512:(half + 1) * 512], scalar=ALPHA,
                                                                in1=pt[:, :], op0=ALU.mult, op1=ALU.add),
                          reads=[hb_b, pt_b], writes=[hb_b])
                layer_norm("ln2", hb, hb_b, 4, lnscr)
                fw.dma(SP, h2_d[i * 128:(i + 1) * 128, :], hb[:], wbuf=h2_b, rbuf=hb_b, sembuf=hb_b)
                mark(None)
            fw.barrier()
        if stop == "A":
            return nc

        TB = 2 if NT % 2 == 0 else 1
        T = TB * 128
        with ExitStack() as stB:
            wpq, wpq_b = sb(stB, "wpq", [128, 8, 2 * D], BF16)
            ksT, ksT_b = sb(stB, "ksT", [128, 2, 128], BF16)
            skf, skf_b = sb(stB, "skf", [128, 128])
            skb, skb_b = sb(stB, "skb", [128, 128], BF16)
            load_weight_bf(w_pq_d, 2 * D, wpq, wpq_b)
            lnvB, lnvB_b = sb(stB, "lnvB", [128, 2, D])
            for i_ in range(2):
                fw.dma(SP, lnvB[:, i_, :], vec_d[6 + i_:7 + i_, :].partition_broadcast(128), wbuf=lnvB_b, nowait_prev=(i_ > 0))
            lnv_box[0], lnv_box[1], lnv_box[2] = lnvB, lnvB_b, 6
            for p in range(2):
                fw.dma(SP, skf[:], sk_d[p], wbuf=skf_b)
                fw.op(ACT, lambda: sca.copy(out=skb[:], in_=skf[:]), reads=[skf_b], writes=[skb_b])
                pt, pt_b = bank(4, 8)
                ptv = pt[:].bitcast(BF16)
                fw.op(PE, lambda: tns.transpose(out=ptv[:, 0:128], in_=skb[:], identity=idb[:]), reads=[skb_b, idb_b], writes=[pt_b])
                fw.op(DVE, lambda: vec.tensor_copy(out=ksT[:, p, :], in_=ptv[:, 0:128]), reads=[pt_b], writes=[ksT_b])
            h2t = [sb(stB, "h2t%d" % i, [128, D]) for i in range(2)]
            h2bf, h2bf_b = sb(stB, "h2bf", [128, D], BF16)
            h2Ts = [sb(stB, "h2T%d" % i, [128, 8, T], BF16) for i in range(2)]
            pqT, pqT_b = sb(stB, "pqT", [128, 16, T], BF16)
            ssb, ssb_b = sb(stB, "ssb", [128, 16, 128])
            srep, srep_b = sb(stB, "srep", [128, 16, 128])
            vtop, vtop_b = sb(stB, "vtop", [128, 16, 16])
            cand, cand_b = ssb[:].rearrange("p (h a) n -> p h (a n)", a=2), ssb_b
            cand2, cand2_b = srep[:].rearrange("p (h a) n -> p h (a n)", a=2), srep_b
            best, best_b = sb(stB, "best", [128, 8, 16])
            etmp, etmp_b = sb(stB, "etmp", [128, 8, 16])
            zz, zz_b = sb(stB, "zz", [128, 8])
            c1, c1_b = sb(stB, "c1", [128, 8])
            v0c, v0c_b = sb(stB, "v0c", [128, 8, 16])
            thr, thr_b = sb(stB, "thr", [128, 8, 16])
            bia, bia_b = sb(stB, "bia", [128, 8, 16])
            v0Ts = [sb(stB, "v0T%d" % i, [128, 128]) for i in range(TB)]
            thrTs = [sb(stB, "thrT%d" % i, [128, 128]) for i in range(TB)]
            biaTs = [sb(stB, "biaT%d" % i, [128, 128]) for i in range(TB)]
            pcs, pcs_b = sb(stB, "pcs", [128, 3, 128], BF16)
            rres, rres_b = sb(stB, "rres", [128, 128])
            At = [(sb(stB, "At%d" % i, [128, 4, 128], BF16)[0], [Buf("At%d_%d" % (i, k)) for k in range(4)]) for i in range(3)]
            Bt = [(sb(stB, "Bt%d" % i, [128, 4, 128], BF16)[0], [Buf("Bt%d_%d" % (i, k)) for k in range(4)]) for i in range(3)]
            E4 = [sb(stB, "E4%d" % i, [128, 512]) for i in range(2)]
            ebTs = [sb(stB, "ebT%d" % i, [128, 128]) for i in range(TB)]
            NB = 6
            Qrep = [sb(stB, "Qrep%d" % i, [128, 4, 128], BF16) for i in range(4)]
            Gs, Gs_b = sb(stB, "Gs", [128, 128, T], BF16)
            utc = [sb(stB, "utc%d" % i, [128, 8, 128], BF16) for i in range(NB)]
            vbc = [sb(stB, "vbc%d" % i, [128, D], BF16) for i in range(NB)]
            ag = [sb(stB, "ag%d" % i, [128, T], BF16) for i in range(3)]
            cT = [sb(stB, "cT%d" % i, [128, T], BF16) for i in range(3)]
            print("phaseB sbuf remaining", nc.sbuf_bytes_remaining)
            st6, st6_b = sb(stB, "st6B", [128, 2, 6])
            mv, mv_b = sb(stB, "mvB", [128, 2])
            rs, rs_b = sb(stB, "rsB", [128, 1])
            lnscr = (st6, st6_b, mv, mv_b, rs, rs_b)

            NTB = NT // TB
            per = 512 // T
            fst = {}
            LT = h2t[1 % len(h2t)]

            def piece_load(bt, ts_):
                hT, hT_b = h2Ts[bt % 2]
                t_, t_b = h2t[0]
                row0 = (bt * TB + ts_) * 128
                fw.dma(SP, t_[:], h2_d[row0:row0 + 128, :], wbuf=t_b, rbuf=h2_b)
                fw.op(ACT, lambda: sca.copy(out=h2bf[:], in_=t_[:]), reads=[t_b], writes=[h2bf_b])
                pt, pt_b = bank(4, 8)
                ptv = pt[:].bitcast(BF16)
                for kc in range(8):
                    fw.op(PE, lambda: tns.transpose(out=ptv[:, kc * 128:(kc + 1) * 128], in_=h2bf[:, kc * 128:(kc + 1) * 128],
                                                    identity=idb[:]), reads=[h2bf_b, idb_b], writes=[pt_b])
                fw.op(DVE, lambda: vec.tensor_copy(out=hT[:, :, ts_ * 128:(ts_ + 1) * 128],
                                                   in_=ptv.rearrange("p (k t) -> p k t", k=8)), reads=[pt_b], writes=[hT_b])

            def piece_pq(bt, g0):
                hT, hT_b = h2Ts[bt % 2]
                pt, pt_b = bank(4, 8)
                for bi in range(per):
                    hp = g0 + bi
                    for kc in range(8):
                        fw.op(PE, lambda: tns.matmul(pt[:, bi * T:(bi + 1) * T], lhsT=wpq[:, kc, hp * 128:(hp + 1) * 128],
                                                     rhs=hT[:, kc, :], start=(kc == 0), stop=(kc == 7)),
                              reads=[wpq_b, hT_b], writes=[pt_b])
                fw.op(ACT, lambda: sca.copy(out=pqT[:, g0:g0 + per, :], in_=pt[:, 0:per * T].rearrange("p (b t) -> p b t", b=per)),
                      reads=[pt_b], writes=[pqT_b])

            def piece_s(ts_, g):
                tsl = slice(ts_ * 128, (ts_ + 1) * 128)
                pt, pt_b = bank(4, 8)
                for bi in range(4):
                    hp = g * 4 + bi
                    fw.op(PE, lambda: tns.matmul(pt[:, bi * 128:(bi + 1) * 128], lhsT=pqT[:, hp, tsl], rhs=ksT[:, hp % 2, :],
                                                 start=True, stop=True), reads=[pqT_b, ksT_b], writes=[pt_b])
                fw.op(ACT, lambda: sca.copy(out=ssb[:, g * 4:g * 4 + 4, :], in_=pt[:, :].rearrange("p (b n) -> p b n", b=4)),
                      reads=[pt_b], writes=[ssb_b])

            def piece_top_a(ts_):
                fst["vt"] = [Buf("vt%d" % k) for k in range(32)]
                fst["sr"] = [Buf("sr%d" % k) for k in range(16)]
                vt_bs = fst["vt"]
                for hp in range(16):
                    fw.op(DVE, lambda: vec.max(out=vtop[:, hp, 0:8], in_=ssb[:, hp, :]), reads=[ssb_b, vtop_b],
                          writes=[vt_bs[hp * 2]] + ([vtop_b] if hp == 0 else []))

            def piece_top_b(ts_):
                vt_bs, sr_bs = fst["vt"], fst["sr"]
                for hp in range(16):
                    fw.op(DVE, lambda: vec.match_replace(out=srep[:, hp, :], in_to_replace=vtop[:, hp, 0:8], in_values=ssb[:, hp, :],
                                                         imm_value=-1e30), reads=[ssb_b, vt_bs[hp * 2], srep_b], writes=[sr_bs[hp]])

            def piece_top_c(ts_):
                vt_bs, sr_bs = fst["vt"], fst["sr"]
                for hp in range(16):
                    fw.op(DVE, lambda: vec.max(out=vtop[:, hp, 8:16], in_=srep[:, hp, :]), reads=[sr_bs[hp], vtop_b], writes=[vt_bs[hp * 2 + 1]])
                vt4 = vtop[:].rearrange("p (h q) k -> p h q k", q=2)
                fw.op(POOL, lambda: gps.tensor_tensor(out=cand[:].rearrange("p h (a b) -> p h a b", a=16),
                                                      in0=vt4[:, :, 0, :].unsqueeze(3).to_broadcast([128, 8, 16, 16]),
                                                      in1=vt4[:, :, 1, :].unsqueeze(2).to_broadcast([128, 8, 16, 16]), op=ALU.add),
                      reads=vt_bs, writes=[cand_b])

            def piece_best(ts_):
                vt_bs, sr_bs = fst["vt"], fst["sr"]
                bs_bs = [Buf("bs%d" % k) for k in range(16)]
                c2_bs = [Buf("c2%d" % k) for k in range(8)]
                for h in range(8):
                    fw.op(DVE, lambda: vec.max(out=best[:, h, 0:8], in_=cand[:, h, :]), reads=[cand_b, best_b],
                          writes=[bs_bs[h * 2]] + ([best_b] if h == 0 else []))
                for h in range(8):
                    fw.op(DVE, lambda: vec.match_replace(out=cand2[:, h, :], in_to_replace=best[:, h, 0:8], in_values=cand[:, h, :],
                                                         imm_value=-1e30), reads=[cand_b, bs_bs[h * 2], cand2_b] + sr_bs, writes=[c2_bs[h]])
                for h in range(8):
                    fw.op(DVE, lambda: vec.max(out=best[:, h, 8:16], in_=cand2[:, h, :]), reads=[c2_bs[h], best_b], writes=[bs_bs[h * 2 + 1]])
                for bb_ in bs_bs:
                    for k_, v_ in bb_.w.items():
                        _mx(best_b.w, k_, v_)
                    for k_, v_ in bb_.r.items():
                        _mx(best_b.r, k_, v_)
                for bb_ in vt_bs:
                    for k_, v_ in bb_.w.items():
                        _mx(vtop_b.w, k_, v_)
                    for k_, v_ in bb_.r.items():
                        _mx(vtop_b.r, k_, v_)
                for bb_ in sr_bs + c2_bs:
                    for k_, v_ in bb_.w.items():
                        _mx(srep_b.w, k_, v_)
                    for k_, v_ in bb_.r.items():
                        _mx(srep_b.r, k_, v_)

            def piece_misc(ts_):
                vt4 = vtop[:].rearrange("p (h q) k -> p h q k", q=2)
                fw.op(POOL, lambda: gps.tensor_tensor(out=etmp[:], in0=best[:], in1=best[:, :, 0:1].to_broadcast([128, 8, 16]),
                                                      op=ALU.subtract), reads=[best_b], writes=[etmp_b])
                fw.op(ACT, lambda: sca.activation(out=etmp[:], in_=etmp[:], func=AF.Exp), reads=[etmp_b], writes=[etmp_b])
                fw.op(DVE, lambda: vec.reduce_sum(out=zz[:], in_=etmp[:], axis=AX.X), reads=[etmp_b], writes=[zz_b])
                fw.op(ACT, lambda: sca.activation(out=zz[:], in_=zz[:], func=AF.Ln), reads=[zz_b], writes=[zz_b])
                fw.op(POOL, lambda: gps.tensor_tensor(out=c1[:], in0=zz[:], in1=best[:, :, 0], op=ALU.add),
                      reads=[zz_b, best_b], writes=[c1_b])
                fw.op(POOL, lambda: gps.tensor_copy(out=v0c[:], in_=vt4[:, :, 0, :]), reads=[vtop_b], writes=[v0c_b])
                fw.op(POOL, lambda: gps.tensor_tensor(out=thr[:], in0=best[:, :, 15:16].to_broadcast([128, 8, 16]), in1=v0c[:],
                                                      op=ALU.subtract), reads=[best_b, v0c_b], writes=[thr_b])
                fw.op(POOL, lambda: gps.tensor_scalar(out=thr[:], in0=thr[:], scalar1=-1e-5, scalar2=None, op0=ALU.add),
                      reads=[thr_b], writes=[thr_b])
                fw.op(POOL, lambda: gps.tensor_tensor(out=bia[:], in0=v0c[:], in1=c1[:].unsqueeze(2).to_broadcast([128, 8, 16]),
                                                      op=ALU.subtract), reads=[c1_b, v0c_b], writes=[bia_b])

            def piece_tr(ts_, which):
                src, src_b = ((v0c, v0c_b), (thr, thr_b), (bia, bia_b))[which]
                dst, dst_b = (v0Ts[ts_], thrTs[ts_], biaTs[ts_])[which]
                srcf = src[:].rearrange("p h r -> p (h r)")
                pt, pt_b = bank(4, 8)
                ptv = pt[:].bitcast(BF16)
                for pc_i in range(3):
                    fw.op(ACT, lambda: sca.copy(out=pcs[:, pc_i, :], in_=(srcf if pc_i == 0 else rres[:])),
                          reads=[src_b, rres_b], writes=[pcs_b])
                    if pc_i < 2:
                        fw.op(DVE, lambda: vec.tensor_tensor(out=rres[:], in0=(srcf if pc_i == 0 else rres[:]), in1=pcs[:, pc_i, :],
                                                             op=ALU.subtract), reads=[src_b, rres_b, pcs_b], writes=[rres_b])
                    fw.op(PE, lambda: tns.transpose(out=ptv[:, pc_i * 128:(pc_i + 1) * 128], in_=pcs[:, pc_i, :], identity=idb[:]),
                          reads=[pcs_b, idb_b], writes=[pt_b])
                fw.op(ACT, lambda: sca.copy(out=dst[:], in_=ptv[:, 0:128]), reads=[pt_b], writes=[dst_b])
                fw.op(DVE, lambda: vec.tensor_tensor(out=dst[:], in0=dst[:], in1=ptv[:, 128:256], op=ALU.add),
                      reads=[pt_b, dst_b], writes=[dst_b])
                fw.op(DVE, lambda: vec.tensor_tensor(out=dst[:], in0=dst[:], in1=ptv[:, 256:384], op=ALU.add),
                      reads=[pt_b, dst_b], writes=[dst_b])
                if which == 2:
                    ebT, ebT_b = ebTs[ts_]
                    fw.op(ACT, lambda: sca.activation(out=ebT[:], in_=dst[:], func=AF.Exp), reads=[dst_b], writes=[ebT_b])

            def front_pieces(bt):
                P = []
                for ts_ in range(TB):
                    P.append(lambda ts_=ts_: piece_load(bt, ts_))
                for g0 in range(0, 16, per):
                    P.append(lambda g0=g0: piece_pq(bt, g0))
                for ts_ in range(TB):
                    for g in range(4):
                        P.append(lambda ts_=ts_, g=g: piece_s(ts_, g))
                    P.append(lambda ts_=ts_: piece_top_a(ts_))
                    P.append(lambda ts_=ts_: piece_top_b(ts_))
                    P.append(lambda ts_=ts_: piece_top_c(ts_))
                    P.append(lambda ts_=ts_: piece_best(ts_))
                    P.append(lambda ts_=ts_: piece_misc(ts_))
                    for which in range(3):
                        P.append(lambda ts_=ts_, which=which: piece_tr(ts_, which))
                return P

            def tok_loop(bt, ts_):
                v0T, v0T_b = v0Ts[ts_]
                thrT, thrT_b = thrTs[ts_]
                biaT, biaT_b = biaTs[ts_]
                itst = {}
                e4s = [[Buf("e4s%d_%d" % (i_, k_)) for k_ in range(4)] for i_ in range(2)]

                def st12(t4):
                    a_t, a_bs = At[t4 % 3]
                    b_t, b_bs = Bt[t4 % 3]
                    e4, e4_b = E4[t4 % 2]
                    p0_, p0_b = bank(0, 8)
                    p1_, p1_b = bank(0, 8)
                    tt0 = ts_ * 128 + t4 * 4
                    qr = []
                    for p_ in range(2):
                        q_t, q_b = Qrep[(t4 % 2) * 2 + p_]
                        src = pqT[:, :, tt0:tt0 + 4].rearrange("c (h q) t -> c q t h", q=2)[:, p_]
                        fw.op(POOL, lambda: gps.tensor_copy(out=q_t[:].rearrange("c t (h r) -> c t h r", r=16),
                                                            in_=src.unsqueeze(3).to_broadcast([128, 4, 8, 16])),
                              reads=[pqT_b], writes=[q_b])
                        qr.append((q_t, q_b))
                    for k in range(4):
                        for p_, (pp, pp_b) in enumerate(((p0_, p0_b), (p1_, p1_b))):
                            q_t, q_b = qr[p_]
                            fw.op(PE, lambda: tns.matmul(pp[:, k * 128:(k + 1) * 128], lhsT=q_t[:, k, :], rhs=ksT[:, p_, :],
                                                         start=True, stop=True), reads=[q_b, ksT_b], writes=[pp_b])
                    tl = t4 * 4
                    for k in range(4):
                        fw.op(ACT, lambda: sca.activation(out=e4[:, k * 128:(k + 1) * 128], in_=p1_[:, k * 128:(k + 1) * 128], func=AF.Exp,
                                                          bias=biaT[:, tl + k:tl + k + 1]), reads=[p1_b, biaT_b],
                              writes=[e4s[t4 % 2][k]] + ([e4_b] if k == 0 else []))
                    fw.op(DVE, lambda: vec.tensor_tensor(out=a_t[:], in0=p0_[:, :].rearrange("p (k n) -> p k n", k=4),
                                                         in1=v0T[:, tl:tl + 4].unsqueeze(2).to_broadcast([128, 4, 128]),
                                                         op=ALU.is_equal), reads=[p0_b, v0T_b], writes=a_bs)
                    for k in range(4):
                        fw.op(DVE, lambda: vec.scalar_tensor_tensor(out=b_t[:, k, :], in0=p1_[:, k * 128:(k + 1) * 128],
                                                                    scalar=thrT[:, tl + k:tl + k + 1], in1=e4[:, k * 128:(k + 1) * 128],
                                                                    op0=ALU.is_ge, op1=ALU.mult),
                              reads=[p1_b, thrT_b, e4_b] + e4s[t4 % 2], writes=[b_bs[k]])

                def st3(t4):
                    a_t, a_bs = At[t4 % 3]
                    b_t, b_bs = Bt[t4 % 3]
                    pg, pg_b = bank(0, 8)
                    for k in range(4):
                        fw.op(PE, lambda: tns.matmul(pg[:, k * 128:(k + 1) * 128], lhsT=b_t[:, k, :], rhs=a_t[:, k, :],
                                                     start=True, stop=True), reads=[a_bs[k], b_bs[k]], writes=[pg_b])
                    itst[t4] = (pg, pg_b)

                def st4(t4):
                    pg, pg_b = itst.pop(t4)
                    tg = ts_ * 128 + t4 * 4
                    fw.op(ACT, lambda: sca.copy(out=Gs[:, :, tg:tg + 4].rearrange("p n t -> p t n"),
                                                in_=pg[:, :].rearrange("p (t n) -> p t n", t=4)), reads=[pg_b], writes=[Gs_b])

                for t4 in range(34):
                    if t4 < 32:
                        st12(t4)
                    if 0 <= t4 - 1 < 32:
                        st3(t4 - 1)
                    if 0 <= t4 - 2 < 32:
                        st4(t4 - 2)

            accs = [banks[j] for j in range(TB * 2)]

            def expert_loop(bt, nxt):
                hT, hT_b = h2Ts[bt % 2]
                every = max(1, 120 // max(1, len(nxt)))

                def issue_load(n):
                    u_t, u_b = utc[n % NB]
                    v_t, v_b = vbc[n % NB]
                    fw.dma(SP, u_t[:], ut_d[n], wbuf=u_b, rbuf=ut_b)
                    fw.dma(SP, v_t[:], vb_d[n], wbuf=v_b, rbuf=vb_b)

                def stage2(n):
                    c_t, c_b = cT[n % 3]
                    v_t, v_b = vbc[n % NB]
                    for ts_ in range(TB):
                        for half in range(2):
                            ac, ac_b = accs[ts_ * 2 + half]
                            fw.op(PE, lambda: tns.matmul(ac[:, :], lhsT=c_t[:, ts_ * 128:(ts_ + 1) * 128],
                                                         rhs=v_t[:, half * 512:(half + 1) * 512], start=(n == 0), stop=(n == 127)),
                                  reads=[c_b, v_b], writes=[ac_b])

                for n1 in range(NB - 1):
                    issue_load(n1)
                for n1 in range(128):
                    u_t, u_b = utc[n1 % NB]
                    pa, pa_b = bank(4, 8)
                    for kc in range(8):
                        fw.op(PE, lambda: tns.matmul(pa[:, 0:T], lhsT=u_t[:, kc, :], rhs=hT[:, kc, :], start=(kc == 0), stop=(kc == 7)),
                              reads=[u_b, hT_b], writes=[pa_b])
                    g_t, g_b = ag[n1 % 3]
                    c_t, c_b = cT[n1 % 3]
                    fw.op(ACT, lambda: sca.activation(out=g_t[:], in_=pa[:, 0:T], func=AF.Gelu), reads=[pa_b], writes=[g_b])
                    eng, ee = POOL, gps
                    fw.op(eng, lambda: ee.tensor_tensor(out=c_t[:], in0=g_t[:], in1=Gs[:, n1, :], op=ALU.mult),
                          reads=[g_b, Gs_b], writes=[c_b])
                    if n1 >= 1:
                        stage2(n1 - 1)
                    if n1 + NB - 1 < 128:
                        issue_load(n1 + NB - 1)
                    if nxt and n1 >= 2 and n1 % every == 0:
                        nxt.pop(0)()
                stage2(127)
                while nxt:
                    nxt.pop(0)()

            def tail(bt):
                for ts_ in range(TB):
                    t_, t_b = LT
                    row0 = (bt * TB + ts_) * 128
                    fw.dma(SP, t_[:], h2_d[row0:row0 + 128, :], wbuf=t_b, rbuf=h2_b)
                    for half in range(2):
                        ac, ac_b = accs[ts_ * 2 + half]
                        fw.op(DVE, lambda: vec.scalar_tensor_tensor(out=t_[:, half * 512:(half + 1) * 512],
                                                                    in0=t_[:, half * 512:(half + 1) * 512], scalar=ALPHA,
                                                                    in1=ac[:, :], op0=ALU.mult, op1=ALU.add),
                              reads=[t_b, ac_b], writes=[t_b])
                    layer_norm("ln3", t_, t_b, 6, lnscr)
                    fw.dma(SP, y_d[row0:row0 + 128, :], t_[:], rbuf=t_b, sembuf=t_b)

            for p_ in front_pieces(0):
                p_()
            for bt in range(NTB):
                mark("B_tok", bt == 1)
                for ts_ in range(TB):
                    tok_loop(bt, ts_)
                mark("B_exp", bt == 1)
                expert_loop(bt, front_pieces(bt + 1) if bt + 1 < NTB else [])
                mark("B_ln3", bt == 1)
                tail(bt)
                mark(None)
            fw.barrier()
        print("n_instr", fw.n_instr, {e.name: e.cnt for e in fw.engs})
    return nc


def prep_inputs(inputs, S):
    f = lambda a: np.ascontiguousarray(np.asarray(a, dtype=np.float32))
    x = f(inputs["x"])
    mem = f(inputs["mem"])
    vecs = np.stack([f(inputs["ln_in_g"]), f(inputs["ln_in_b"]), f(inputs["ln1_g"])[0], f(inputs["ln1_b"])[0],
                     f(inputs["ln2_g"])[0], f(inputs["ln2_b"])[0], f(inputs["ln3_g"])[0], f(inputs["ln3_b"])[0]], axis=0)
    conv_w = f(inputs["conv_w"])[0]
    cw = np.ascontiguousarray(conv_w.reshape(4, 8, 128).transpose(2, 1, 0))
    cb = np.ascontiguousarray(f(inputs["conv_b"])[0].reshape(8, 128).T)
    gb = np.concatenate([f(inputs["mlstm_i_bias"])[0], f(inputs["mlstm_f_bias"])[0]])[None, :]
    rel_bias = f(inputs["rel_bias"])[0]
    s = np.arange(128)[:, None]
    t = np.arange(128)[None, :]
    tabs = []
    for r in range(2):
        idx = np.clip(s - t - 128 * r, -128, 128) + 128
        tabs.append(rel_bias[:, idx])
    rbT = np.ascontiguousarray(np.stack(tabs, axis=0).transpose(2, 1, 0, 3))
    rb0 = np.ascontiguousarray(rel_bias[:, 0][None, :])
    common = {
        "vecs": np.ascontiguousarray(vecs), "w_in": f(inputs["w_in"])[0], "cw": cw, "cb": cb, "gbias": np.ascontiguousarray(gb),
        "norm_g": f(inputs["mlstm_norm_g"]), "rbT": rbT, "rb0": rb0, "w_out": f(inputs["w_out"])[0],
        "w_q": f(inputs["xattn_w_q"])[0], "w_kv": f(inputs["xattn_w_kv"])[0], "w_o": f(inputs["xattn_w_o"])[0],
        "w_pq": f(inputs["peer_w_query"])[0], "sub_keys": f(inputs["peer_sub_keys"])[0],
        "peer_u": f(inputs["peer_u"])[0], "peer_v": f(inputs["peer_v"])[0],
    }
    maps = []
    for c in range(x.shape[0]):
        m = dict(common)
        m["x"] = np.ascontiguousarray(x[c, :S])
        m["mem"] = np.ascontiguousarray(mem[c])
        maps.append(m)
    return maps


def kernel(**inputs):
    S = int(np.asarray(inputs["x"]).shape[1])
    nb = int(np.asarray(inputs["x"]).shape[0])
    nc = build(S)
    maps = prep_inputs(inputs, S)
    res = run_bass_kernel_spmd(nc, maps, core_ids=list(range(nb)))
    out = np.stack([np.asarray(res.results[c]["y"], dtype=np.float32) for c in range(nb)], axis=0)
    return out
```

```python
import math
from contextlib import ExitStack
import numpy as np
import concourse.bass as bass
import concourse.mybir as mybir
from concourse.bass_utils import run_bass_kernel_spmd

F32 = mybir.dt.float32
BF16 = mybir.dt.bfloat16
ALU = mybir.AluOpType
AF = mybir.ActivationFunctionType
AX = mybir.AxisListType

D = 1024
SEQ = 8192
NCORES = 8
LN_EPS = 1e-5
ALPHA = 2.0 ** 0.25
NEG = -30000.0
INW = 3592


class Buf:
    __slots__ = ("name", "w", "r", "dw", "dr", "dsem", "dcnt")

    def __init__(self, name):
        self.name = name
        self.w = {}
        self.r = {}
        self.dw = {}
        self.dr = {}
        self.dsem = None
        self.dcnt = 0


class Eng:
    def __init__(self, name, eng, sem, same_sync):
        self.name = name
        self.eng = eng
        self.sem = sem
        self.cnt = 0
        self.seen = {}
        self.seen_dma = {}
        self.same_sync = same_sync


def _mx(d, k, v):
    if d.get(k, 0) < v:
        d[k] = v


class FW:
    def __init__(self, nc, st):
        self.nc = nc
        self.st = st
        mk = lambda n: st.enter_context(nc.semaphore(n))
        self.pe = Eng("pe", nc.tensor, mk("s_pe"), False)
        self.act = Eng("act", nc.scalar, mk("s_act"), True)
        self.dve = Eng("dve", nc.vector, mk("s_dve"), True)
        self.pool = Eng("pool", nc.gpsimd, mk("s_pool"), True)
        self.sp = Eng("sp", nc.sync, mk("s_sp"), False)
        self.engs = [self.pe, self.act, self.dve, self.pool, self.sp]
        self.n_instr = 0
        self.dma_bufs = []

    def _dsem(self, b):
        if b.dsem is None:
            b.dsem = self.st.enter_context(self.nc.semaphore("d_" + b.name))
            self.dma_bufs.append(b)
        return b.dsem

    def _need(self, deps, ddeps, b, is_write):
        for k, v in b.w.items():
            _mx(deps, k, v)
        for k, v in b.dw.items():
            _mx(ddeps, k, v)
        if is_write:
            for k, v in b.r.items():
                _mx(deps, k, v)
            for k, v in b.dr.items():
                _mx(ddeps, k, v)

    def _waits(self, e, deps, ddeps, is_dma=False):
        for oe, c in deps.items():
            if oe is e and not e.same_sync:
                continue
            if e.seen.get(oe, 0) >= c:
                continue
            e.eng.wait_ge(oe.sem, c)
            e.seen[oe] = c
        for sb_, c in ddeps.items():
            if e.seen_dma.get(sb_, 0) >= c:
                continue
            e.eng.wait_ge(sb_.dsem, c)
            e.seen_dma[sb_] = c

    def op(self, e, fn, reads=(), writes=()):
        deps, ddeps = {}, {}
        for b in reads:
            self._need(deps, ddeps, b, False)
        for b in writes:
            self._need(deps, ddeps, b, True)
        self._waits(e, deps, ddeps)
        ins = fn()
        e.cnt += 1
        ins.then_inc(e.sem, 1)
        self.n_instr += 1
        for b in reads:
            b.r[e] = e.cnt
        for b in writes:
            b.w[e] = e.cnt
        return ins

    def dma(self, q, out_ap, in_ap, wbuf=None, rbuf=None, sembuf=None, nowait_prev=False, **kw):
        sb_ = sembuf or wbuf or rbuf
        self._dsem(sb_)
        deps, ddeps = {}, {}
        if rbuf is not None:
            self._need(deps, ddeps, rbuf, False)
        if wbuf is not None:
            self._need(deps, ddeps, wbuf, True)
            if nowait_prev:
                for k in list(ddeps.keys()):
                    if k in wbuf.dw and ddeps[k] == wbuf.dw[k] and (rbuf is None or k not in rbuf.dw):
                        del ddeps[k]
        self._waits(q, deps, ddeps)
        ins = q.eng.dma_start(out=out_ap, in_=in_ap, **kw)
        sb_.dcnt += 16
        ins.then_inc(sb_.dsem, 16)
        self.n_instr += 1
        if wbuf is not None:
            wbuf.dw[sb_] = sb_.dcnt
        if rbuf is not None:
            rbuf.dr[sb_] = sb_.dcnt
        return ins

    def barrier(self):
        for e in self.engs:
            for oe in self.engs:
                if oe is e or oe.cnt == 0:
                    continue
                if e.seen.get(oe, 0) >= oe.cnt:
                    continue
                e.eng.wait_ge(oe.sem, oe.cnt)
                e.seen[oe] = oe.cnt
            for b in self.dma_bufs:
                if b.dcnt and e.seen_dma.get(b, 0) < b.dcnt:
                    e.eng.wait_ge(b.dsem, b.dcnt)
                    e.seen_dma[b] = b.dcnt


def build(S, debug=False, stop=None, tap=None, prof=False):
    NT = S // 128
    nc = bass.Bass("TRN2", target_bir_lowering=False)

    def din(name, shape):
        return nc.dram_tensor(name, list(shape), F32, kind="ExternalInput").ap()

    x_d = din("x", [S, D])
    mem_d = din("mem", [256, D])
    vec_d = din("vecs", [8, D])
    w_in_d = din("w_in", [D, INW])
    cw_d = din("cw", [128, 8, 4])
    cb_d = din("cb", [128, 8])
    gb_d = din("gbias", [1, 8])
    ng_d = din("norm_g", [1, 512])
    rbT_d = din("rbT", [128, 8, 2, 128])
    rb0_d = din("rb0", [1, 8])
    w_out_d = din("w_out", [D, D])
    w_q_d = din("w_q", [D, D])
    w_kv_d = din("w_kv", [D, 2 * D])
    w_o_d = din("w_o", [D, D])
    w_pq_d = din("w_pq", [D, 2 * D])
    sk_d = din("sub_keys", [2, 128, 128])
    pu_d = din("peer_u", [16384, D])
    pv_d = din("peer_v", [16384, D])
    y_d = nc.dram_tensor("y", [S, D], F32, kind="ExternalOutput").ap()
    if debug:
        h2_d = nc.dram_tensor("h2dbg", [S, D], F32, kind="ExternalOutput").ap()
    else:
        h2_d = nc.dram_tensor("h2s", [S, D], F32, kind="Internal").ap()
    ut_d = nc.dram_tensor("ut_s", [128, 128, 8, 128], BF16, kind="Internal").ap()
    vb_d = nc.dram_tensor("vb_s", [128, 128, D], BF16, kind="Internal").ap()
    h2_b = Buf("h2s")
    ut_b = Buf("ut_s")
    vb_b = Buf("vb_s")

    with ExitStack() as st0:
        fw = FW(nc, st0)
        PE, ACT, DVE, POOL, SP = fw.pe, fw.act, fw.dve, fw.pool, fw.sp
        tns, vec, sca, gps = nc.tensor, nc.vector, nc.scalar, nc.gpsimd

        from contextlib import nullcontext

        cur_scope = [None]

        def mark(name, cond=True):
            if not prof:
                return
            if cur_scope[0] is not None:
                nc.leave_named_scope(cur_scope[0][0], cur_scope[0][1], False)
                cur_scope[0] = None
            if name is not None and cond:
                sid, _ = nc.enter_named_scope(name, False)
                cur_scope[0] = (name, sid)

        def sb(st, name, shape, dt=F32):
            t = st.enter_context(nc.sbuf_tensor("sb_" + name, list(shape), dt))
            return t, Buf(name)

        banks = []
        for i in range(8):
            t = st0.enter_context(nc.psum_tensor("pb%d" % i, [128, 512], F32))
            banks.append((t, Buf("pb%d" % i)))
        bank_rr = [0]

        def bank(lo=0, hi=8):
            i = bank_rr[0]
            if i < lo or i >= hi:
                i = lo
            bank_rr[0] = i + 1 if i + 1 < hi else lo
            return banks[i]

        idf, idf_b = sb(st0, "idf", [128, 128])
        idb, idb_b = sb(st0, "idb", [128, 128], BF16)
        fw.op(POOL, lambda: gps.memset(idf[:], 0.0), writes=[idf_b])
        fw.op(POOL, lambda: gps.affine_select(out=idf[:], in_=idf[:], pattern=[[-1, 128]], compare_op=ALU.not_equal,
                                              fill=1.0, base=0, channel_multiplier=1), reads=[idf_b], writes=[idf_b])
        fw.op(DVE, lambda: vec.tensor_copy(out=idb[:], in_=idf[:]), reads=[idf_b], writes=[idb_b])
        lnv_box = [None, None, 0]
        epsc, epsc_b = sb(st0, "epsc", [128, 1])
        fw.op(POOL, lambda: gps.memset(epsc[:], LN_EPS), writes=[epsc_b])
        stg = [sb(st0, "stg%d" % i, [128, 1024]) for i in range(2)]
        stg_rr = [0]
        cast_rr = [0]

        def cast_copy(out_ap, in_ap, rb, wb):
            k = cast_rr[0] % 3
            cast_rr[0] += 1
            if k == 0:
                fw.op(DVE, lambda: vec.tensor_copy(out=out_ap, in_=in_ap), reads=[rb], writes=[wb])
            elif k == 1:
                fw.op(ACT, lambda: sca.copy(out=out_ap, in_=in_ap), reads=[rb], writes=[wb])
            else:
                fw.op(POOL, lambda: gps.tensor_copy(out=out_ap, in_=in_ap), reads=[rb], writes=[wb])

        def load_weight_bf(w_d, ncols, dst, dst_b):
            for kc in range(8):
                for c0 in range(0, ncols, 1024):
                    cw = min(1024, ncols - c0)
                    s_t, s_b = stg[stg_rr[0] % 2]
                    stg_rr[0] += 1
                    fw.dma(SP, s_t[:, 0:cw], w_d[kc * 128:(kc + 1) * 128, c0:c0 + cw], wbuf=s_b)
                    cast_copy(dst[:, kc, c0:c0 + cw], s_t[:, 0:cw], s_b, dst_b)

        def layer_norm(st_name, xt, xt_b, gi, scr):
            st6, st6_b, mv, mv_b, rs, rs_b = scr
            lnv, lnv_b, base = lnv_box
            gi = gi - base
            for hh in range(2):
                fw.op(DVE, lambda: vec.bn_stats(out=st6[:, hh, :], in_=xt[:, hh * 512:(hh + 1) * 512]),
                      reads=[xt_b], writes=[st6_b])
            fw.op(DVE, lambda: vec.bn_aggr(out=mv[:], in_=st6[:].rearrange("p a b -> p (a b)")), reads=[st6_b], writes=[mv_b])
            fw.op(ACT, lambda: sca.activation(out=rs[:], in_=mv[:, 1:2], func=AF.Sqrt, bias=epsc[:, 0:1]), reads=[mv_b, epsc_b], writes=[rs_b])
            fw.op(DVE, lambda: vec.reciprocal(out=rs[:], in_=rs[:]), reads=[rs_b], writes=[rs_b])
            fw.op(DVE, lambda: vec.scalar_tensor_tensor(out=xt[:], in0=xt[:], scalar=mv[:, 0:1], in1=lnv[:, gi, :],
                                                        op0=ALU.subtract, op1=ALU.mult), reads=[xt_b, mv_b, lnv_b], writes=[xt_b])
            fw.op(DVE, lambda: vec.scalar_tensor_tensor(out=xt[:], in0=xt[:], scalar=rs[:, 0:1], in1=lnv[:, gi + 1, :],
                                                        op0=ALU.mult, op1=ALU.add), reads=[xt_b, rs_b, lnv_b], writes=[xt_b])

        def to_featmajor(src, src_b, srcbf, srcbf_b, dstT, dstT_b, off=0, width=128, src_is_bf=False):
            if not src_is_bf:
                fw.op(ACT, lambda: sca.copy(out=srcbf[:], in_=src[:]), reads=[src_b], writes=[srcbf_b])
            else:
                srcbf, srcbf_b = src, src_b
            pt, pt_b = bank()
            ptv = pt[:].bitcast(BF16)
            for kc in range(8):
                fw.op(PE, lambda: tns.transpose(out=ptv[:, kc * 128:(kc + 1) * 128], in_=srcbf[:, kc * 128:(kc + 1) * 128],
                                                identity=idb[:]), reads=[srcbf_b, idb_b], writes=[pt_b])
            fw.op(DVE, lambda: vec.tensor_copy(out=dstT[:, :, off:off + 128],
                                               in_=ptv.rearrange("p (k t) -> p k t", k=8)), reads=[pt_b], writes=[dstT_b])

        with ExitStack() as stA:
            kxT, kxT_b = sb(stA, "kxT", [128, 8, 256], BF16)
            vx, vx_b = sb(stA, "vx", [128, 2, 4, 257], BF16)
            rbT, rbT_b = sb(stA, "rbT", [128, 8, 2, 128])
            rb0, rb0_b = sb(stA, "rb0", [128, 8])
            cw, cw_b = sb(stA, "cw", [128, 8, 4])
            cb, cb_b = sb(stA, "cb", [128, 8])
            gbias, gbias_b = sb(stA, "gbias", [128, 8])
            ng, ng_b = sb(stA, "ng", [128, 512])
            triu, triu_b = sb(stA, "triu", [128, 128])
            m01, m01_b = sb(stA, "m01", [128, 128])
            fw.dma(SP, rbT[:], rbT_d[:, :, :, :], wbuf=rbT_b)
            fw.dma(SP, rb0[:], rb0_d[0:1, :].partition_broadcast(128), wbuf=rb0_b)
            fw.dma(SP, cw[:], cw_d[:, :, :], wbuf=cw_b)
            fw.dma(SP, cb[:], cb_d[:, :], wbuf=cb_b)
            fw.dma(SP, gbias[:], gb_d[0:1, :].partition_broadcast(128), wbuf=gbias_b)
            fw.dma(SP, ng[:], ng_d[0:1, :].partition_broadcast(128), wbuf=ng_b)
            fw.op(POOL, lambda: gps.memset(triu[:], 1.0), writes=[triu_b])
            fw.op(POOL, lambda: gps.affine_select(out=triu[:], in_=triu[:], pattern=[[1, 128]], compare_op=ALU.is_ge,
                                                  fill=0.0, base=0, channel_multiplier=-1), reads=[triu_b], writes=[triu_b])
            fw.op(POOL, lambda: gps.tensor_copy(out=m01[:], in_=triu[:]), reads=[triu_b], writes=[m01_b])
            fw.op(POOL, lambda: gps.memset(rbT[64:128, :, 0, 0:64], NEG), reads=[rbT_b], writes=[rbT_b])

            with ExitStack() as stM:
                wkv, wkv_b = sb(stM, "wkv", [128, 8, 2 * D], BF16)
                memT, memT_b = sb(stM, "memT", [128, 8, 256], BF16)
                mt, mt_b = sb(stM, "mt", [128, D])
                mtb, mtb_b = sb(stM, "mtb", [128, D], BF16)
                load_weight_bf(w_kv_d, 2 * D, wkv, wkv_b)
                for mc in range(2):
                    fw.dma(SP, mt[:], mem_d[mc * 128:(mc + 1) * 128, :], wbuf=mt_b)
                    to_featmajor(mt, mt_b, mtb, mtb_b, memT, memT_b, off=mc * 128)
                for c in range(8):
                    pt, pt_b = bank()
                    for kc in range(8):
                        fw.op(PE, lambda: tns.matmul(pt[:, 0:256], lhsT=wkv[:, kc, c * 128:(c + 1) * 128], rhs=memT[:, kc, :],
                                                     start=(kc == 0), stop=(kc == 7)), reads=[wkv_b, memT_b], writes=[pt_b])
                    fw.op(ACT, lambda: sca.activation(out=kxT[:, c, :], in_=pt[:, 0:256], func=AF.Copy, scale=1.0 / 16.0),
                          reads=[pt_b], writes=[kxT_b])
                for mc in range(2):
                    for hf in range(2):
                        pt, pt_b = bank()
                        for kc in range(8):
                            fw.op(PE, lambda: tns.matmul(pt[:, :], lhsT=memT[:, kc, mc * 128:(mc + 1) * 128],
                                                         rhs=wkv[:, kc, D + hf * 512:D + (hf + 1) * 512],
                                                         start=(kc == 0), stop=(kc == 7)), reads=[wkv_b, memT_b], writes=[pt_b])
                        fw.op(DVE, lambda: vec.tensor_copy(out=vx[:, mc, 2 * hf:2 * hf + 2, 0:256],
                                                           in_=pt[:, :].rearrange("p (h d) -> p h d", h=2)),
                              reads=[pt_b], writes=[vx_b])
                fw.op(DVE, lambda: vec.memset(vx[:, :, :, 256:257], 1.0), writes=[vx_b])

                ub, ub_b = sb(stM, "ub", [128, D], BF16)
                uts = [sb(stM, "uts%d" % i, [128, 8, 128], BF16) for i in range(2)]
                vbs = [sb(stM, "vbs%d" % i, [128, D], BF16) for i in range(2)]
                for n1 in range(128):
                    s_t, s_b = stg[stg_rr[0] % 2]
                    stg_rr[0] += 1
                    fw.dma(SP, s_t[:, 0:D], pu_d[n1 * 128:(n1 + 1) * 128, :], wbuf=s_b)
                    cast_copy(ub[:], s_t[:, 0:D], s_b, ub_b)
                    u_t, u_b = uts[n1 % 2]
                    to_featmajor(ub, ub_b, None, None, u_t, u_b, off=0, src_is_bf=True)
                    fw.dma(SP, ut_d[n1], u_t[:], wbuf=ut_b, rbuf=u_b, sembuf=u_b)
                    s_t, s_b = stg[stg_rr[0] % 2]
                    stg_rr[0] += 1
                    fw.dma(SP, s_t[:, 0:D], pv_d[n1 * 128:(n1 + 1) * 128, :], wbuf=s_b)
                    v_t, v_b = vbs[n1 % 2]
                    cast_copy(v_t[:], s_t[:, 0:D], s_b, v_b)
                    fw.dma(SP, vb_d[n1], v_t[:], wbuf=vb_b, rbuf=v_b, sembuf=v_b)
                fw.barrier()
            if stop == "setup":
                return nc

            win, win_b = sb(stA, "win", [128, 8, INW], BF16)
            wout, wout_b = sb(stA, "wout", [128, 8, D], BF16)
            wq, wq_b = sb(stA, "wq", [128, 8, D], BF16)
            wo, wo_b = sb(stA, "wo", [128, 8, D], BF16)
            load_weight_bf(w_in_d, INW, win, win_b)
            load_weight_bf(w_out_d, D, wout, wout_b)
            load_weight_bf(w_q_d, D, wq, wq_b)
            load_weight_bf(w_o_d, D, wo, wo_b)
            lnvA, lnvA_b = sb(stA, "lnvA", [128, 6, D])
            for i_ in range(6):
                fw.dma(SP, lnvA[:, i_, :], vec_d[i_:i_ + 1, :].partition_broadcast(128), wbuf=lnvA_b, nowait_prev=(i_ > 0))
            lnv_box[0], lnv_box[1], lnv_box[2] = lnvA, lnvA_b, 0
            hbs = [sb(stA, "hb%d" % i, [128, D]) for i in range(1)]
            hbf, hbf_b = sb(stA, "hbf", [128, D], BF16)
            hT, hT_b = sb(stA, "hT", [128, 8, 128], BF16)
            st6, st6_b = sb(stA, "st6", [128, 2, 6])
            mv, mv_b = sb(stA, "mv", [128, 2])
            rs, rs_b = sb(stA, "rs", [128, 1])
            lnscr = (st6, st6_b, mv, mv_b, rs, rs_b)
            qT, qT_b = sb(stA, "qT", [128, 4, 128], BF16)
            kring, kring_b = sb(stA, "kring", [128, 4, 5, 128], BF16)
            vring, vring_b = sb(stA, "vring", [128, 5, 8, 65], BF16)
            pTs = [sb(stA, "pT%d" % i, [128, 5, 128], BF16) for i in range(2)]
            stmp, stmp_b = sb(stA, "stmp", [128, 256])
            cat, cat_b = sb(stA, "cat", [128, D], BF16)
            rcp, rcp_b = sb(stA, "rcp", [128, 8])
            pc, pc_b = sb(stA, "pc", [128, 8, 131])
            cv, cv_b = sb(stA, "cv", [128, 4, 128])
            qkT, qkT_b = sb(stA, "qkT", [128, 8, 128], BF16)
            vaug, vaug_b = sb(stA, "vaug", [128, 4, 129], BF16)
            og, og_b = sb(stA, "og", [128, 512])
            gif, gif_b = sb(stA, "gif", [128, 8])
            lgf, lgf_b = sb(stA, "lgf", [128, 4])
            dbias, dbias_b = sb(stA, "dbias", [128, 4])
            DT, DT_b = sb(stA, "DT", [128, 4, 128])
            Eb, Eb_b = sb(stA, "Eb", [128, 4, 128])
            qpT, qpT_b = sb(stA, "qpT", [128, 4, 128], BF16)
            WT, WT_b = sb(stA, "WT", [128, 4, 128], BF16)
            Cst, Cst_b = sb(stA, "Cst", [128, 4, 129])
            Cbf, Cbf_b = sb(stA, "Cbf", [128, 4, 129], BF16)
            ksc, ksc_b = sb(stA, "ksc", [128, 4, 128], BF16)
            hm, hm_b = sb(stA, "hm", [128, 4, 128])
            rden, rden_b = sb(stA, "rden", [128, 4])
            hst, hst_b = sb(stA, "hst", [128, 4, 6])
            hmv, hmv_b = sb(stA, "hmv", [128, 4, 2])
            hrs, hrs_b = sb(stA, "hrs", [128, 4])
            pxT, pxT_b = qkT, qkT_b
            ob, ob_b = cat, cat_b
            rdx, rdx_b = sb(stA, "rdx", [128, 4])

            fw.op(POOL, lambda: gps.memset(pc[:], 0.0), writes=[pc_b])
            Cst_hb = [Buf("Cst_h%d" % k) for k in range(4)]
            hst_hb = [Buf("hst_h%d" % k) for k in range(4)]
            hmv_hb = [Buf("hmv_h%d" % k) for k in range(4)]
            fw.op(POOL, lambda: gps.memset(Cst[:], 0.0), writes=[Cst_b] + Cst_hb)
            fw.op(POOL, lambda: gps.memset(Cbf[:], 0.0), writes=[Cbf_b])
            fw.op(POOL, lambda: gps.memset(vring[:, :, :, 64:65], 1.0), writes=[vring_b])
            fw.op(POOL, lambda: gps.memset(vaug[:, :, 128:129], 1.0), writes=[vaug_b])
            for i in range(2):
                fw.op(POOL, lambda: gps.memset(pTs[i][0][:], 0.0), writes=[pTs[i][1]])

            def load_x(i):
                fw.dma(SP, hbs[0][0][:], x_d[i * 128:(i + 1) * 128, :], wbuf=hbs[0][1])

            for i in range(NT):
                load_x(i)
                hb, hb_b = hbs[0]
                slot = i % 5
                mark("A_lnin", i == 2)
                layer_norm("lnin", hb, hb_b, 0, lnscr)
                if tap == "h0":
                    fw.dma(SP, h2_d[i * 128:(i + 1) * 128, :], hb[:], wbuf=h2_b, rbuf=hb_b, sembuf=hb_b)
                    continue
                to_featmajor(hb, hb_b, hbf, hbf_b, hT, hT_b)
                mark("A_inproj", i == 2)
                for grp in range(4):
                    pt, pt_b = bank()
                    for bi in range(4):
                        blk = grp * 4 + bi
                        col0 = blk * 128 if blk < 8 else 1536 + (blk - 8) * 128
                        for kc in range(8):
                            fw.op(PE, lambda: tns.matmul(pt[:, bi * 128:(bi + 1) * 128], lhsT=win[:, kc, col0:col0 + 128],
                                                         rhs=hT[:, kc, :], start=(kc == 0), stop=(kc == 7)),
                                  reads=[win_b, hT_b], writes=[pt_b])
                    pv3 = pt[:, :].rearrange("p (b t) -> p b t", b=4)
                    if grp == 0:
                        fw.op(ACT, lambda: sca.activation(out=qT[:], in_=pv3, func=AF.Copy, scale=0.125),
                              reads=[pt_b], writes=[qT_b])
                    elif grp == 1:
                        fw.op(ACT, lambda: sca.copy(out=kring[:, :, slot, :], in_=pv3), reads=[pt_b], writes=[kring_b])
                    else:
                        b0 = (grp - 2) * 4
                        fw.op(ACT, lambda: sca.copy(out=pc[:, b0:b0 + 4, 3:131], in_=pv3), reads=[pt_b], writes=[pc_b])
                for j, col0 in enumerate((1024, 2560, 3072)):
                    pt, pt_b = bank()
                    for kc in range(8):
                        fw.op(PE, lambda: tns.matmul(pt[:, :], lhsT=hT[:, kc, :], rhs=win[:, kc, col0:col0 + 512],
                                                     start=(kc == 0), stop=(kc == 7)), reads=[win_b, hT_b], writes=[pt_b])
                    if j == 0:
                        fw.op(DVE, lambda: vec.tensor_copy(out=vring[:, slot, :, 0:64],
                                                           in_=pt[:, :].rearrange("p (h d) -> p h d", h=8)),
                              reads=[pt_b], writes=[vring_b])
                    elif j == 1:
                        fw.op(DVE, lambda: vec.tensor_copy(out=vaug[:, :, 0:128],
                                                           in_=pt[:, :].rearrange("p (h d) -> p h d", h=4)),
                              reads=[pt_b], writes=[vaug_b])
                    else:
                        fw.op(ACT, lambda: sca.activation(out=og[:], in_=pt[:, :], func=AF.Sigmoid), reads=[pt_b], writes=[og_b])
                pt, pt_b = bank()
                for kc in range(8):
                    fw.op(PE, lambda: tns.matmul(pt[:, 0:8], lhsT=hT[:, kc, :], rhs=win[:, kc, 3584:3592],
                                                 start=(kc == 0), stop=(kc == 7)), reads=[win_b, hT_b], writes=[pt_b])
                fw.op(DVE, lambda: vec.tensor_tensor(out=gif[:], in0=pt[:, 0:8], in1=gbias[:], op=ALU.add),
                      reads=[pt_b, gbias_b], writes=[gif_b])

                mark("A_attn", i == 2)
                nr = min(i, 4) + 1
                pacc = [banks[0], banks[1]]
                for h in range(8):
                    pr, hh = h // 2, h % 2
                    p0 = hh * 64
                    pA, pA_b = bank(2, 8)
                    pB, pB_b = bank(2, 8)
                    pT, pT_b = pTs[h % 2]
                    for r in range(nr):
                        sl = (i - r) % 5
                        dst = pA[:, r * 128:(r + 1) * 128] if r < 4 else pB[:, 0:128]
                        fw.op(PE, lambda: tns.matmul(dst, lhsT=kring[p0:p0 + 64, pr, sl, :], rhs=qT[p0:p0 + 64, pr, :],
                                                     start=True, stop=True), reads=[kring_b, qT_b],
                              writes=[pA_b if r < 4 else pB_b])
                    n01 = min(nr, 2)
                    fw.op(DVE, lambda: vec.tensor_tensor(out=stmp[:, 0:n01 * 128], in0=pA[:, 0:n01 * 128],
                                                         in1=rbT[:, h, 0:n01, :].rearrange("p r t -> p (r t)"), op=ALU.add),
                          reads=[pA_b, rbT_b], writes=[stmp_b])
                    fw.op(ACT, lambda: sca.activation(out=pT[:, 0:n01, :].rearrange("p r t -> p (r t)"), in_=stmp[:, 0:n01 * 128],
                                                      func=AF.Exp), reads=[stmp_b], writes=[pT_b])
                    if nr > 2:
                        n23 = min(nr, 4) - 2
                        fw.op(ACT, lambda: sca.activation(out=pT[:, 2:2 + n23, :].rearrange("p r t -> p (r t)"),
                                                          in_=pA[:, 256:256 + n23 * 128], func=AF.Exp, bias=rb0[:, h:h + 1]),
                              reads=[pA_b, rb0_b], writes=[pT_b])
                    if nr > 4:
                        fw.op(ACT, lambda: sca.activation(out=pT[64:128, 4, :], in_=pB[64:128, 0:128], func=AF.Exp,
                                                          bias=rb0[64:128, h:h + 1]), reads=[pB_b, rb0_b], writes=[pT_b])
                        fw.op(ACT, lambda: sca.activation(out=pT[0:64, 4, 0:64], in_=pB[0:64, 0:64], func=AF.Exp,
                                                          bias=rb0[0:64, h:h + 1]), reads=[pB_b, rb0_b], writes=[pT_b])
                    pc_t, pc_tb = pacc[h // 4]
                    o0 = (h % 4) * 65
                    for r in range(nr):
                        sl = (i - r) % 5
                        fw.op(PE, lambda: tns.matmul(pc_t[:, o0:o0 + 65], lhsT=pT[:, r, :], rhs=vring[:, sl, h, :],
                                                     start=(r == 0), stop=(r == nr - 1)), reads=[pT_b, vring_b], writes=[pc_tb])
                for half in range(2):
                    pc_t, pc_tb = pacc[half]
                    v3 = pc_t[:, 0:260].rearrange("p (h d) -> p h d", h=4)
                    fw.op(DVE, lambda: vec.reciprocal(out=rcp[:, half * 4:half * 4 + 4], in_=v3[:, :, 64]),
                          reads=[pc_tb], writes=[rcp_b])
                    fw.op(DVE, lambda: vec.tensor_tensor(
                        out=cat[:, half * 256:(half + 1) * 256].rearrange("p (h d) -> p h d", h=4), in0=v3[:, :, 0:64],
                        in1=rcp[:, half * 4:half * 4 + 4].unsqueeze(2).to_broadcast([128, 4, 64]), op=ALU.mult),
                        reads=[pc_tb, rcp_b], writes=[cat_b])

                mark("A_mlstm", i == 2)
                for cg in range(2):
                    cvb = [Buf("cv%d" % k) for k in range(4)]
                    for b4 in range(4):
                        blk = cg * 4 + b4
                        fw.op(POOL, lambda: gps.tensor_scalar(out=cv[:, b4, :], in0=pc[:, blk, 0:128], scalar1=cw[:, blk, 0:1],
                                                              scalar2=cb[:, blk:blk + 1], op0=ALU.mult, op1=ALU.add),
                              reads=[pc_b, cw_b, cb_b], writes=[cvb[b4]] + ([cv_b] if b4 == 0 else []))
                    for j in range(1, 4):
                        for b4 in range(4):
                            blk = cg * 4 + b4
                            fw.op(DVE, lambda: vec.scalar_tensor_tensor(out=cv[:, b4, :], in0=pc[:, blk, j:j + 128],
                                                                        scalar=cw[:, blk, j:j + 1], in1=cv[:, b4, :],
                                                                        op0=ALU.mult, op1=ALU.add),
                                  reads=[pc_b, cw_b, cvb[b4]], writes=[cvb[b4]])
                    for bb_ in cvb:
                        for k_, v_ in bb_.w.items():
                            _mx(cv_b.w, k_, v_)
                        for k_, v_ in bb_.r.items():
                            _mx(cv_b.r, k_, v_)
                    fw.op(ACT, lambda: sca.activation(out=qkT[:, cg * 4:cg * 4 + 4, :], in_=cv[:], func=AF.Silu),
                          reads=[cv_b], writes=[qkT_b])
                fw.op(POOL, lambda: gps.tensor_copy(out=pc[:, :, 0:3], in_=pc[:, :, 128:131]), reads=[pc_b], writes=[pc_b])
                fw.op(ACT, lambda: sca.activation(out=lgf[:], in_=gif[:, 4:8], func=AF.Exp, scale=-1.0), reads=[gif_b], writes=[lgf_b])
                fw.op(ACT, lambda: sca.activation(out=lgf[:], in_=lgf[:], func=AF.Ln, bias=1.0), reads=[lgf_b], writes=[lgf_b])
                fw.op(DVE, lambda: vec.tensor_scalar(out=lgf[:], in0=lgf[:], scalar1=-1.0, scalar2=None, op0=ALU.mult),
                      reads=[lgf_b], writes=[lgf_b])
                pbb, pbb_b = bank()
                for h in range(4):
                    fw.op(PE, lambda: tns.matmul(pbb[:, h * 128:(h + 1) * 128], lhsT=lgf[:, h:h + 1].to_broadcast([128, 128]),
                                                 rhs=triu[:], start=True, stop=True), reads=[lgf_b, triu_b], writes=[pbb_b])
                pcol, pcol_b = bank()
                fw.op(PE, lambda: tns.matmul(pcol[:, 0:4], lhsT=triu[:], rhs=lgf[:], start=True, stop=True),
                      reads=[lgf_b, triu_b], writes=[pcol_b])
                fw.op(DVE, lambda: vec.scalar_tensor_tensor(out=dbias[:], in0=gif[:, 0:4], scalar=-0.5 * math.log(128.0),
                                                            in1=pcol[:, 0:4], op0=ALU.add, op1=ALU.subtract),
                      reads=[gif_b, pcol_b], writes=[dbias_b])
                fw.op(ACT, lambda: sca.activation(out=Eb[:].rearrange("p h t -> p (h t)"), in_=pbb[:, :], func=AF.Exp),
                      reads=[pbb_b], writes=[Eb_b])
                for h in range(4):
                    fw.op(ACT, lambda: sca.activation(out=DT[:, h, :], in_=pbb[:, h * 128:(h + 1) * 128], func=AF.Exp,
                                                      bias=dbias[:, h:h + 1]), reads=[pbb_b, dbias_b], writes=[DT_b])
                fw.op(POOL, lambda: gps.tensor_tensor(out=DT[:], in0=DT[:], in1=m01[:].unsqueeze(1).to_broadcast([128, 4, 128]),
                                                      op=ALU.mult), reads=[DT_b, m01_b], writes=[DT_b])
                fw.op(POOL, lambda: gps.tensor_tensor(out=qpT[:], in0=qkT[:, 0:4, :], in1=Eb[:], op=ALU.mult),
                      reads=[qkT_b, Eb_b], writes=[qpT_b])
                pqk, pqk_b = bank()
                for h in range(4):
                    fw.op(PE, lambda: tns.matmul(pqk[:, h * 128:(h + 1) * 128], lhsT=qkT[:, 4 + h, :], rhs=qkT[:, h, :],
                                                 start=True, stop=True), reads=[qkT_b], writes=[pqk_b])
                fw.op(DVE, lambda: vec.tensor_tensor(out=WT[:].rearrange("p h t -> p (h t)"), in0=pqk[:, :],
                                                     in1=DT[:].rearrange("p h t -> p (h t)"), op=ALU.mult),
                      reads=[pqk_b, DT_b], writes=[WT_b])
                pn = [bank(), bank()]
                for h in range(4):
                    pn_t, pn_b = pn[h // 2]
                    o0 = (h % 2) * 129
                    fw.op(PE, lambda: tns.matmul(pn_t[:, o0:o0 + 129], lhsT=WT[:, h, :], rhs=vaug[:, h, :], start=True, stop=False),
                          reads=[WT_b, vaug_b], writes=[pn_b])
                    fw.op(PE, lambda: tns.matmul(pn_t[:, o0:o0 + 129], lhsT=qpT[:, h, :], rhs=Cbf[:, h, :], start=False, stop=True),
                          reads=[qpT_b, Cbf_b], writes=[pn_b])
                for half in range(2):
                    pn_t, pn_b = pn[half]
                    v3 = pn_t[:, 0:258].rearrange("p (h d) -> p h d", h=2)
                    fw.op(ACT, lambda: sca.activation(out=rden[:, half * 2:half * 2 + 2], in_=v3[:, :, 128], func=AF.Abs),
                          reads=[pn_b], writes=[rden_b])
                    fw.op(DVE, lambda: vec.tensor_scalar(out=rden[:, half * 2:half * 2 + 2], in0=rden[:, half * 2:half * 2 + 2],
                                                         scalar1=1.0, scalar2=None, op0=ALU.max), reads=[rden_b], writes=[rden_b])
                    fw.op(DVE, lambda: vec.reciprocal(out=rden[:, half * 2:half * 2 + 2], in_=rden[:, half * 2:half * 2 + 2]),
                          reads=[rden_b], writes=[rden_b])
                    fw.op(DVE, lambda: vec.tensor_tensor(out=hm[:, half * 2:half * 2 + 2, :], in0=v3[:, :, 0:128],
                                                         in1=rden[:, half * 2:half * 2 + 2].unsqueeze(2).to_broadcast([128, 2, 128]),
                                                         op=ALU.mult), reads=[pn_b, rden_b], writes=[hm_b])
                pkt, pkt_b = bank()
                pktv = pkt[:].bitcast(BF16)
                for h in range(4):
                    fw.op(PE, lambda: tns.transpose(out=pktv[:, h * 128:(h + 1) * 128], in_=qkT[:, 4 + h, :], identity=idb[:]),
                          reads=[qkT_b, idb_b], writes=[pkt_b])
                for h in range(4):
                    fw.op(ACT, lambda: sca.activation(out=ksc[:, h, :], in_=pktv[:, h * 128:(h + 1) * 128], func=AF.Copy,
                                                      scale=DT[:, h, 127:128]), reads=[pkt_b, DT_b], writes=[ksc_b])
                pcu = [bank(), bank()]
                for h in range(4):
                    pcu_t, pcu_b = pcu[h // 2]
                    o0 = (h % 2) * 129
                    fw.op(PE, lambda: tns.matmul(pcu_t[:, o0:o0 + 129], lhsT=ksc[:, h, :], rhs=vaug[:, h, :], start=True, stop=True),
                          reads=[ksc_b, vaug_b], writes=[pcu_b])
                for h in range(4):
                    pcu_t, pcu_b = pcu[h // 2]
                    o0 = (h % 2) * 129
                    fw.op(DVE, lambda: vec.scalar_tensor_tensor(out=Cst[:, h, :], in0=Cst[:, h, :], scalar=Eb[:, h, 127:128],
                                                                in1=pcu_t[:, o0:o0 + 129], op0=ALU.mult, op1=ALU.add),
                          reads=[Cst_hb[h], Eb_b, pcu_b], writes=[Cst_hb[h]])
                fw.op(POOL, lambda: gps.tensor_copy(out=Cbf[:], in_=Cst[:]), reads=Cst_hb, writes=[Cbf_b])
                for h in range(4):
                    fw.op(DVE, lambda: vec.bn_stats(out=hst[:, h, :], in_=hm[:, h, :]), reads=[hm_b], writes=[hst_hb[h]])
                for h in range(4):
                    fw.op(DVE, lambda: vec.bn_aggr(out=hmv[:, h, :], in_=hst[:, h, :]), reads=[hst_hb[h]], writes=[hmv_hb[h]])
                fw.op(ACT, lambda: sca.activation(out=hrs[:], in_=hmv[:, :, 1], func=AF.Sqrt, bias=epsc[:, 0:1]), reads=hmv_hb + [epsc_b], writes=[hrs_b])
                fw.op(DVE, lambda: vec.reciprocal(out=hrs[:], in_=hrs[:]), reads=[hrs_b], writes=[hrs_b])
                fw.op(DVE, lambda: vec.tensor_tensor(out=hm[:], in0=hm[:], in1=hmv[:, :, 0:1].to_broadcast([128, 4, 128]),
                                                     op=ALU.subtract), reads=[hm_b] + hmv_hb, writes=[hm_b])
                fw.op(DVE, lambda: vec.tensor_tensor(out=hm[:], in0=hm[:], in1=hrs[:].unsqueeze(2).to_broadcast([128, 4, 128]),
                                                     op=ALU.mult), reads=[hm_b, hrs_b], writes=[hm_b])
                fw.op(POOL, lambda: gps.tensor_tensor(out=hm[:].rearrange("p h d -> p (h d)"), in0=hm[:].rearrange("p h d -> p (h d)"),
                                                      in1=ng[:], op=ALU.mult), reads=[hm_b, ng_b], writes=[hm_b])
                fw.op(POOL, lambda: gps.tensor_tensor(out=cat[:, 512:1024], in0=hm[:].rearrange("p h d -> p (h d)"), in1=og[:],
                                                      op=ALU.mult), reads=[hm_b, og_b], writes=[cat_b])

                mark("A_wout", i == 2)
                if tap == "cat":
                    fw.op(ACT, lambda: sca.copy(out=hb[:], in_=cat[:]), reads=[cat_b], writes=[hb_b])
                    fw.dma(SP, h2_d[i * 128:(i + 1) * 128, :], hb[:], wbuf=h2_b, rbuf=hb_b, sembuf=hb_b)
                    continue
                to_featmajor(cat, cat_b, None, None, hT, hT_b, src_is_bf=True)
                for half in range(2):
                    pt, pt_b = bank()
                    for kc in range(8):
                        fw.op(PE, lambda: tns.matmul(pt[:, :], lhsT=hT[:, kc, :], rhs=wout[:, kc, half * 512:(half + 1) * 512],
                                                     start=(kc == 0), stop=(kc == 7)), reads=[wout_b, hT_b], writes=[pt_b])
                    fw.op(DVE, lambda: vec.scalar_tensor_tensor(out=hb[:, half * 512:(half + 1) * 512],
                                                                in0=hb[:, half * 512:(half + 1) * 512], scalar=ALPHA,
                                                                in1=pt[:, :], op0=ALU.mult, op1=ALU.add),
                          reads=[hb_b, pt_b], writes=[hb_b])
                layer_norm("ln1", hb, hb_b, 2, lnscr)
                if tap == "h1":
                    fw.dma(SP, h2_d[i * 128:(i + 1) * 128, :], hb[:], wbuf=h2_b, rbuf=hb_b, sembuf=hb_b)
                    continue

                mark("A_xattn", i == 2)
                to_featmajor(hb, hb_b, hbf, hbf_b, hT, hT_b)
                for grp in range(2):
                    pt, pt_b = bank()
                    for bi in range(4):
                        c = grp * 4 + bi
                        for kc in range(8):
                            fw.op(PE, lambda: tns.matmul(pt[:, bi * 128:(bi + 1) * 128], lhsT=wq[:, kc, c * 128:(c + 1) * 128],
                                                         rhs=hT[:, kc, :], start=(kc == 0), stop=(kc == 7)),
                                  reads=[wq_b, hT_b], writes=[pt_b])
                    fw.op(ACT, lambda: sca.copy(out=hbf[:, grp * 512:(grp + 1) * 512], in_=pt[:, :]),
                          reads=[pt_b], writes=[hbf_b])
                for grp in range(2):
                    pt, pt_b = bank()
                    for bi in range(4):
                        hx = grp * 2 + bi // 2
                        mc = bi % 2
                        for hf in range(2):
                            fw.op(PE, lambda: tns.matmul(pt[:, bi * 128:(bi + 1) * 128],
                                                         lhsT=kxT[:, hx * 2 + hf, mc * 128:(mc + 1) * 128],
                                                         rhs=hbf[:, (hx * 2 + hf) * 128:(hx * 2 + hf + 1) * 128],
                                                         start=(hf == 0), stop=(hf == 1)),
                                  reads=[kxT_b, hbf_b], writes=[pt_b])
                    fw.op(ACT, lambda: sca.activation(out=pxT[:, grp * 4:grp * 4 + 4, :].rearrange("p b t -> p (b t)"), in_=pt[:, :],
                                                      func=AF.Exp), reads=[pt_b], writes=[pxT_b])
                for hx in range(4):
                    pt, pt_b = bank()
                    for mc in range(2):
                        fw.op(PE, lambda: tns.matmul(pt[:, 0:257], lhsT=pxT[:, hx * 2 + mc, :], rhs=vx[:, mc, hx, :],
                                                     start=(mc == 0), stop=(mc == 1)), reads=[pxT_b, vx_b], writes=[pt_b])
                    fw.op(DVE, lambda: vec.reciprocal(out=rdx[:, hx:hx + 1], in_=pt[:, 256:257]), reads=[pt_b], writes=[rdx_b])
                    fw.op(DVE, lambda: vec.tensor_scalar(out=ob[:, hx * 256:(hx + 1) * 256], in0=pt[:, 0:256],
                                                         scalar1=rdx[:, hx:hx + 1], scalar2=None, op0=ALU.mult),
                          reads=[pt_b, rdx_b], writes=[ob_b])
                to_featmajor(ob, ob_b, None, None, hT, hT_b, src_is_bf=True)
                for half in range(2):
                    pt, pt_b = bank()
                    for kc in range(8):
                        fw.op(PE, lambda: tns.matmul(pt[:, :], lhsT=hT[:, kc, :], rhs=wo[:, kc, half * 512:(half + 1) * 512],
                                                     start=(kc == 0), stop=(kc == 7)), reads=[wo_b, hT_b], writes=[pt_b])
                    fw.op(DVE, lambda: vec.scalar_tensor_tensor(out=hb[:, half * 512:(half + 1) * 512],
                                                                in0=hb[:, half * 512:(half + 1) * 512], scalar=ALPHA,
                                                                in1=pt[:, :], op0=ALU.mult, op1=ALU.add),
                          reads=[hb_b, pt_b], writes=[hb_b])
                layer_norm("ln2", hb, hb_b, 4, lnscr)
                fw.dma(SP, h2_d[i * 128:(i + 1) * 128, :], hb[:], wbuf=h2_b, rbuf=hb_b, sembuf=hb_b)
                mark(None)
            fw.barrier()
        if stop == "A":
            return nc

        TB = 2 if NT % 2 == 0 else 1
        T = TB * 128
        with ExitStack() as stB:
            wpq, wpq_b = sb(stB, "wpq", [128, 8, 2 * D], BF16)
            ksT, ksT_b = sb(stB, "ksT", [128, 2, 128], BF16)
            skf, skf_b = sb(stB, "skf", [128, 128])
            skb, skb_b = sb(stB, "skb", [128, 128], BF16)
            load_weight_bf(w_pq_d, 2 * D, wpq, wpq_b)
            lnvB, lnvB_b = sb(stB, "lnvB", [128, 2, D])
            for i_ in range(2):
                fw.dma(SP, lnvB[:, i_, :], vec_d[6 + i_:7 + i_, :].partition_broadcast(128), wbuf=lnvB_b, nowait_prev=(i_ > 0))
            lnv_box[0], lnv_box[1], lnv_box[2] = lnvB, lnvB_b, 6
            for p in range(2):
                fw.dma(SP, skf[:], sk_d[p], wbuf=skf_b)
                fw.op(ACT, lambda: sca.copy(out=skb[:], in_=skf[:]), reads=[skf_b], writes=[skb_b])
                pt, pt_b = bank(4, 8)
                ptv = pt[:].bitcast(BF16)
                fw.op(PE, lambda: tns.transpose(out=ptv[:, 0:128], in_=skb[:], identity=idb[:]), reads=[skb_b, idb_b], writes=[pt_b])
                fw.op(DVE, lambda: vec.tensor_copy(out=ksT[:, p, :], in_=ptv[:, 0:128]), reads=[pt_b], writes=[ksT_b])
            h2t = [sb(stB, "h2t%d" % i, [128, D]) for i in range(2)]
            h2bf, h2bf_b = sb(stB, "h2bf", [128, D], BF16)
            h2Ts = [sb(stB, "h2T%d" % i, [128, 8, T], BF16) for i in range(2)]
            pqT, pqT_b = sb(stB, "pqT", [128, 16, T], BF16)
            ssb, ssb_b = sb(stB, "ssb", [128, 16, 128])
            srep, srep_b = sb(stB, "srep", [128, 16, 128])
            vtop, vtop_b = sb(stB, "vtop", [128, 16, 16])
            cand, cand_b = ssb[:].rearrange("p (h a) n -> p h (a n)", a=2), ssb_b
            cand2, cand2_b = srep[:].rearrange("p (h a) n -> p h (a n)", a=2), srep_b
            best, best_b = sb(stB, "best", [128, 8, 16])
            etmp, etmp_b = sb(stB, "etmp", [128, 8, 16])
            zz, zz_b = sb(stB, "zz", [128, 8])
            c1, c1_b = sb(stB, "c1", [128, 8])
            v0c, v0c_b = sb(stB, "v0c", [128, 8, 16])
            thr, thr_b = sb(stB, "thr", [128, 8, 16])
            bia, bia_b = sb(stB, "bia", [128, 8, 16])
            v0Ts = [sb(stB, "v0T%d" % i, [128, 128]) for i in range(TB)]
            thrTs = [sb(stB, "thrT%d" % i, [128, 128]) for i in range(TB)]
            biaTs = [sb(stB, "biaT%d" % i, [128, 128]) for i in range(TB)]
            pcs, pcs_b = sb(stB, "pcs", [128, 3, 128], BF16)
            rres, rres_b = sb(stB, "rres", [128, 128])
            At = [(sb(stB, "At%d" % i, [128, 4, 128], BF16)[0], [Buf("At%d_%d" % (i, k)) for k in range(4)]) for i in range(3)]
            Bt = [(sb(stB, "Bt%d" % i, [128, 4, 128], BF16)[0], [Buf("Bt%d_%d" % (i, k)) for k in range(4)]) for i in range(3)]
            E4 = [sb(stB, "E4%d" % i, [128, 512]) for i in range(2)]
            ebTs = [sb(stB, "ebT%d" % i, [128, 128]) for i in range(TB)]
            NB = 6
            Qrep = [sb(stB, "Qrep%d" % i, [128, 4, 128], BF16) for i in range(4)]
            Gs, Gs_b = sb(stB, "Gs", [128, 128, T], BF16)
            utc = [sb(stB, "utc%d" % i, [128, 8, 128], BF16) for i in range(NB)]
            vbc = [sb(stB, "vbc%d" % i, [128, D], BF16) for i in range(NB)]
            ag = [sb(stB, "ag%d" % i, [128, T], BF16) for i in range(3)]
            cT = [sb(stB, "cT%d" % i, [128, T], BF16) for i in range(3)]
            print("phaseB sbuf remaining", nc.sbuf_bytes_remaining)
            st6, st6_b = sb(stB, "st6B", [128, 2, 6])
            mv, mv_b = sb(stB, "mvB", [128, 2])
            rs, rs_b = sb(stB, "rsB", [128, 1])
            lnscr = (st6, st6_b, mv, mv_b, rs, rs_b)

            NTB = NT // TB
            per = 512 // T
            fst = {}
            LT = h2t[1 % len(h2t)]

            def piece_load(bt, ts_):
                hT, hT_b = h2Ts[bt % 2]
                t_, t_b = h2t[0]
                row0 = (bt * TB + ts_) * 128
                fw.dma(SP, t_[:], h2_d[row0:row0 + 128, :], wbuf=t_b, rbuf=h2_b)
                fw.op(ACT, lambda: sca.copy(out=h2bf[:], in_=t_[:]), reads=[t_b], writes=[h2bf_b])
                pt, pt_b = bank(4, 8)
                ptv = pt[:].bitcast(BF16)
                for kc in range(8):
                    fw.op(PE, lambda: tns.transpose(out=ptv[:, kc * 128:(kc + 1) * 128], in_=h2bf[:, kc * 128:(kc + 1) * 128],
                                                    identity=idb[:]), reads=[h2bf_b, idb_b], writes=[pt_b])
                fw.op(DVE, lambda: vec.tensor_copy(out=hT[:, :, ts_ * 128:(ts_ + 1) * 128],
                                                   in_=ptv.rearrange("p (k t) -> p k t", k=8)), reads=[pt_b], writes=[hT_b])

            def piece_pq(bt, g0):
                hT, hT_b = h2Ts[bt % 2]
                pt, pt_b = bank(4, 8)
                for bi in range(per):
                    hp = g0 + bi
                    for kc in range(8):
                        fw.op(PE, lambda: tns.matmul(pt[:, bi * T:(bi + 1) * T], lhsT=wpq[:, kc, hp * 128:(hp + 1) * 128],
                                                     rhs=hT[:, kc, :], start=(kc == 0), stop=(kc == 7)),
                              reads=[wpq_b, hT_b], writes=[pt_b])
                fw.op(ACT, lambda: sca.copy(out=pqT[:, g0:g0 + per, :], in_=pt[:, 0:per * T].rearrange("p (b t) -> p b t", b=per)),
                      reads=[pt_b], writes=[pqT_b])

            def piece_s(ts_, g):
                tsl = slice(ts_ * 128, (ts_ + 1) * 128)
                pt, pt_b = bank(4, 8)
                for bi in range(4):
                    hp = g * 4 + bi
                    fw.op(PE, lambda: tns.matmul(pt[:, bi * 128:(bi + 1) * 128], lhsT=pqT[:, hp, tsl], rhs=ksT[:, hp % 2, :],
                                                 start=True, stop=True), reads=[pqT_b, ksT_b], writes=[pt_b])
                fw.op(ACT, lambda: sca.copy(out=ssb[:, g * 4:g * 4 + 4, :], in_=pt[:, :].rearrange("p (b n) -> p b n", b=4)),
                      reads=[pt_b], writes=[ssb_b])

            def piece_top_a(ts_):
                fst["vt"] = [Buf("vt%d" % k) for k in range(32)]
                fst["sr"] = [Buf("sr%d" % k) for k in range(16)]
                vt_bs = fst["vt"]
                for hp in range(16):
                    fw.op(DVE, lambda: vec.max(out=vtop[:, hp, 0:8], in_=ssb[:, hp, :]), reads=[ssb_b, vtop_b],
                          writes=[vt_bs[hp * 2]] + ([vtop_b] if hp == 0 else []))

            def piece_top_b(ts_):
                vt_bs, sr_bs = fst["vt"], fst["sr"]
                for hp in range(16):
                    fw.op(DVE, lambda: vec.match_replace(out=srep[:, hp, :], in_to_replace=vtop[:, hp, 0:8], in_values=ssb[:, hp, :],
                                                         imm_value=-1e30), reads=[ssb_b, vt_bs[hp * 2], srep_b], writes=[sr_bs[hp]])

            def piece_top_c(ts_):
                vt_bs, sr_bs = fst["vt"], fst["sr"]
                for hp in range(16):
                    fw.op(DVE, lambda: vec.max(out=vtop[:, hp, 8:16], in_=srep[:, hp, :]), reads=[sr_bs[hp], vtop_b], writes=[vt_bs[hp * 2 + 1]])
                vt4 = vtop[:].rearrange("p (h q) k -> p h q k", q=2)
                fw.op(POOL, lambda: gps.tensor_tensor(out=cand[:].rearrange("p h (a b) -> p h a b", a=16),
                                                      in0=vt4[:, :, 0, :].unsqueeze(3).to_broadcast([128, 8, 16, 16]),
                                                      in1=vt4[:, :, 1, :].unsqueeze(2).to_broadcast([128, 8, 16, 16]), op=ALU.add),
                      reads=vt_bs, writes=[cand_b])

            def piece_best(ts_):
                vt_bs, sr_bs = fst["vt"], fst["sr"]
                bs_bs = [Buf("bs%d" % k) for k in range(16)]
                c2_bs = [Buf("c2%d" % k) for k in range(8)]
                for h in range(8):
                    fw.op(DVE, lambda: vec.max(out=best[:, h, 0:8], in_=cand[:, h, :]), reads=[cand_b, best_b],
                          writes=[bs_bs[h * 2]] + ([best_b] if h == 0 else []))
                for h in range(8):
                    fw.op(DVE, lambda: vec.match_replace(out=cand2[:, h, :], in_to_replace=best[:, h, 0:8], in_values=cand[:, h, :],
                                                         imm_value=-1e30), reads=[cand_b, bs_bs[h * 2], cand2_b] + sr_bs, writes=[c2_bs[h]])
                for h in range(8):
                    fw.op(DVE, lambda: vec.max(out=best[:, h, 8:16], in_=cand2[:, h, :]), reads=[c2_bs[h], best_b], writes=[bs_bs[h * 2 + 1]])
                for bb_ in bs_bs:
                    for k_, v_ in bb_.w.items():
                        _mx(best_b.w, k_, v_)
                    for k_, v_ in bb_.r.items():
                        _mx(best_b.r, k_, v_)
                for bb_ in vt_bs:
                    for k_, v_ in bb_.w.items():
                        _mx(vtop_b.w, k_, v_)
                    for k_, v_ in bb_.r.items():
                        _mx(vtop_b.r, k_, v_)
                for bb_ in sr_bs + c2_bs:
                    for k_, v_ in bb_.w.items():
                        _mx(srep_b.w, k_, v_)
                    for k_, v_ in bb_.r.items():
                        _mx(srep_b.r, k_, v_)

            def piece_misc(ts_):
                vt4 = vtop[:].rearrange("p (h q) k -> p h q k", q=2)
                fw.op(POOL, lambda: gps.tensor_tensor(out=etmp[:], in0=best[:], in1=best[:, :, 0:1].to_broadcast([128, 8, 16]),
                                                      op=ALU.subtract), reads=[best_b], writes=[etmp_b])
                fw.op(ACT, lambda: sca.activation(out=etmp[:], in_=etmp[:], func=AF.Exp), reads=[etmp_b], writes=[etmp_b])
                fw.op(DVE, lambda: vec.reduce_sum(out=zz[:], in_=etmp[:], axis=AX.X), reads=[etmp_b], writes=[zz_b])
                fw.op(ACT, lambda: sca.activation(out=zz[:], in_=zz[:], func=AF.Ln), reads=[zz_b], writes=[zz_b])
                fw.op(POOL, lambda: gps.tensor_tensor(out=c1[:], in0=zz[:], in1=best[:, :, 0], op=ALU.add),
                      reads=[zz_b, best_b], writes=[c1_b])
                fw.op(POOL, lambda: gps.tensor_copy(out=v0c[:], in_=vt4[:, :, 0, :]), reads=[vtop_b], writes=[v0c_b])
                fw.op(POOL, lambda: gps.tensor_tensor(out=thr[:], in0=best[:, :, 15:16].to_broadcast([128, 8, 16]), in1=v0c[:],
                                                      op=ALU.subtract), reads=[best_b, v0c_b], writes=[thr_b])
                fw.op(POOL, lambda: gps.tensor_scalar(out=thr[:], in0=thr[:], scalar1=-1e-5, scalar2=None, op0=ALU.add),
                      reads=[thr_b], writes=[thr_b])
                fw.op(POOL, lambda: gps.tensor_tensor(out=bia[:], in0=v0c[:], in1=c1[:].unsqueeze(2).to_broadcast([128, 8, 16]),
                                                      op=ALU.subtract), reads=[c1_b, v0c_b], writes=[bia_b])

            def piece_tr(ts_, which):
                src, src_b = ((v0c, v0c_b), (thr, thr_b), (bia, bia_b))[which]
                dst, dst_b = (v0Ts[ts_], thrTs[ts_], biaTs[ts_])[which]
                srcf = src[:].rearrange("p h r -> p (h r)")
                pt, pt_b = bank(4, 8)
                ptv = pt[:].bitcast(BF16)
                for pc_i in range(3):
                    fw.op(ACT, lambda: sca.copy(out=pcs[:, pc_i, :], in_=(srcf if pc_i == 0 else rres[:])),
                          reads=[src_b, rres_b], writes=[pcs_b])
                    if pc_i < 2:
                        fw.op(DVE, lambda: vec.tensor_tensor(out=rres[:], in0=(srcf if pc_i == 0 else rres[:]), in1=pcs[:, pc_i, :],
                                                             op=ALU.subtract), reads=[src_b, rres_b, pcs_b], writes=[rres_b])
                    fw.op(PE, lambda: tns.transpose(out=ptv[:, pc_i * 128:(pc_i + 1) * 128], in_=pcs[:, pc_i, :], identity=idb[:]),
                          reads=[pcs_b, idb_b], writes=[pt_b])
                fw.op(ACT, lambda: sca.copy(out=dst[:], in_=ptv[:, 0:128]), reads=[pt_b], writes=[dst_b])
                fw.op(DVE, lambda: vec.tensor_tensor(out=dst[:], in0=dst[:], in1=ptv[:, 128:256], op=ALU.add),
                      reads=[pt_b, dst_b], writes=[dst_b])
                fw.op(DVE, lambda: vec.tensor_tensor(out=dst[:], in0=dst[:], in1=ptv[:, 256:384], op=ALU.add),
                      reads=[pt_b, dst_b], writes=[dst_b])
                if which == 2:
                    ebT, ebT_b = ebTs[ts_]
                    fw.op(ACT, lambda: sca.activation(out=ebT[:], in_=dst[:], func=AF.Exp), reads=[dst_b], writes=[ebT_b])

            def front_pieces(bt):
                P = []
                for ts_ in range(TB):
                    P.append(lambda ts_=ts_: piece_load(bt, ts_))
                for g0 in range(0, 16, per):
                    P.append(lambda g0=g0: piece_pq(bt, g0))
                for ts_ in range(TB):
                    for g in range(4):
                        P.append(lambda ts_=ts_, g=g: piece_s(ts_, g))
                    P.append(lambda ts_=ts_: piece_top_a(ts_))
                    P.append(lambda ts_=ts_: piece_top_b(ts_))
                    P.append(lambda ts_=ts_: piece_top_c(ts_))
                    P.append(lambda ts_=ts_: piece_best(ts_))
                    P.append(lambda ts_=ts_: piece_misc(ts_))
                    for which in range(3):
                        P.append(lambda ts_=ts_, which=which: piece_tr(ts_, which))
                return P

            def tok_loop(bt, ts_):
                v0T, v0T_b = v0Ts[ts_]
                thrT, thrT_b = thrTs[ts_]
                ebT, ebT_b = ebTs[ts_]
                itst = {}

                def st12(t4):
                    a_t, a_bs = At[t4 % 3]
                    b_t, b_bs = Bt[t4 % 3]
                    e4, e4_b = E4[t4 % 2]
                    p0_, p0_b = bank(0, 8)
                    p1_, p1_b = bank(0, 8)
                    tt0 = ts_ * 128 + t4 * 4
                    qr = []
                    for p_ in range(2):
                        q_t, q_b = Qrep[(t4 % 2) * 2 + p_]
                        src = pqT[:, :, tt0:tt0 + 4].rearrange("c (h q) t -> c q t h", q=2)[:, p_]
                        fw.op(POOL, lambda: gps.tensor_copy(out=q_t[:].rearrange("c t (h r) -> c t h r", r=16),
                                                            in_=src.unsqueeze(3).to_broadcast([128, 4, 8, 16])),
                              reads=[pqT_b], writes=[q_b])
                        qr.append((q_t, q_b))
                    for k in range(4):
                        for p_, (pp, pp_b) in enumerate(((p0_, p0_b), (p1_, p1_b))):
                            q_t, q_b = qr[p_]
                            fw.op(PE, lambda: tns.matmul(pp[:, k * 128:(k + 1) * 128], lhsT=q_t[:, k, :], rhs=ksT[:, p_, :],
                                                         start=True, stop=True), reads=[q_b, ksT_b], writes=[pp_b])
                    tl = t4 * 4
                    fw.op(ACT, lambda: sca.activation(out=e4[:], in_=p1_[:, :], func=AF.Exp), reads=[p1_b], writes=[e4_b])
                    for k in range(4):
                        fw.op(DVE, lambda: vec.tensor_scalar(out=a_t[:, k, :], in0=p0_[:, k * 128:(k + 1) * 128],
                                                             scalar1=v0T[:, tl + k:tl + k + 1], scalar2=ebT[:, tl + k:tl + k + 1],
                                                             op0=ALU.is_equal, op1=ALU.mult),
                              reads=[p0_b, v0T_b, ebT_b], writes=[a_bs[k]])
                    for k in range(4):
                        fw.op(DVE, lambda: vec.scalar_tensor_tensor(out=b_t[:, k, :], in0=p1_[:, k * 128:(k + 1) * 128],
                                                                    scalar=thrT[:, tl + k:tl + k + 1], in1=e4[:, k * 128:(k + 1) * 128],
                                                                    op0=ALU.is_ge, op1=ALU.mult),
                              reads=[p1_b, thrT_b, e4_b], writes=[b_bs[k]])

                def st3(t4):
                    a_t, a_bs = At[t4 % 3]
                    b_t, b_bs = Bt[t4 % 3]
                    pg, pg_b = bank(0, 8)
                    for k in range(4):
                        fw.op(PE, lambda: tns.matmul(pg[:, k * 128:(k + 1) * 128], lhsT=b_t[:, k, :], rhs=a_t[:, k, :],
                                                     start=True, stop=True), reads=[a_bs[k], b_bs[k]], writes=[pg_b])
                    itst[t4] = (pg, pg_b)

                def st4(t4):
                    pg, pg_b = itst.pop(t4)
                    tg = ts_ * 128 + t4 * 4
                    fw.op(ACT, lambda: sca.copy(out=Gs[:, :, tg:tg + 4].rearrange("p n t -> p t n"),
                                                in_=pg[:, :].rearrange("p (t n) -> p t n", t=4)), reads=[pg_b], writes=[Gs_b])

                for t4 in range(34):
                    if t4 < 32:
                        st12(t4)
                    if 0 <= t4 - 1 < 32:
                        st3(t4 - 1)
                    if 0 <= t4 - 2 < 32:
                        st4(t4 - 2)

            accs = [banks[j] for j in range(TB * 2)]

            def expert_loop(bt, nxt):
                hT, hT_b = h2Ts[bt % 2]
                every = max(1, 120 // max(1, len(nxt)))

                def issue_load(n):
                    u_t, u_b = utc[n % NB]
                    v_t, v_b = vbc[n % NB]
                    fw.dma(SP, u_t[:], ut_d[n], wbuf=u_b, rbuf=ut_b)
                    fw.dma(SP, v_t[:], vb_d[n], wbuf=v_b, rbuf=vb_b)

                def stage2(n):
                    c_t, c_b = cT[n % 3]
                    v_t, v_b = vbc[n % NB]
                    for ts_ in range(TB):
                        for half in range(2):
                            ac, ac_b = accs[ts_ * 2 + half]
                            fw.op(PE, lambda: tns.matmul(ac[:, :], lhsT=c_t[:, ts_ * 128:(ts_ + 1) * 128],
                                                         rhs=v_t[:, half * 512:(half + 1) * 512], start=(n == 0), stop=(n == 127)),
                                  reads=[c_b, v_b], writes=[ac_b])

                for n1 in range(NB - 1):
                    issue_load(n1)
                for n1 in range(128):
                    u_t, u_b = utc[n1 % NB]
                    pa, pa_b = bank(4, 8)
                    for kc in range(8):
                        fw.op(PE, lambda: tns.matmul(pa[:, 0:T], lhsT=u_t[:, kc, :], rhs=hT[:, kc, :], start=(kc == 0), stop=(kc == 7)),
                              reads=[u_b, hT_b], writes=[pa_b])
                    g_t, g_b = ag[n1 % 3]
                    c_t, c_b = cT[n1 % 3]
                    fw.op(ACT, lambda: sca.activation(out=g_t[:], in_=pa[:, 0:T], func=AF.Gelu), reads=[pa_b], writes=[g_b])
                    eng, ee = POOL, gps
                    fw.op(eng, lambda: ee.tensor_tensor(out=c_t[:], in0=g_t[:], in1=Gs[:, n1, :], op=ALU.mult),
                          reads=[g_b, Gs_b], writes=[c_b])
                    if n1 >= 1:
                        stage2(n1 - 1)
                    if n1 + NB - 1 < 128:
                        issue_load(n1 + NB - 1)
                    if nxt and n1 >= 2 and n1 % every == 0:
                        nxt.pop(0)()
                stage2(127)
                while nxt:
                    nxt.pop(0)()

            def tail(bt):
                for ts_ in range(TB):
                    t_, t_b = LT
                    row0 = (bt * TB + ts_) * 128
                    fw.dma(SP, t_[:], h2_d[row0:row0 + 128, :], wbuf=t_b, rbuf=h2_b)
                    for half in range(2):
                        ac, ac_b = accs[ts_ * 2 + half]
                        fw.op(DVE, lambda: vec.scalar_tensor_tensor(out=t_[:, half * 512:(half + 1) * 512],
                                                                    in0=t_[:, half * 512:(half + 1) * 512], scalar=ALPHA,
                                                                    in1=ac[:, :], op0=ALU.mult, op1=ALU.add),
                              reads=[t_b, ac_b], writes=[t_b])
                    layer_norm("ln3", t_, t_b, 6, lnscr)
                    fw.dma(SP, y_d[row0:row0 + 128, :], t_[:], rbuf=t_b, sembuf=t_b)

            for p_ in front_pieces(0):
                p_()
            for bt in range(NTB):
                mark("B_tok", bt == 1)
                for ts_ in range(TB):
                    tok_loop(bt, ts_)
                mark("B_exp", bt == 1)
                expert_loop(bt, front_pieces(bt + 1) if bt + 1 < NTB else [])
                mark("B_ln3", bt == 1)
                tail(bt)
                mark(None)
            fw.barrier()
        print("n_instr", fw.n_instr, {e.name: e.cnt for e in fw.engs})
    return nc


def prep_inputs(inputs, S):
    f = lambda a: np.ascontiguousarray(np.asarray(a, dtype=np.float32))
    x = f(inputs["x"])
    mem = f(inputs["mem"])
    vecs = np.stack([f(inputs["ln_in_g"]), f(inputs["ln_in_b"]), f(inputs["ln1_g"])[0], f(inputs["ln1_b"])[0],
                     f(inputs["ln2_g"])[0], f(inputs["ln2_b"])[0], f(inputs["ln3_g"])[0], f(inputs["ln3_b"])[0]], axis=0)
    conv_w = f(inputs["conv_w"])[0]
    cw = np.ascontiguousarray(conv_w.reshape(4, 8, 128).transpose(2, 1, 0))
    cb = np.ascontiguousarray(f(inputs["conv_b"])[0].reshape(8, 128).T)
    gb = np.concatenate([f(inputs["mlstm_i_bias"])[0], f(inputs["mlstm_f_bias"])[0]])[None, :]
    rel_bias = f(inputs["rel_bias"])[0]
    s = np.arange(128)[:, None]
    t = np.arange(128)[None, :]
    tabs = []
    for r in range(2):
        idx = np.clip(s - t - 128 * r, -128, 128) + 128
        tabs.append(rel_bias[:, idx])
    rbT = np.ascontiguousarray(np.stack(tabs, axis=0).transpose(2, 1, 0, 3))
    rb0 = np.ascontiguousarray(rel_bias[:, 0][None, :])
    common = {
        "vecs": np.ascontiguousarray(vecs), "w_in": f(inputs["w_in"])[0], "cw": cw, "cb": cb, "gbias": np.ascontiguousarray(gb),
        "norm_g": f(inputs["mlstm_norm_g"]), "rbT": rbT, "rb0": rb0, "w_out": f(inputs["w_out"])[0],
        "w_q": f(inputs["xattn_w_q"])[0], "w_kv": f(inputs["xattn_w_kv"])[0], "w_o": f(inputs["xattn_w_o"])[0],
        "w_pq": f(inputs["peer_w_query"])[0], "sub_keys": f(inputs["peer_sub_keys"])[0],
        "peer_u": f(inputs["peer_u"])[0], "peer_v": f(inputs["peer_v"])[0],
    }
    maps = []
    for c in range(x.shape[0]):
        m = dict(common)
        m["x"] = np.ascontiguousarray(x[c, :S])
        m["mem"] = np.ascontiguousarray(mem[c])
        maps.append(m)
    return maps


def kernel(**inputs):
    S = int(np.asarray(inputs["x"]).shape[1])
    nb = int(np.asarray(inputs["x"]).shape[0])
    nc = build(S)
    maps = prep_inputs(inputs, S)
    res = run_bass_kernel_spmd(nc, maps, core_ids=list(range(nb)))
    out = np.stack([np.asarray(res.results[c]["y"], dtype=np.float32) for c in range(nb)], axis=0)
    return out
```
